# Optimizing a Trainium2 kernel written in Bass

```python
import jax, jax.numpy as jnp
from jax import lax
import numpy as np

D_MODEL = 1024
BATCH = 16
SEQ = 4096
DEPTH = 1

RET_HEADS = 8
RET_HEAD_DIM = 64
RET_WIDTH = RET_HEADS * RET_HEAD_DIM
RET_CHUNK = 128
MLA_HEADS = 8
MLA_Q_RANK = 256
MLA_KV_RANK = 128
MLA_NOPE_DIM = 64
MLA_ROPE_DIM = 32
MLA_QK_DIM = MLA_NOPE_DIM + MLA_ROPE_DIM
MLA_V_DIM = 64
MLA_WIDTH = MLA_HEADS * MLA_V_DIM
Q_BLOCK = 128
MIX_WIDTH = RET_WIDTH + MLA_WIDTH
IN_WIDTH = 4 * RET_WIDTH + MLA_Q_RANK + MLA_KV_RANK + MLA_ROPE_DIM
D_FF = -(-8 * D_MODEL // (3 * 256)) * 256
ROPE_BASE = 10000.0
EPS = 1e-6
GN_EPS = 1e-5

kernel_name = 'hybrid_retention_mla_encoder_block'


def rms_norm(x, g, eps=EPS):
    xf = x.astype(jnp.float32)
    y = xf * lax.rsqrt(jnp.mean(xf * xf, axis=-1, keepdims=True) + eps)
    return (y * g.astype(jnp.float32)).astype(x.dtype)


def rotary(x, positions):
    half = x.shape[-1] // 2
    inv_freq = ROPE_BASE ** (-jnp.arange(half, dtype=jnp.float32) / half)
    ang = positions.astype(jnp.float32)[..., None] * inv_freq
    cos = jnp.cos(ang)[:, :, None, :]
    sin = jnp.sin(ang)[:, :, None, :]
    xf = x.astype(jnp.float32)
    x1, x2 = xf[..., :half], xf[..., half:]
    out = jnp.concatenate([x1 * cos - x2 * sin, x2 * cos + x1 * sin], axis=-1)
    return out.astype(x.dtype)


def retention_one_direction(q, k, v, log_gamma, strict):
    C = q.shape[3]
    idx = jnp.arange(C, dtype=jnp.float32)
    diff = idx[:, None] - idx[None, :]
    mask = (diff > 0) if strict else (diff >= 0)
    lg = log_gamma[:, None, None]
    decay = jnp.where(mask, jnp.exp(lg * jnp.where(mask, diff, 0.0)), 0.0)
    scores = jnp.einsum('bhncd,bhnmd->bhncm', q, k) * decay[None, :, None]
    inner = jnp.einsum('bhncm,bhnmd->bhncd', scores, v)
    zeta = jnp.exp(log_gamma[:, None] * (C - 1 - idx)[None, :])
    xi = jnp.exp(log_gamma[:, None] * (idx + 1)[None, :])
    chunk_kv = jnp.einsum('bhncd,bhnce->nbhde', k * zeta[None, :, None, :, None], v)
    chunk_decay = jnp.exp(log_gamma * C)[None, :, None, None]

    def step(state, kv):
        return state * chunk_decay + kv, state

    _, prev = lax.scan(step, jnp.zeros_like(chunk_kv[0]), chunk_kv)
    cross = jnp.einsum('bhncd,nbhde->bhnce', q, prev) * xi[None, :, None, :, None]
    return inner + cross


def retention_mixer(q, k, v, g, positions, logit_fwd, logit_bwd):
    B, S, _ = q.shape
    n_chunks = S // RET_CHUNK

    def heads(t):
        return t.reshape(B, S, RET_HEADS, RET_HEAD_DIM)

    qh = rotary(heads(q), positions).astype(jnp.float32)
    kh = rotary(heads(k), positions).astype(jnp.float32) * (RET_HEAD_DIM ** -0.5)
    vh = heads(v).astype(jnp.float32)

    def chunked(t):
        return t.transpose(0, 2, 1, 3).reshape(B, RET_HEADS, n_chunks, RET_CHUNK, RET_HEAD_DIM)

    lg_f = jax.nn.log_sigmoid(logit_fwd.astype(jnp.float32))
    lg_b = jax.nn.log_sigmoid(logit_bwd.astype(jnp.float32))
    o_f = retention_one_direction(chunked(qh), chunked(kh), chunked(vh), lg_f, strict=False)
    o_b = retention_one_direction(chunked(jnp.flip(qh, 1)), chunked(jnp.flip(kh, 1)),
                                  chunked(jnp.flip(vh, 1)), lg_b, strict=True)
    o = (o_f.reshape(B, RET_HEADS, S, RET_HEAD_DIM)
         + jnp.flip(o_b.reshape(B, RET_HEADS, S, RET_HEAD_DIM), axis=2))
    mu = jnp.mean(o, axis=-1, keepdims=True)
    var = jnp.mean(jnp.square(o - mu), axis=-1, keepdims=True)
    o = (o - mu) * lax.rsqrt(var + GN_EPS)
    o = o.transpose(0, 2, 1, 3).reshape(B, S, RET_WIDTH)
    return (jax.nn.silu(g.astype(jnp.float32)) * o).astype(g.dtype)


def mla_mixer(c_q, c_kv, k_rope, positions, q_a_norm_g, w_uq, kv_a_norm_g, w_ukv, q_norm_g, k_norm_g):
    B, S, _ = c_q.shape
    q = (rms_norm(c_q, q_a_norm_g) @ w_uq).reshape(B, S, MLA_HEADS, MLA_QK_DIM)
    kv = (rms_norm(c_kv, kv_a_norm_g) @ w_ukv).reshape(B, S, MLA_HEADS, MLA_NOPE_DIM + MLA_V_DIM)
    k_nope, v = kv[..., :MLA_NOPE_DIM], kv[..., MLA_NOPE_DIM:]
    k_pe = jnp.broadcast_to(k_rope[:, :, None, :], (B, S, MLA_HEADS, MLA_ROPE_DIM))
    k = jnp.concatenate([k_nope, k_pe], axis=-1)
    q = rms_norm(q, q_norm_g)
    k = rms_norm(k, k_norm_g)
    q = jnp.concatenate([q[..., :MLA_NOPE_DIM], rotary(q[..., MLA_NOPE_DIM:], positions)], axis=-1)
    k = jnp.concatenate([k[..., :MLA_NOPE_DIM], rotary(k[..., MLA_NOPE_DIM:], positions)], axis=-1)
    scale = MLA_QK_DIM ** -0.5
    n_blocks = S // Q_BLOCK
    q_blocks = q.reshape(B, n_blocks, Q_BLOCK, MLA_HEADS, MLA_QK_DIM).transpose(1, 0, 2, 3, 4)

    def attend(qb):
        s = jnp.einsum('bqhd,bkhd->bhqk', qb, k).astype(jnp.float32) * scale
        p = jax.nn.softmax(s, axis=-1)
        return jnp.einsum('bhqk,bkhd->bqhd', p.astype(v.dtype), v)

    o = lax.map(attend, q_blocks)
    return o.transpose(1, 0, 2, 3, 4).reshape(B, S, MLA_WIDTH)


def swiglu(h, w_gate, w_up, w_down):
    a = jax.nn.silu((h @ w_gate).astype(jnp.float32)).astype(h.dtype)
    return (a * (h @ w_up)) @ w_down


def setup_inputs(seed: int = 0) -> dict:
    key = jax.random.key(seed)
    ks = jax.random.split(key, 20)
    f32 = jnp.float32

    def w(k, shape, fan_in):
        return jax.random.normal(k, shape, f32) * (fan_in ** -0.5)

    def gain(k, shape):
        return jnp.ones(shape, f32) + 0.01 * jax.random.normal(k, shape, f32)

    base_logit = jnp.asarray(np.log(2.0 ** (5 + np.arange(RET_HEADS)) - 1.0), dtype=f32)
    x = jax.random.normal(ks[0], (BATCH, SEQ, D_MODEL), f32)
    offset = jax.random.randint(ks[1], (BATCH, 1), 0, 1024, dtype=jnp.int32)
    positions = (jnp.arange(SEQ, dtype=jnp.int32)[None, :] + offset).astype(jnp.int32)
    return {
        'x': x,
        'positions': positions,
        'norm1_g': gain(ks[2], (DEPTH, D_MODEL)),
        'w_in': w(ks[3], (DEPTH, D_MODEL, IN_WIDTH), D_MODEL),
        'ret_decay_logit_fwd': base_logit[None] + 0.1 * jax.random.normal(ks[4], (DEPTH, RET_HEADS), f32),
        'ret_decay_logit_bwd': base_logit[None] + 0.1 * jax.random.normal(ks[5], (DEPTH, RET_HEADS), f32),
        'q_a_norm_g': gain(ks[6], (DEPTH, MLA_Q_RANK)),
        'w_uq': w(ks[7], (DEPTH, MLA_Q_RANK, MLA_HEADS * MLA_QK_DIM), MLA_Q_RANK),
        'kv_a_norm_g': gain(ks[8], (DEPTH, MLA_KV_RANK)),
        'w_ukv': w(ks[9], (DEPTH, MLA_KV_RANK, MLA_HEADS * (MLA_NOPE_DIM + MLA_V_DIM)), MLA_KV_RANK),
        'q_norm_g': gain(ks[10], (DEPTH, MLA_QK_DIM)),
        'k_norm_g': gain(ks[11], (DEPTH, MLA_QK_DIM)),
        'w_o': w(ks[12], (DEPTH, MIX_WIDTH, D_MODEL), MIX_WIDTH),
        'norm2_g': gain(ks[13], (DEPTH, D_MODEL)),
        'w_gate': w(ks[14], (DEPTH, D_MODEL, D_FF), D_MODEL),
        'w_up': w(ks[15], (DEPTH, D_MODEL, D_FF), D_MODEL),
        'w_down': w(ks[16], (DEPTH, D_FF, D_MODEL), D_FF),
    }


def reference(x, positions, norm1_g, w_in, ret_decay_logit_fwd, ret_decay_logit_bwd, q_a_norm_g, w_uq,
              kv_a_norm_g, w_ukv, q_norm_g, k_norm_g, w_o, norm2_g, w_gate, w_up, w_down):
    split_points = [RET_WIDTH, 2 * RET_WIDTH, 3 * RET_WIDTH, 4 * RET_WIDTH,
                    4 * RET_WIDTH + MLA_Q_RANK, 4 * RET_WIDTH + MLA_Q_RANK + MLA_KV_RANK]
    for layer in range(DEPTH):
        h = rms_norm(x, norm1_g[layer])
        proj = h @ w_in[layer]
        q_r, k_r, v_r, g_r, c_q, c_kv, k_rope = jnp.split(proj, split_points, axis=-1)
        y_ret = retention_mixer(q_r, k_r, v_r, g_r, positions,
                                ret_decay_logit_fwd[layer], ret_decay_logit_bwd[layer])
        y_mla = mla_mixer(c_q, c_kv, k_rope, positions, q_a_norm_g[layer], w_uq[layer],
                          kv_a_norm_g[layer], w_ukv[layer], q_norm_g[layer], k_norm_g[layer])
        x = x + jnp.concatenate([y_ret, y_mla], axis=-1) @ w_o[layer]
        x = x + swiglu(rms_norm(x, norm2_g[layer]), w_gate[layer], w_up[layer], w_down[layer])
    return x
```

```python
import numpy as np
import concourse.bass as bass
import concourse.mybir as mybir

F32 = mybir.dt.float32
BF16 = mybir.dt.bfloat16
I32 = mybir.dt.int32
ALU = mybir.AluOpType
AF = mybir.ActivationFunctionType
AX = mybir.AxisListType


class Tok:
    __slots__ = ("w", "r", "name", "excl")

    def __init__(self, name="", excl=False):
        self.w = None
        self.r = {}
        self.name = name
        self.excl = excl


class Op:
    __slots__ = ("eng", "fn", "deps", "sig", "sigval", "sem", "dma", "idx")


class Prog:
    ENGS = ("pe", "act", "dve", "pool", "sp")

    def __init__(self, nc, ndma=8):
        self.nc = nc
        self.ops = {e: [] for e in self.ENGS}
        self.ndma = ndma
        self.dma_hist = {e: [] for e in self.ENGS}
        self.nops = 0

    def add(self, eng, fn, reads=(), writes=(), dma=False):
        op = Op()
        op.eng = eng
        op.fn = fn
        op.dma = dma
        op.sig = dma
        op.sigval = 0
        op.sem = None
        op.idx = self.nops
        self.nops += 1
        deps = {}
        xr = [t for t in reads if t.excl]
        if xr:
            reads = [t for t in reads if not t.excl]
            writes = list(writes) + [t for t in xr if t not in writes]

        def need(d, war):
            if d is None:
                return
            if (not d.dma) and (not dma) and d.eng == eng:
                if eng == "pe" or war:
                    return
            deps[d.idx] = d

        for t in reads:
            need(t.w, False)
        for t in writes:
            need(t.w, False)
            for r in t.r.values():
                need(r, True)
        if dma:
            h = self.dma_hist[eng]
            k = len(h)
            op.sem = (eng, k % self.ndma)
            op.sigval = 16 * (k // self.ndma + 1)
            if k >= self.ndma:
                d = h[k - self.ndma]
                deps[d.idx] = d
            h.append(op)
        op.deps = list(deps.values())
        for d in op.deps:
            d.sig = True
        for t in reads:
            key = (eng, op.idx) if dma else eng
            t.r[key] = op
        for t in writes:
            t.w = op
            t.r = {}
        self.ops[eng].append(op)
        return op

    def barrier(self):
        lasts = []
        for e in self.ENGS:
            comp = [o for o in self.ops[e] if not o.dma and o.fn is not None]
            if comp:
                lasts.append(comp[-1])
            h = self.dma_hist[e]
            lasts.extend(h[-self.ndma:])
        for e in self.ENGS:
            op = Op()
            op.eng = e
            op.fn = None
            op.dma = False
            op.sig = False
            op.sigval = 0
            op.sem = None
            op.idx = self.nops
            self.nops += 1
            op.deps = [d for d in lasts if not (d.eng == e and not d.dma)]
            for d in op.deps:
                d.sig = True
            self.ops[e].append(op)

    def emit(self, stack):
        nc = self.nc
        sems = {}
        for e in self.ENGS:
            sems[e] = stack.enter_context(nc.semaphore("s_" + e))
            for k in range(self.ndma):
                sems[(e, k)] = stack.enter_context(nc.semaphore("d_%s%d" % (e, k)))
        for e in self.ENGS:
            c = 0
            for op in self.ops[e]:
                if op.dma or op.fn is None:
                    continue
                if op.sig:
                    c += 1
                    op.sigval = c
                    op.sem = e
        block = stack.enter_context(nc.Block())
        prog = self

        def run(engname, eng):
            waited = {}
            for op in prog.ops[engname]:
                for d in op.deps:
                    if waited.get(d.sem, 0) >= d.sigval:
                        continue
                    eng.wait_ge(sems[d.sem], d.sigval)
                    waited[d.sem] = d.sigval
                if op.fn is None:
                    continue
                ins = op.fn(eng)
                if op.sig:
                    ins.then_inc(sems[op.sem], 16 if op.dma else 1)
            for op in prog.dma_hist[engname][-prog.ndma:]:
                if waited.get(op.sem, 0) < op.sigval:
                    eng.wait_ge(sems[op.sem], op.sigval)
                    waited[op.sem] = op.sigval

        @block.tensor
        def _(e):
            run("pe", e)

        @block.scalar
        def _(e):
            run("act", e)

        @block.vector
        def _(e):
            run("dve", e)

        @block.gpsimd
        def _(e):
            run("pool", e)

        @block.sync
        def _(e):
            run("sp", e)


import ml_dtypes
from contextlib import ExitStack
from concourse.bass_utils import run_bass_kernel_spmd

D = 1024
DFF = 2816
NF = DFF // 128
INW = 2464
EPS = 1e-6
GN_EPS = 1e-5
TWO_PI = 6.283185307179586
C1 = 6.28125
C2 = TWO_PI - C1
MAGIC = 12582912.0
ARENA = 60416


class _Stop(Exception):
    pass


def build(S, NSEQ, stop=None):
    NT = S // 128
    NQB = S // 512
    nc = bass.Bass("TRN2", target_bir_lowering=False)

    def din(name, shape, dt=F32):
        return nc.dram_tensor(name, shape, dt, kind="ExternalInput").ap()

    x_d = din("x", [NSEQ * S, D])
    pos_d = din("positions", [NSEQ * S], I32)
    n1_d = din("norm1_g", [1, D])
    win_d = din("w_in", [D, INW])
    lf_d = din("ret_decay_logit_fwd", [1, 8])
    lb_d = din("ret_decay_logit_bwd", [1, 8])
    gqa_d = din("q_a_norm_g", [1, 256])
    wuq_d = din("w_uq", [256, 768])
    gkva_d = din("kv_a_norm_g", [1, 128])
    wukv_d = din("w_ukv", [128, 1024])
    gq_d = din("q_norm_g", [1, 96])
    gk_d = din("k_norm_g", [1, 96])
    wo_d = din("w_o", [D, D])
    n2_d = din("norm2_g", [1, D])
    wg_d = din("w_gate", [D, DFF])
    wu_d = din("w_up", [D, DFF])
    wd_d = din("w_down", [DFF, D])
    cf_d = din("cf", [128, 308])
    id_d = din("ident", [128, 128], BF16)
    out_d = nc.dram_tensor("out", [NSEQ * S, D], F32, kind="ExternalOutput").ap()
    win_s = nc.dram_tensor("win_s", [D, INW], BF16).ap()
    wuq_s = nc.dram_tensor("wuq_s", [256, 768], BF16).ap()
    wukv_s = nc.dram_tensor("wukv_s", [128, 1024], BF16).ap()
    wo_s = nc.dram_tensor("wo_s", [D, D], BF16).ap()
    wgu_s = nc.dram_tensor("wgu_s", [NF, 128, 2, 8, 128], BF16).ap()
    wd_s = nc.dram_tensor("wd_s", [DFF, D], BF16).ap()

    st = ExitStack()
    with st:
        def sb(name, shape, dt):
            return st.enter_context(nc.sbuf_tensor(name, shape, dt))

        AR = sb("arena", [128, ARENA], BF16)
        YT = sb("YT", [128, 8 * S], BF16)
        XT = [sb("XT%d" % i, [128, D], F32) for i in range(2)]
        H = sb("H", [128, D], BF16)
        HT = [sb("HT%d" % i, [128, D], BF16) for i in range(2)]
        CF = sb("CF", [128, 308], F32)
        ID = sb("ID", [128, 128], BF16)
        G1 = sb("G1", [128, 8], F32)
        G2 = sb("G2", [128, 8], F32)
        GQA = sb("GQA", [128, 2], F32)
        GKVA = sb("GKVA", [128, 1], F32)
        GQS = sb("GQS", [128, 96], F32)
        GK = sb("GK", [128, 96], F32)
        LGT = sb("LGT", [128, 16], F32)
        LG = sb("LG", [128, 16], F32)
        ZT = sb("ZT", [128, 48], F32)
        ZX = sb("ZX", [128, 48], F32)
        XFB = sb("XFB", [128, 16], F32)
        STT_ = sb("STATE", [128, 512], F32)
        POSI = sb("POSI", [128, NT], I32)
        POSF = sb("POSF", [128, NT], F32)
        SM = sb("SM", [128, 64], F32)
        SSR = sb("SSR", [128, NT], F32)
        PS = [st.enter_context(nc.psum_tensor("ps%d" % i, [128, 1024], F32)) for i in range(4)]

        def bank(i):
            return PS[i // 2][:, (i % 2) * 512:(i % 2 + 1) * 512]

        def bankb(i):
            return bank(i).bitcast(BF16)

        def arb(off, n):
            return AR[:, off:off + n]

        def arf(off, n):
            return AR[:, off:off + 2 * n].bitcast(F32)

        P = Prog(nc)
        toks = {}

        def T(name):
            t = toks.get(name)
            if t is None:
                excl = (name[0] == "b" and name[1:].isdigit()) or name in ("S", "S0", "S1", "bO0", "bO1")
                t = toks[name] = Tok(name, excl)
            return t

        def R(*names):
            return [T(n) for n in names]

        def MM(out, lhsT, rhs, start, stop, r, w, **kw):
            return P.add("pe", lambda e: e.matmul(out, lhsT=lhsT, rhs=rhs, start=start, stop=stop, **kw), r, w)

        def TR(out, in_, r, w):
            return P.add("pe", lambda e: e.transpose(out=out, in_=in_, identity=ID[:, :]), r + [T("ID")], w)

        def TTo(eng, out, in0, in1, op, r, w):
            return P.add(eng, lambda e: e.tensor_tensor(out=out, in0=in0, in1=in1, op=op), r, w)

        def TS(out, in0, s1, s2, op0, op1, r, w):
            if s2 is None:
                return P.add("dve", lambda e: e.tensor_scalar(out=out, in0=in0, scalar1=s1, scalar2=None, op0=op0), r, w)
            return P.add("dve", lambda e: e.tensor_scalar(out=out, in0=in0, scalar1=s1, scalar2=s2, op0=op0, op1=op1), r, w)

        def STT(out, in0, scalar, in1, op0, op1, r, w):
            return P.add("dve", lambda e: e.scalar_tensor_tensor(out=out, in0=in0, scalar=scalar, in1=in1, op0=op0, op1=op1), r, w)

        def ACTV(out, in_, func, r, w, scale=1.0, bias=0.0, accum=None):
            if accum is None:
                return P.add("act", lambda e: e.activation(out=out, in_=in_, func=func, bias=bias, scale=scale), r, w)
            return P.add("act", lambda e: e.activation(out=out, in_=in_, func=func, bias=bias, scale=scale, accum_out=accum), r, w)

        def CP(eng, out, in_, r, w):
            if eng == "act":
                return P.add("act", lambda e: e.copy(out=out, in_=in_), r, w)
            return P.add(eng, lambda e: e.tensor_copy(out=out, in_=in_), r, w)

        def RED(out, in_, r, w):
            return P.add("dve", lambda e: e.reduce_sum(out=out, in_=in_, axis=AX.X), r, w)

        def RECIP(out, in_, r, w):
            return P.add("dve", lambda e: e.reciprocal(out=out, in_=in_), r, w)

        def MEMSET(eng, ap, val, w):
            return P.add(eng, lambda e: e.memset(ap, val), [], w)

        def DMA(q, out, in_, r, w, slow=False):
            if slow:
                return P.add(q, lambda e: e.dma_start(out=out, in_=in_, allow_slow_non_contiguous=True), r, w, dma=True)
            return P.add(q, lambda e: e.dma_start(out=out, in_=in_), r, w, dma=True)

        def rsqrt_chain(dst, src, n_inv, eps, r, w, tmp):
            ACTV(tmp, src, AF.Ln, r, [T("rs_tmp")], scale=n_inv, bias=eps)
            ACTV(dst, tmp, AF.Exp, [T("rs_tmp")], w, scale=-0.5)

        DMA("sp", CF[:, :], cf_d[:, :], [], R("CF"))
        DMA("sp", ID[:, :], id_d[:, :], [], R("ID"))
        DMA("sp", G1[:, :], n1_d.rearrange("o (k p) -> p (o k)", p=128), [], R("G1"), slow=True)
        DMA("sp", G2[:, :], n2_d.rearrange("o (k p) -> p (o k)", p=128), [], R("G2"), slow=True)
        DMA("sp", GQA[:, :], gqa_d.rearrange("o (k p) -> p (o k)", p=128), [], R("GQA"), slow=True)
        DMA("sp", GKVA[:, :], gkva_d.rearrange("o (k p) -> p (o k)", p=128), [], R("GKVA"), slow=True)
        DMA("sp", GQS[:, :], gq_d.to_broadcast([128, 96]), [], R("GQS"), slow=True)
        DMA("sp", GK[:, :], gk_d.to_broadcast([128, 96]), [], R("GK"), slow=True)
        DMA("sp", LGT[:, 0:8], lf_d.to_broadcast([128, 8]), [], R("LGT"), slow=True)
        DMA("sp", LGT[:, 8:16], lb_d.to_broadcast([128, 8]), R("LGT"), R("LGT"), slow=True)
        TS(GQS[:, :], GQS[:, :], 96.0 ** -0.5, None, ALU.mult, None, R("GQS"), R("GQS"))
        ACTV(LG[:, :], LGT[:, :], AF.Exp, R("LGT"), R("LG"), scale=-1.0)
        ACTV(LG[:, :], LG[:, :], AF.Ln, R("LG"), R("LG"), bias=1.0)
        TS(LG[:, :], LG[:, :], -1.0, None, ALU.mult, None, R("LG"), R("LG"))
        IDX = CF[:, 304:308]
        for j, (lo, ic) in enumerate([(0, 0), (8, 1), (0, 2), (8, 3)]):
            TS(ZX[:, j * 8:(j + 1) * 8], LG[:, lo:lo + 8], IDX[:, ic:ic + 1], None, ALU.mult, None, R("LG", "CF"), R("ZX"))
        TS(ZX[:, 32:48], LG[:, 0:16], 128.0, None, ALU.mult, None, R("LG"), R("ZX"))
        ACTV(ZT[:, :], ZX[:, :], AF.Exp, R("ZX"), R("ZT"))
        XFBv = XFB[:, :].rearrange("p (h d) -> p h d", d=2)
        CP("dve", XFBv[:, :, 0], ZT[:, 16:24], R("ZT"), R("XFB"))
        CP("dve", XFBv[:, :, 1], ZT[:, 24:32], R("ZT", "XFB"), R("XFB"))
        ZF, ZB, GCF, GCB = ZT[:, 0:8], ZT[:, 8:16], ZT[:, 32:40], ZT[:, 40:48]

        SF = [arf(i * 5632, 2816) for i in range(2)]
        SBF = [arb(11264 + i * 2816, 2816) for i in range(2)]
        cnt = [0]

        def prep(src, K, N, gain, gtok, dst_fn):
            for kc in range(K // 128):
                s = cnt[0] % 2
                cnt[0] += 1
                DMA("sp", SF[s][:, 0:N], src[kc * 128:(kc + 1) * 128, :], [], R("SF%d" % s))
                if gain is not None:
                    TS(SBF[s][:, 0:N], SF[s][:, 0:N], gain[:, kc:kc + 1], None, ALU.mult, None,
                       R("SF%d" % s, gtok), R("SB%d" % s))
                else:
                    CP("act", SBF[s][:, 0:N], SF[s][:, 0:N], R("SF%d" % s), R("SB%d" % s))
                dst_fn(kc, SBF[s], R("SB%d" % s))

        prep(win_d, D, INW, G1, "G1", lambda kc, t, r: DMA("pool", win_s[kc * 128:(kc + 1) * 128, :], t[:, 0:INW], r, []))
        prep(wuq_d, 256, 768, GQA, "GQA", lambda kc, t, r: DMA("pool", wuq_s[kc * 128:(kc + 1) * 128, :], t[:, 0:768], r, []))
        prep(wukv_d, 128, 1024, GKVA, "GKVA", lambda kc, t, r: DMA("pool", wukv_s[:, :], t[:, 0:1024], r, []))
        prep(wo_d, D, D, None, None, lambda kc, t, r: DMA("pool", wo_s[kc * 128:(kc + 1) * 128, :], t[:, 0:D], r, []))
        for gi, wsrc in enumerate((wg_d, wu_d)):
            prep(wsrc, D, DFF, G2, "G2",
                 lambda kc, t, r, gi=gi: DMA("pool", wgu_s[:, :, gi, kc, :].rearrange("f p n -> p f n"),
                                             t[:, 0:DFF].rearrange("p (f n) -> p f n", n=128), r, []))
        prep(wd_d, DFF, D, None, None, lambda kc, t, r: DMA("pool", wd_s[kc * 128:(kc + 1) * 128, :], t[:, 0:D], r, []))
        P.barrier()

        xc = [0]

        def xnorm(row0, xdst=None, xtok=None, ht_eng="act"):
            s = xc[0] % 2
            xc[0] += 1
            if xdst is None:
                xdst, xtok = XT[s][:, :], "XT%d" % s
                DMA("sp", xdst, x_d[row0:row0 + 128, :], [], R(xtok))
            ACTV(H[:, :], xdst, AF.Square, R(xtok), R("H", "xss"), accum=SM[:, 0:1])
            rsqrt_chain(SM[:, 2:3], SM[:, 0:1], 1.0 / D, EPS, R("xss"), R("xrs"), SM[:, 1:2])
            TS(H[:, :], xdst, SM[:, 2:3], None, ALU.mult, None, R(xtok, "xrs"), R("H"))
            bT = bankb(0)
            for k in range(8):
                TR(bT[:, k * 128:(k + 1) * 128], H[:, k * 128:(k + 1) * 128], R("H"), R("b0"))
            CP(ht_eng, HT[s][:, :], bT[:, :], R("b0"), R("HT%d" % s))
            return HT[s], "HT%d" % s

        def proj(bi, ht, httok, W, wtok, c0, n):
            for k in range(8):
                MM(bank(bi)[:, 0:n], ht[:, k * 128:(k + 1) * 128], W[:, k, c0:c0 + n], k == 0, k == 7,
                   R(httok, wtok), R("b%d" % bi))

        def rotary(src, stok, dst, dtok, cos, sin, tabtok, nh, half, tA, tB, tAtok, tBtok):
            cb = cos.unsqueeze(1).unsqueeze(1).to_broadcast([128, nh, 2, half])
            sbh = sin.unsqueeze(1).to_broadcast([128, nh, half])
            TTo("dve", tA, src, cb, ALU.mult, R(stok, tabtok, tAtok), R(tAtok))
            TTo("pool", tB[:, :, 0, :], src[:, :, 1, :], sbh, ALU.mult, R(stok, tabtok, tBtok), R(tBtok))
            TTo("pool", tB[:, :, 1, :], src[:, :, 0, :], sbh, ALU.mult, R(stok, tabtok, tBtok), R(tBtok))
            TTo("dve", dst[:, :, 0, :], tA[:, :, 0, :], tB[:, :, 0, :], ALU.subtract, R(tAtok, tBtok, dtok), R(dtok))
            TTo("dve", dst[:, :, 1, :], tA[:, :, 1, :], tB[:, :, 1, :], ALU.add, R(tAtok, tBtok, dtok), R(dtok))

        def trig_tables(n, invf, cosd, sind, tmp, tabtok):
            ang, u, r_ = tmp
            TTo("dve", ang.rearrange("p (t f) -> p t f", f=n), POSF[:, :].unsqueeze(2).to_broadcast([128, NT, n]),
                invf.unsqueeze(1).to_broadcast([128, NT, n]), ALU.mult, R("POSF", "CF"), R("ang"))
            for shift, dst in ((0.0, sind), (np.pi / 2, cosd)):
                if shift:
                    TS(ang, ang, shift, None, ALU.add, None, R("ang"), R("ang"))
                TS(u, ang, 1.0 / TWO_PI, MAGIC, ALU.mult, ALU.add, R("ang"), R("trU"))
                TS(u, u, -MAGIC, None, ALU.add, None, R("trU"), R("trU"))
                STT(r_, u, -C1, ang, ALU.mult, ALU.add, R("trU", "ang"), R("trR"))
                STT(r_, u, -C2, r_, ALU.mult, ALU.add, R("trU", "trR"), R("trR"))
                ACTV(dst, r_, AF.Sin, R("trR"), R(tabtok))

        def checkpoint(k):
            if stop is not None and k == stop:
                raise _Stop()

        try:
            for sq in range(NSEQ):
                base = sq * S
                checkpoint(0)
                DMA("sp", POSI[:, :], pos_d[base:base + S].rearrange("(t p) -> p t", p=128), [], R("POSI"), slow=True)
                CP("dve", POSF[:, :], POSI[:, :], R("POSI"), R("POSF"))

                checkpoint(1)
                WR = arb(0, 12288).rearrange("p (k n) -> p k n", k=8)
                WR1 = arb(0, 8192).rearrange("p (k n) -> p k n", k=8)
                KTOK = arb(12288, NT * 512).rearrange("p (t n) -> p t n", n=512)
                RBC = arb(28672, (NT // 2 + 1) * 512).rearrange("p (t n) -> p t n", n=512)
                RJ = arb(37888, 1024).rearrange("p (t n) -> p t n", n=512)
                DTm = arf(38912, 1024)
                COSR = arf(40960, NT * 32)
                SINR = arf(40960 + NT * 64, NT * 32)
                o = 40960 + NT * 128
                F = [arf(o + i * 1024, 512) for i in range(4)]
                o += 4096
                B = [arb(o + i * 1024, 1024) for i in range(6)]
                o += 6144
                assert o <= ARENA
                tmp3 = [arf(12288 + i * (NT * 64), NT * 32) for i in range(3)]
                trig_tables(32, CF[:, 0:32], COSR, SINR, tmp3, "TABR")
                for h in range(8):
                    TS(F[0][:, 0:128], CF[:, 48:176], LG[:, h:h + 1], None, ALU.mult, None, R("CF", "LG"), R("F0"))
                    STT(F[1][:, 0:128], CF[:, 176:304], LG[:, 8 + h:9 + h], F[0][:, 0:128], ALU.mult, ALU.add,
                        R("CF", "LG", "F0"), R("F1"))
                    ACTV(DTm[:, h * 128:(h + 1) * 128], F[1][:, 0:128], AF.Exp, R("F1"), R("DT"))
                MEMSET("dve", STT_[:, :], 0.0, R("ST"))
                P.barrier()

                def v4(ap, nh, half):
                    return ap.rearrange("p (h d f) -> p h d f", h=nh, d=2)

                def v3(ap, a):
                    return ap.rearrange("p (a b) -> p a b", a=a)

                checkpoint(2)
                DMA("sp", WR1[:, :, :], win_s[:, 512:1536].rearrange("(k p) n -> p k n", p=128), [], R("WR"))
                for t in range(NT - 1, -1, -1):
                    ht, htt = xnorm(base + t * 128)
                    proj(1, ht, htt, WR1, "WR", 0, 512)
                    proj(2, ht, htt, WR1, "WR", 512, 512)
                    ACTV(F[0][:, :], bank(1), AF.Copy, R("b1"), R("F0"), scale=0.125)
                    rotary(v4(F[0][:, :], 8, 32), "F0", v4(KTOK[:, t, :], 8, 32), "KTOK%d" % t,
                           COSR[:, t * 32:(t + 1) * 32], SINR[:, t * 32:(t + 1) * 32], "TABR", 8, 32,
                           v4(F[1][:, :], 8, 32), v4(F[2][:, :], 8, 32), "F1", "F2")
                    TTo("dve", v3(B[0][:, 0:512], 8), v3(bank(2), 8), ZB.unsqueeze(2).to_broadcast([128, 8, 64]), ALU.mult,
                        R("b2", "ZT"), R("B0"))
                    for h in range(8):
                        MM(bank(3)[64:128, h * 64:(h + 1) * 64], KTOK[:, t, h * 64:(h + 1) * 64], B[0][:, h * 64:(h + 1) * 64],
                           True, True, R("KTOK%d" % t, "B0"), R("b3"), tile_position=(0, 64))
                    pr = (t % 2) * 64
                    CP("act", RBC[pr:pr + 64, t // 2, :], STT_[64:128, :], R("ST"), R("RBC%d" % t))
                    TTo("dve", v3(F[3][64:128, :], 8), v3(STT_[64:128, :], 8), GCB[64:128, :].unsqueeze(2).to_broadcast([64, 8, 64]),
                        ALU.mult, R("ST", "ZT"), R("F3"))
                    TTo("dve", STT_[64:128, :], F[3][64:128, :], bank(3)[64:128, :], ALU.add, R("F3", "b3"), R("ST"))
                P.barrier()

                checkpoint(3)
                DMA("sp", WR[:, :, 0:512], win_s[:, 0:512].rearrange("(k p) n -> p k n", p=128), [], R("WR"))
                DMA("sp", WR[:, :, 512:1024], win_s[:, 1536:2048].rearrange("(k p) n -> p k n", p=128), R("WR"), R("WR"))
                DMA("sp", WR[:, :, 1024:1536], win_s[:, 1024:1536].rearrange("(k p) n -> p k n", p=128), R("WR"), R("WR"))
                MEMSET("dve", RJ[0:64, 0, :], 0.0, R("RJ0"))
                CP("dve", RJ[64:128, 0, :], RBC[0:64, 0, :], R("RBC0", "RJ0"), R("RJ0"))
                for t in range(NT):
                    ht, htt = xnorm(base + t * 128)
                    proj(1, ht, htt, WR, "WR", 0, 512)
                    proj(2, ht, htt, WR, "WR", 512, 512)
                    proj(3, ht, htt, WR, "WR", 1024, 512)
                    checkpoint(30)
                    ACTV(F[0][:, :], bank(1), AF.Copy, R("b1"), R("F0"))
                    rotary(v4(F[0][:, :], 8, 32), "F0", v4(F[3][:, :], 8, 32), "F3",
                           COSR[:, t * 32:(t + 1) * 32], SINR[:, t * 32:(t + 1) * 32], "TABR", 8, 32,
                           v4(F[1][:, :], 8, 32), v4(F[2][:, :], 8, 32), "F1", "F2")
                    CP("dve", B[0][:, 0:512], F[3][:, :], R("F3"), R("B0"))
                    TTo("dve", B[1][:, :].rearrange("p (h d e) -> p h d e", h=8, d=2),
                        v3(F[3][:, :], 8).unsqueeze(2).to_broadcast([128, 8, 2, 64]),
                        XFB[:, :].rearrange("p (h d) -> p h d", d=2).unsqueeze(3).to_broadcast([128, 8, 2, 64]),
                        ALU.mult, R("F3", "XFB"), R("B1"))
                    checkpoint(31)
                    CP("act", B[2][:, 0:512], bank(3), R("b3"), R("B2a"))
                    TTo("dve", v3(B[2][:, 512:1024], 8), v3(bank(3), 8), ZF.unsqueeze(2).to_broadcast([128, 8, 64]), ALU.mult,
                        R("b3", "ZT"), R("B2b"))
                    checkpoint(32)
                    b6 = bankb(6)
                    for j in range(4):
                        TR(b6[:, j * 128:(j + 1) * 128], B[0][:, j * 128:(j + 1) * 128], R("B0"), R("b6"))
                    for j in range(4):
                        TR(b6[:, 512 + j * 128:512 + (j + 1) * 128], KTOK[:, t, j * 128:(j + 1) * 128], R("KTOK%d" % t), R("b6"))
                    CP("act", B[3][:, :], b6[:, :], R("b6"), R("B3"))
                    for h in range(8):
                        TR(b6[:, h * 128:(h + 1) * 128], B[1][:, h * 128:(h + 1) * 128], R("B1"), R("b6"))
                    CP("dve", B[4][:, :], b6[:, :], R("b6"), R("B4"))
                    checkpoint(33)
                    for h in range(8):
                        j, hp = h // 2, (h % 2) * 64
                        MM(PS[2][:, (h % 2) * 512 + j * 128:(h % 2) * 512 + (j + 1) * 128],
                           B[3][hp:hp + 64, 512 + j * 128:512 + (j + 1) * 128],
                           B[3][hp:hp + 64, j * 128:(j + 1) * 128], True, True, R("B3"), R("S"))
                    for c2 in range(2):
                        TTo("dve", B[5][:, :].rearrange("p (j q c) -> p j q c", j=4, q=2)[:, :, c2, :],
                            PS[2][:, c2 * 512:(c2 + 1) * 512].rearrange("p (j c) -> p j c", j=4),
                            DTm[:, :].rearrange("p (j q c) -> p j q c", j=4, q=2)[:, :, c2, :], ALU.mult, R("S", "DT", "B5"), R("B5"))
                    checkpoint(34)
                    for h in range(8):
                        MM(bank(7)[0:64, h * 64:(h + 1) * 64], KTOK[:, t, h * 64:(h + 1) * 64], B[2][:, 512 + h * 64:512 + (h + 1) * 64],
                           True, True, R("KTOK%d" % t, "B2b"), R("b7"))
                    rj = t % 2
                    for h in range(8):
                        MM(bank(3)[:, h * 64:(h + 1) * 64], B[5][:, h * 128:(h + 1) * 128], B[2][:, h * 64:(h + 1) * 64],
                           True, False, R("B5", "B2a", "B2b"), R("b3"))
                        MM(bank(3)[:, h * 64:(h + 1) * 64], B[4][:, h * 128:(h + 1) * 128], RJ[:, rj, h * 64:(h + 1) * 64],
                           False, True, R("B4", "RJ%d" % rj), R("b3"))
                    checkpoint(35)
                    if t + 1 < NT:
                        nrj = (t + 1) % 2
                        TTo("dve", v3(F[0][0:64, :], 8), v3(STT_[0:64, :], 8), GCF[0:64, :].unsqueeze(2).to_broadcast([64, 8, 64]),
                            ALU.mult, R("ST", "ZT", "F0"), R("F0"))
                        TTo("dve", STT_[0:64, :], F[0][0:64, :], bank(7)[0:64, :], ALU.add, R("F0", "b7"), R("ST"))
                        CP("act", RJ[0:64, nrj, :], STT_[0:64, :], R("ST"), R("RJ%d" % nrj))
                        pr = ((t + 1) % 2) * 64
                        CP("dve", RJ[64:128, nrj, :], RBC[pr:pr + 64, (t + 1) // 2, :], R("RBC%d" % (t + 1), "RJ%d" % nrj), R("RJ%d" % nrj))
                    checkpoint(36)
                    ACTV(F[0][:, :], bank(3), AF.Copy, R("b3"), R("F0"))
                    RED(SM[:, 8:16], v3(F[0][:, :], 8), R("F0"), R("mu"))
                    ACTV(F[1][:, :], F[0][:, :], AF.Square, R("F0"), R("F1"))
                    RED(SM[:, 16:24], v3(F[1][:, :], 8), R("F1"), R("m2"))
                    TS(SM[:, 8:16], SM[:, 8:16], 1.0 / 64, None, ALU.mult, None, R("mu"), R("mu"))
                    TTo("dve", SM[:, 24:32], SM[:, 8:16], SM[:, 8:16], ALU.mult, R("mu"), R("msq"))
                    STT(SM[:, 16:24], SM[:, 16:24], 1.0 / 64, SM[:, 24:32], ALU.mult, ALU.subtract, R("m2", "msq"), R("m2"))
                    rsqrt_chain(SM[:, 32:40], SM[:, 16:24], 1.0, GN_EPS, R("m2"), R("grs"), SM[:, 40:48])
                    TTo("dve", v3(F[2][:, :], 8), v3(F[0][:, :], 8), SM[:, 8:16].unsqueeze(2).to_broadcast([128, 8, 64]), ALU.subtract,
                        R("F0", "mu"), R("F2"))
                    TTo("dve", v3(F[2][:, :], 8), v3(F[2][:, :], 8), SM[:, 32:40].unsqueeze(2).to_broadcast([128, 8, 64]), ALU.mult,
                        R("F2", "grs"), R("F2"))
                    ACTV(F[1][:, :], bank(2), AF.Exp, R("b2", "F1"), R("F1"), scale=-1.0)
                    ACTV(F[1][:, :], F[1][:, :], AF.Ln, R("F1"), R("F1"), bias=1.0)
                    ACTV(F[1][:, :], F[1][:, :], AF.Exp, R("F1"), R("F1"), scale=-1.0)
                    TTo("dve", F[3][:, :], bank(2), F[1][:, :], ALU.mult, R("b2", "F1", "F3"), R("F3"))
                    TTo("dve", B[0][:, 0:512], F[2][:, :], F[3][:, :], ALU.mult, R("F2", "F3", "B0"), R("B0"))
                    checkpoint(37)
                    bT = bankb(0)
                    for j in range(4):
                        TR(bT[:, j * 128:(j + 1) * 128], B[0][:, j * 128:(j + 1) * 128], R("B0"), R("b0"))
                    CP("act", YT[:, :].rearrange("p (k s) -> p k s", k=8)[:, 0:4, t * 128:(t + 1) * 128],
                       bT[:, 0:512].rearrange("p (k s) -> p k s", k=4), R("b0"), R("YTr%d" % t))
                P.barrier()

                checkpoint(4)
                CQT = arb(0, 2 * S).rearrange("p (k s) -> p k s", k=2)
                CKVT = arb(8192, S)
                KRB = arf(12288, NT * 32)
                KT = arb(14336, 4 * S).rearrange("p (h s) -> p h s", h=4)
                VA = arb(30720, NT * 512).rearrange("p (t h c) -> p t h c", h=4, c=128)
                WM = arb(47104, 3328).rearrange("p (k n) -> p k n", k=8)
                QT = arb(47104, 2048).rearrange("p (h s) -> p h s", h=4)
                RL = arf(47104 + 2048, 512)
                WUQ = arb(50432, 1536).rearrange("p (k n) -> p k n", k=2)
                WUKV = arb(51968, 1024)
                PTA = [arb(52992 + i * 1024, 1024) for i in range(3)]
                COSM = arf(56064, NT * 16)
                SINM = arf(57088, NT * 16)
                MF = [arf(58112 + i * 1024, 512) for i in range(2)]
                tmpm = [arf(52992 + i * 1024, NT * 16) for i in range(3)]
                DMA("sp", WM[:, :, :], win_s[:, 2048:2464].rearrange("(k p) n -> p k n", p=128), [], R("WM"))
                DMA("sp", WUQ[:, :, :], wuq_s.rearrange("(k p) n -> p k n", p=128), [], R("WUQ"))
                DMA("sp", WUKV[:, :], wukv_s[:, :], [], R("WUKV"))
                checkpoint(39)
                trig_tables(16, CF[:, 32:48], COSM, SINM, tmpm, "TABM")
                P.barrier()
                checkpoint(40)
                for t in range(NT):
                    ht, htt = xnorm(base + t * 128)
                    proj(1, ht, htt, WM, "WM", 0, 416)
                    checkpoint(41)
                    b1 = bank(1)
                    ACTV(MF[0][:, 0:256], b1[:, 0:256], AF.Square, R("b1"), R("MF0", "ssq"), accum=SM[:, 8:9])
                    rsqrt_chain(SM[:, 10:11], SM[:, 8:9], 1.0 / 256, EPS, R("ssq"), R("rq"), SM[:, 9:10])
                    TS(B_cq := MF[1][:, 0:128].bitcast(BF16), b1[:, 0:256], SM[:, 10:11], None, ALU.mult, None, R("b1", "rq"), R("cq"))
                    ACTV(MF[0][:, 256:384], b1[:, 256:384], AF.Square, R("b1"), R("MF0b", "sskv"), accum=SM[:, 11:12])
                    rsqrt_chain(SM[:, 13:14], SM[:, 11:12], 1.0 / 128, EPS, R("sskv"), R("rkv"), SM[:, 12:13])
                    TS(B_ckv := MF[1][:, 128:192].bitcast(BF16), b1[:, 256:384], SM[:, 13:14], None, ALU.mult, None, R("b1", "rkv"), R("ckv"))
                    checkpoint(42)
                    b2 = bankb(2)
                    TR(b2[:, 0:128], B_cq[:, 0:128], R("cq"), R("b2"))
                    TR(b2[:, 128:256], B_cq[:, 128:256], R("cq"), R("b2"))
                    TR(b2[:, 256:384], B_ckv[:, 0:128], R("ckv"), R("b2"))
                    checkpoint(45)
                    CP("act", CQT[:, :, t * 128:(t + 1) * 128], b2[:, 0:256].rearrange("p (k s) -> p k s", k=2), R("b2"), R("CQT%d" % t))
                    checkpoint(46)
                    CP("dve", CKVT[:, t * 128:(t + 1) * 128], b2[:, 256:384], R("b2"), R("CKVT%d" % t))
                    checkpoint(43)
                    ACTV(MF[0][:, 384:416], b1[:, 384:416], AF.Square, R("b1"), R("MF0c", "SSR"), accum=SSR[:, t:t + 1])
                    TTo("dve", MF[0][:, 448:480], b1[:, 384:416], GK[:, 64:96], ALU.mult, R("b1", "GK"), R("krg"))
                    checkpoint(44)
                    rotary(MF[0][:, 448:480].rearrange("p (h d f) -> p h d f", h=1, d=2), "krg",
                           KRB[:, t * 32:(t + 1) * 32].rearrange("p (h d f) -> p h d f", h=1, d=2), "KRB%d" % t,
                           COSM[:, t * 16:(t + 1) * 16], SINM[:, t * 16:(t + 1) * 16], "TABM", 1, 16,
                           MF[1][:, 256:288].rearrange("p (h d f) -> p h d f", h=1, d=2),
                           MF[1][:, 288:320].rearrange("p (h d f) -> p h d f", h=1, d=2), "mrA", "mrB")
                P.barrier()
                checkpoint(5)
                for hh in range(2):
                    MEMSET("pool", VA[:, :, :, 64:128], 1.0, R("VAones"))
                    for t in range(NT):
                        MM(bank(3), CKVT[:, t * 128:(t + 1) * 128], WUKV[:, hh * 512:(hh + 1) * 512], True, True,
                           R("CKVT%d" % t, "WUKV"), R("b3"))
                        kv = bank(3).rearrange("p (h c) -> p h c", h=4)
                        ACTV(MF[0][:, 0:256].rearrange("p (h c) -> p h c", h=4), kv[:, :, 0:64], AF.Square, R("b3"), R("MF0"))
                        RED(SM[:, 16:20], MF[0][:, 0:256].rearrange("p (h c) -> p h c", h=4), R("MF0"), R("ssh"))
                        TS(SM[:, 16:20], SM[:, 16:20], SSR[:, t:t + 1], None, ALU.add, None, R("ssh", "SSR"), R("ssh"))
                        rsqrt_chain(SM[:, 24:28], SM[:, 16:20], 1.0 / 96, EPS, R("ssh"), R("rk"), SM[:, 20:24])
                        TTo("dve", MF[1][:, 0:256].rearrange("p (h c) -> p h c", h=4), kv[:, :, 0:64],
                            SM[:, 24:28].unsqueeze(2).to_broadcast([128, 4, 64]), ALU.mult, R("b3", "rk"), R("MF1"))
                        KTM = MF[0][:, 256:448].bitcast(BF16).rearrange("p (h c) -> p h c", h=4)
                        TTo("dve", KTM[:, :, 0:64], MF[1][:, 0:256].rearrange("p (h c) -> p h c", h=4),
                            GK[:, 0:64].unsqueeze(1).to_broadcast([128, 4, 64]), ALU.mult, R("MF1", "GK"), R("KTM"))
                        TTo("dve", KTM[:, :, 64:96], KRB[:, t * 32:(t + 1) * 32].unsqueeze(1).to_broadcast([128, 4, 32]),
                            SM[:, 24:28].unsqueeze(2).to_broadcast([128, 4, 32]), ALU.mult, R("KRB%d" % t, "rk", "KTM"), R("KTM"))
                        b2 = bankb(2)
                        for hl in range(4):
                            TR(b2[0:96, hl * 128:(hl + 1) * 128], KTM[:, hl, :], R("KTM"), R("b2"))
                        CP("act", KT[0:96, :, t * 128:(t + 1) * 128], b2[0:96, 0:512].rearrange("p (h s) -> p h s", h=4),
                           R("b2"), R("KT%d" % t))
                        CP("dve", VA[:, t, :, 0:64], kv[:, :, 64:128], R("b3", "VAones"), R("VA%d" % t))
                    P.barrier()
                    checkpoint(6)
                    ac = [0]
                    for qb in range(NQB):
                        for s in range(4):
                            tq = qb * 4 + s
                            b6 = bank(6)
                            for k in range(2):
                                MM(b6[:, 0:384], CQT[:, k, tq * 128:(tq + 1) * 128], WUQ[:, k, hh * 384:(hh + 1) * 384], k == 0, k == 1,
                                   R("CQT%d" % tq, "WUQ"), R("b6"))
                            q3 = b6[:, 0:384].rearrange("p (h c) -> p h c", h=4)
                            m0 = MF[0][:, 0:384].rearrange("p (h c) -> p h c", h=4)
                            m1 = MF[1][:, 0:384].rearrange("p (h c) -> p h c", h=4)
                            ACTV(m0, q3, AF.Square, R("b6"), R("MF0"))
                            RED(SM[:, 16:20], m0, R("MF0"), R("ssh"))
                            rsqrt_chain(SM[:, 24:28], SM[:, 16:20], 1.0 / 96, EPS, R("ssh"), R("rk"), SM[:, 20:24])
                            TTo("dve", m1, q3, SM[:, 24:28].unsqueeze(2).to_broadcast([128, 4, 96]), ALU.mult, R("b6", "rk"), R("MF1"))
                            TTo("dve", m1, m1, GQS[:, :].unsqueeze(1).to_broadcast([128, 4, 96]), ALU.mult, R("MF1", "GQS"), R("MF1"))
                            qtk = H[:, 0:384].rearrange("p (h c) -> p h c", h=4)
                            CP("dve", qtk[:, :, 0:64], m1[:, :, 0:64], R("MF1", "H"), R("H"))
                            src = m1[:, :, 64:96].rearrange("p h (d f) -> p h d f", d=2)
                            dst = qtk[:, :, 64:96].rearrange("p h (d f) -> p h d f", d=2)
                            tA = m0[:, :, 0:32].rearrange("p h (d f) -> p h d f", d=2)
                            tB = m0[:, :, 32:64].rearrange("p h (d f) -> p h d f", d=2)
                            rotary(src, "MF1", dst, "H", COSM[:, tq * 16:(tq + 1) * 16], SINM[:, tq * 16:(tq + 1) * 16], "TABM", 4, 16, tA, tB, "MF0", "MF0")
                            b7 = bankb(7)
                            for hl in range(4):
                                TR(b7[0:96, hl * 128:(hl + 1) * 128], qtk[:, hl, :], R("H"), R("b7"))
                            CP("dve", QT[0:96, :, s * 128:(s + 1) * 128], b7[0:96, 0:512].rearrange("p (h s) -> p h s", h=4),
                               R("b7"), R("QT"))
                        for hl in range(4):
                            h = hh * 4 + hl
                            osl = ac[0] % 2
                            ac[0] += 1
                            bO = bank(4 + osl)
                            for kp in range(NT // 2):
                                ssl = kp % 2
                                psl = kp % 3
                                for j in range(2):
                                    kt = 2 * kp + j
                                    MM(PS[ssl][:, j * 512:(j + 1) * 512], KT[0:96, hl, kt * 128:(kt + 1) * 128], QT[0:96, hl, :], True, True,
                                       R("KT%d" % kt, "QT"), R("S%d" % ssl))
                                ACTV(PTA[psl][:, :], PS[ssl][:, :], AF.Exp, R("S%d" % ssl), R("PTA%d" % psl))
                                for j in range(2):
                                    kt = 2 * kp + j
                                    MM(bO, VA[:, kt, hl, :], PTA[psl][:, j * 512:(j + 1) * 512], kp == 0 and j == 0,
                                       kp == NT // 2 - 1 and j == 1, R("VA%d" % kt, "PTA%d" % psl), R("bO%d" % osl))
                            RECIP(RL[64:128, :], bO[64:128, :], R("bO%d" % osl), R("RL"))
                            pr = (h % 2) * 64
                            TTo("dve", YT[:, :].rearrange("p (k s) -> p k s", k=8)[pr:pr + 64, 4 + h // 2, qb * 512:(qb + 1) * 512],
                                bO[0:64, :], RL[64:128, :], ALU.mult, R("bO%d" % osl, "RL"), R("YTm%d_%d" % (h, qb)))
                    P.barrier()

                checkpoint(7)
                WO = arb(0, 8192).rearrange("p (k n) -> p k n", k=8)
                X1 = arf(8192, 4096).rearrange("p (s n) -> p s n", s=4)
                H2T = arb(16384, 4096).rearrange("p (k s) -> p k s", k=8)
                AT = arb(20480, NF * 512).rearrange("p (f s) -> p f s", f=NF)
                WGU = [arb(31744 + i * 2048, 2048).rearrange("p (g k n) -> p g k n", g=2, k=8) for i in range(2)]
                WDQ = [arb(35840 + i * 5632, 5632).rearrange("p (f n) -> p f n", f=NF) for i in range(3)]
                SG = [arf(52736 + i * 1024, 512) for i in range(2)]
                DMA("sp", WO[:, :, :], wo_s.rearrange("(k p) n -> p k n", p=128), [], R("WO"))
                YT3 = YT[:, :].rearrange("p (k s) -> p k s", k=8)
                wq = [0]
                for qb in range(NQB):
                    for s in range(4):
                        tq = qb * 4 + s
                        row0 = base + tq * 128
                        xs = "X1_%d" % s
                        DMA("sp", X1[:, s, :], x_d[row0:row0 + 128, :], [], R(xs))
                        for c2 in range(2):
                            for k in range(8):
                                MM(bank(c2), YT3[:, k, tq * 128:(tq + 1) * 128], WO[:, k, c2 * 512:(c2 + 1) * 512], k == 0, k == 7,
                                   R("WO"), R("b%d" % c2))
                            TTo("dve", X1[:, s, c2 * 512:(c2 + 1) * 512], X1[:, s, c2 * 512:(c2 + 1) * 512], bank(c2), ALU.add,
                                R(xs, "b%d" % c2), R(xs))
                        ACTV(H[:, :], X1[:, s, :], AF.Square, R(xs), R("H", "xss"), accum=SM[:, 0:1])
                        rsqrt_chain(SM[:, 2:3], SM[:, 0:1], 1.0 / D, EPS, R("xss"), R("xrs"), SM[:, 1:2])
                        TS(H[:, :], X1[:, s, :], SM[:, 2:3], None, ALU.mult, None, R(xs, "xrs"), R("H"))
                        bT = bankb(2)
                        for k in range(8):
                            TR(bT[:, k * 128:(k + 1) * 128], H[:, k * 128:(k + 1) * 128], R("H"), R("b2"))
                        CP("act", H2T[:, :, s * 128:(s + 1) * 128], bT[:, :].rearrange("p (k s) -> p k s", k=8), R("b2"), R("H2T"))
                    for f in range(NF):
                        sl = f % 2
                        DMA("sp", WGU[sl][:, :, :, :], wgu_s[f], [], R("WGU%d" % sl))
                        for g in range(2):
                            bi = 3 + 2 * g + sl
                            for k in range(8):
                                MM(bank(bi), WGU[sl][:, g, k, :], H2T[:, k, :], k == 0, k == 7, R("WGU%d" % sl, "H2T"), R("b%d" % bi))
                        bG, bU = bank(3 + sl), bank(5 + sl)
                        ACTV(SG[0][:, :], bG, AF.Exp, R("b%d" % (3 + sl)), R("SG0"), scale=-1.0)
                        ACTV(SG[0][:, :], SG[0][:, :], AF.Ln, R("SG0"), R("SG0"), bias=1.0)
                        ACTV(SG[0][:, :], SG[0][:, :], AF.Exp, R("SG0"), R("SG0"), scale=-1.0)
                        TTo("dve", SG[1][:, :], bG, SG[0][:, :], ALU.mult, R("b%d" % (3 + sl), "SG0"), R("SG1"))
                        TTo("dve", AT[:, f, :], SG[1][:, :], bU, ALU.mult, R("SG1", "b%d" % (5 + sl)), R("AT"))
                    for q4 in range(4):
                        wsl = wq[0] % 3
                        wq[0] += 1
                        DMA("sp", WDQ[wsl][:, :, :], wd_s[:, q4 * 256:(q4 + 1) * 256].rearrange("(f p) n -> p f n", p=128), [],
                            R("WDQ%d" % wsl))
                        for s in range(4):
                            bi = s % 2
                            for f in range(NF):
                                MM(bank(bi)[:, 0:256], AT[:, f, s * 128:(s + 1) * 128], WDQ[wsl][:, f, :], f == 0, f == NF - 1,
                                   R("AT", "WDQ%d" % wsl), R("b%d" % bi))
                            TTo("dve", X1[:, s, q4 * 256:(q4 + 1) * 256], X1[:, s, q4 * 256:(q4 + 1) * 256], bank(bi)[:, 0:256], ALU.add,
                                R("X1_%d" % s, "b%d" % bi), R("X1_%d" % s))
                    for s in range(4):
                        row0 = base + (qb * 4 + s) * 128
                        DMA("pool", out_d[row0:row0 + 128, :], X1[:, s, :], R("X1_%d" % s), [])
                P.barrier()
        except _Stop:
            pass
        P.emit(st)
    return nc


def make_consts():
    cf = np.zeros((128, 308), np.float32)
    cf[:, 0:32] = (10000.0 ** (-np.arange(32, dtype=np.float32) / 32.0))[None, :]
    cf[:, 32:48] = (10000.0 ** (-np.arange(16, dtype=np.float32) / 16.0))[None, :]
    m = np.arange(128, dtype=np.float32)[:, None]
    c = np.arange(128, dtype=np.float32)[None, :]
    cf[:, 48:176] = np.maximum(c - m, 0.0)
    cf[:, 176:304] = np.maximum(m - c, 0.0)
    cf[:, 304] = 127.0 - m[:, 0]
    cf[:, 305] = m[:, 0]
    cf[:, 306] = m[:, 0] + 1.0
    cf[:, 307] = 128.0 - m[:, 0]
    ident = np.eye(128, dtype=np.float32).astype(ml_dtypes.bfloat16)
    return cf, ident


_CACHE = {}


def run(inputs, S, NSEQ, ncores):
    key = (S, NSEQ)
    if key not in _CACHE:
        _CACHE[key] = build(S, NSEQ)
    nc = _CACHE[key]
    cf, ident = make_consts()
    x = np.ascontiguousarray(np.asarray(inputs["x"], dtype=np.float32))
    pos = np.ascontiguousarray(np.asarray(inputs["positions"], dtype=np.int32))
    wnames = ["norm1_g", "w_in", "ret_decay_logit_fwd", "ret_decay_logit_bwd", "q_a_norm_g", "w_uq", "kv_a_norm_g",
              "w_ukv", "q_norm_g", "k_norm_g", "w_o", "norm2_g", "w_gate", "w_up", "w_down"]
    shared = {}
    for n in wnames:
        a = np.asarray(inputs[n], dtype=np.float32)
        shared[n] = np.ascontiguousarray(a[0])
    for n in ["norm1_g", "ret_decay_logit_fwd", "ret_decay_logit_bwd", "q_a_norm_g", "kv_a_norm_g", "q_norm_g", "k_norm_g", "norm2_g"]:
        shared[n] = shared[n].reshape(1, -1)
    shared["cf"] = cf
    shared["ident"] = ident
    in_maps = []
    for c in range(ncores):
        m = dict(shared)
        m["x"] = x[c * NSEQ:(c + 1) * NSEQ].reshape(NSEQ * S, D)
        m["positions"] = pos[c * NSEQ:(c + 1) * NSEQ].reshape(NSEQ * S)
        in_maps.append(m)
    res = run_bass_kernel_spmd(nc, in_maps, core_ids=list(range(ncores)))
    outs = [np.asarray(r["out"]).reshape(NSEQ, S, D) for r in res.results]
    return np.concatenate(outs, axis=0).astype(np.float32)


def kernel(**inputs):
    return run(inputs, 4096, 2, 8)
```

```python
import numpy as np
import concourse.bass as bass
import concourse.mybir as mybir

F32 = mybir.dt.float32
BF16 = mybir.dt.bfloat16
I32 = mybir.dt.int32
ALU = mybir.AluOpType
AF = mybir.ActivationFunctionType
AX = mybir.AxisListType


class Tok:
    __slots__ = ("w", "r", "name", "excl")

    def __init__(self, name="", excl=False):
        self.w = None
        self.r = {}
        self.name = name
        self.excl = excl


class Op:
    __slots__ = ("eng", "fn", "deps", "sig", "sigval", "sem", "dma", "idx")


class Prog:
    ENGS = ("pe", "act", "dve", "pool", "sp")

    def __init__(self, nc, ndma=8):
        self.nc = nc
        self.ops = {e: [] for e in self.ENGS}
        self.ndma = ndma
        self.dma_hist = {e: [] for e in self.ENGS}
        self.nops = 0

    def add(self, eng, fn, reads=(), writes=(), dma=False):
        op = Op()
        op.eng = eng
        op.fn = fn
        op.dma = dma
        op.sig = dma
        op.sigval = 0
        op.sem = None
        op.idx = self.nops
        self.nops += 1
        deps = {}
        xr = [t for t in reads if t.excl]
        if xr:
            reads = [t for t in reads if not t.excl]
            writes = list(writes) + [t for t in xr if t not in writes]

        def need(d, war):
            if d is None:
                return
            if (not d.dma) and (not dma) and d.eng == eng:
                if eng == "pe" or war:
                    return
            deps[d.idx] = d

        for t in reads:
            need(t.w, False)
        for t in writes:
            need(t.w, False)
            for r in t.r.values():
                need(r, True)
        if dma:
            h = self.dma_hist[eng]
            k = len(h)
            op.sem = (eng, k % self.ndma)
            op.sigval = 16 * (k // self.ndma + 1)
            if k >= self.ndma:
                d = h[k - self.ndma]
                deps[d.idx] = d
            h.append(op)
        op.deps = list(deps.values())
        for d in op.deps:
            d.sig = True
        for t in reads:
            key = (eng, op.idx) if dma else eng
            t.r[key] = op
        for t in writes:
            t.w = op
            t.r = {}
        self.ops[eng].append(op)
        return op

    def barrier(self):
        lasts = []
        for e in self.ENGS:
            comp = [o for o in self.ops[e] if not o.dma and o.fn is not None]
            if comp:
                lasts.append(comp[-1])
            h = self.dma_hist[e]
            lasts.extend(h[-self.ndma:])
        for e in self.ENGS:
            op = Op()
            op.eng = e
            op.fn = None
            op.dma = False
            op.sig = False
            op.sigval = 0
            op.sem = None
            op.idx = self.nops
            self.nops += 1
            op.deps = [d for d in lasts if not (d.eng == e and not d.dma)]
            for d in op.deps:
                d.sig = True
            self.ops[e].append(op)

    def emit(self, stack):
        nc = self.nc
        sems = {}
        for e in self.ENGS:
            sems[e] = stack.enter_context(nc.semaphore("s_" + e))
            for k in range(self.ndma):
                sems[(e, k)] = stack.enter_context(nc.semaphore("d_%s%d" % (e, k)))
        for e in self.ENGS:
            c = 0
            for op in self.ops[e]:
                if op.dma or op.fn is None:
                    continue
                if op.sig:
                    c += 1
                    op.sigval = c
                    op.sem = e
        block = stack.enter_context(nc.Block())
        prog = self

        def run(engname, eng):
            waited = {}
            for op in prog.ops[engname]:
                for d in op.deps:
                    if waited.get(d.sem, 0) >= d.sigval:
                        continue
                    eng.wait_ge(sems[d.sem], d.sigval)
                    waited[d.sem] = d.sigval
                if op.fn is None:
                    continue
                ins = op.fn(eng)
                if op.sig:
                    ins.then_inc(sems[op.sem], 16 if op.dma else 1)
            for op in prog.dma_hist[engname][-prog.ndma:]:
                if waited.get(op.sem, 0) < op.sigval:
                    eng.wait_ge(sems[op.sem], op.sigval)
                    waited[op.sem] = op.sigval

        @block.tensor
        def _(e):
            run("pe", e)

        @block.scalar
        def _(e):
            run("act", e)

        @block.vector
        def _(e):
            run("dve", e)

        @block.gpsimd
        def _(e):
            run("pool", e)

        @block.sync
        def _(e):
            run("sp", e)


import ml_dtypes
from contextlib import ExitStack
from concourse.bass_utils import run_bass_kernel_spmd

D = 1024
DFF = 2816
NF = DFF // 128
INW = 2464
EPS = 1e-6
GN_EPS = 1e-5
TWO_PI = 6.283185307179586
C1 = 6.28125
C2 = TWO_PI - C1
MAGIC = 12582912.0
PI_SAFE = 3.1415925
ARENA = 60416


class _Stop(Exception):
    pass


def build(S, NSEQ, stop=None):
    NT = S // 128
    NQB = S // 512
    nc = bass.Bass("TRN2", target_bir_lowering=False)

    def din(name, shape, dt=F32):
        return nc.dram_tensor(name, shape, dt, kind="ExternalInput").ap()

    x_d = din("x", [NSEQ * S, D])
    pos_d = din("positions", [NSEQ * S], I32)
    n1_d = din("norm1_g", [1, D])
    win_d = din("w_in", [D, INW])
    lf_d = din("ret_decay_logit_fwd", [1, 8])
    lb_d = din("ret_decay_logit_bwd", [1, 8])
    gqa_d = din("q_a_norm_g", [1, 256])
    wuq_d = din("w_uq", [256, 768])
    gkva_d = din("kv_a_norm_g", [1, 128])
    wukv_d = din("w_ukv", [128, 1024])
    gq_d = din("q_norm_g", [1, 96])
    gk_d = din("k_norm_g", [1, 96])
    wo_d = din("w_o", [D, D])
    n2_d = din("norm2_g", [1, D])
    wg_d = din("w_gate", [D, DFF])
    wu_d = din("w_up", [D, DFF])
    wd_d = din("w_down", [DFF, D])
    cf_d = din("cf", [128, 308])
    id_d = din("ident", [128, 128], BF16)
    out_d = nc.dram_tensor("out", [NSEQ * S, D], F32, kind="ExternalOutput").ap()
    win_s = nc.dram_tensor("win_s", [D, INW], BF16).ap()
    wuq_s = nc.dram_tensor("wuq_s", [256, 768], BF16).ap()
    wukv_s = nc.dram_tensor("wukv_s", [128, 1024], BF16).ap()
    wo_s = nc.dram_tensor("wo_s", [D, D], BF16).ap()
    wgu_s = nc.dram_tensor("wgu_s", [NF, 128, 2, 8, 128], BF16).ap()
    wd_s = nc.dram_tensor("wd_s", [DFF, D], BF16).ap()

    st = ExitStack()
    with st:
        def sb(name, shape, dt):
            return st.enter_context(nc.sbuf_tensor(name, shape, dt))

        AR = sb("arena", [128, ARENA], BF16)
        YT = sb("YT", [128, 8 * S], BF16)
        XT = [sb("XT%d" % i, [128, D], F32) for i in range(2)]
        H = sb("H", [128, D], BF16)
        HT = [sb("HT%d" % i, [128, D], BF16) for i in range(2)]
        CF = sb("CF", [128, 308], F32)
        ID = sb("ID", [128, 128], BF16)
        G1 = sb("G1", [128, 8], F32)
        G2 = sb("G2", [128, 8], F32)
        GQA = sb("GQA", [128, 2], F32)
        GKVA = sb("GKVA", [128, 1], F32)
        GQS = sb("GQS", [128, 96], F32)
        GK = sb("GK", [128, 96], F32)
        LGT = sb("LGT", [128, 16], F32)
        LG = sb("LG", [128, 16], F32)
        ZT = sb("ZT", [128, 48], F32)
        ZX = sb("ZX", [128, 48], F32)
        XFB = sb("XFB", [128, 16], F32)
        STT_ = sb("STATE", [128, 512], F32)
        POSI = sb("POSI", [128, NT], I32)
        POSF = sb("POSF", [128, NT], F32)
        SM = sb("SM", [128, 64], F32)
        SSR = sb("SSR", [128, NT], F32)
        QT2 = sb("QT2", [128, 2048], BF16)
        QT2v = QT2[:, :].rearrange("p (h s) -> p h s", h=4)
        PS = [st.enter_context(nc.psum_tensor("ps%d" % i, [128, 1024], F32)) for i in range(4)]

        def bank(i):
            return PS[i // 2][:, (i % 2) * 512:(i % 2 + 1) * 512]

        def bankb(i):
            return bank(i).bitcast(BF16)

        def arb(off, n):
            return AR[:, off:off + n]

        def arf(off, n):
            return AR[:, off:off + 2 * n].bitcast(F32)

        P = Prog(nc)
        toks = {}

        def T(name):
            t = toks.get(name)
            if t is None:
                excl = (name[0] == "b" and name[1:].isdigit()) or name in ("S", "S0", "S1", "bO0", "bO1")
                t = toks[name] = Tok(name, excl)
            return t

        def R(*names):
            return [T(n) for n in names]

        def MM(out, lhsT, rhs, start, stop, r, w, **kw):
            return P.add("pe", lambda e: e.matmul(out, lhsT=lhsT, rhs=rhs, start=start, stop=stop, **kw), r, w)

        def TR(out, in_, r, w):
            return P.add("pe", lambda e: e.transpose(out=out, in_=in_, identity=ID[:, :]), r + [T("ID")], w)

        def TTo(eng, out, in0, in1, op, r, w):
            return P.add(eng, lambda e: e.tensor_tensor(out=out, in0=in0, in1=in1, op=op), r, w)

        def TS(out, in0, s1, s2, op0, op1, r, w):
            if s2 is None:
                return P.add("dve", lambda e: e.tensor_scalar(out=out, in0=in0, scalar1=s1, scalar2=None, op0=op0), r, w)
            return P.add("dve", lambda e: e.tensor_scalar(out=out, in0=in0, scalar1=s1, scalar2=s2, op0=op0, op1=op1), r, w)

        def STT(out, in0, scalar, in1, op0, op1, r, w):
            return P.add("dve", lambda e: e.scalar_tensor_tensor(out=out, in0=in0, scalar=scalar, in1=in1, op0=op0, op1=op1), r, w)

        def ACTV(out, in_, func, r, w, scale=1.0, bias=0.0, accum=None):
            if accum is None:
                return P.add("act", lambda e: e.activation(out=out, in_=in_, func=func, bias=bias, scale=scale), r, w)
            return P.add("act", lambda e: e.activation(out=out, in_=in_, func=func, bias=bias, scale=scale, accum_out=accum), r, w)

        def CP(eng, out, in_, r, w):
            if eng == "act":
                return P.add("act", lambda e: e.copy(out=out, in_=in_), r, w)
            return P.add(eng, lambda e: e.tensor_copy(out=out, in_=in_), r, w)

        def RED(out, in_, r, w):
            return P.add("dve", lambda e: e.reduce_sum(out=out, in_=in_, axis=AX.X), r, w)

        def RECIP(out, in_, r, w):
            return P.add("dve", lambda e: e.reciprocal(out=out, in_=in_), r, w)

        def MEMSET(eng, ap, val, w):
            return P.add(eng, lambda e: e.memset(ap, val), [], w)

        def DMA(q, out, in_, r, w, slow=False):
            if slow:
                return P.add(q, lambda e: e.dma_start(out=out, in_=in_, allow_slow_non_contiguous=True), r, w, dma=True)
            return P.add(q, lambda e: e.dma_start(out=out, in_=in_), r, w, dma=True)

        def rsqrt_chain(dst, src, n_inv, eps, r, w, tmp):
            ACTV(tmp, src, AF.Ln, r, [T("rs_tmp")], scale=n_inv, bias=eps)
            ACTV(dst, tmp, AF.Exp, [T("rs_tmp")], w, scale=-0.5)

        DMA("sp", CF[:, :], cf_d[:, :], [], R("CF"))
        DMA("sp", ID[:, :], id_d[:, :], [], R("ID"))
        DMA("sp", G1[:, :], n1_d.rearrange("o (k p) -> p (o k)", p=128), [], R("G1"), slow=True)
        DMA("sp", G2[:, :], n2_d.rearrange("o (k p) -> p (o k)", p=128), [], R("G2"), slow=True)
        DMA("sp", GQA[:, :], gqa_d.rearrange("o (k p) -> p (o k)", p=128), [], R("GQA"), slow=True)
        DMA("sp", GKVA[:, :], gkva_d.rearrange("o (k p) -> p (o k)", p=128), [], R("GKVA"), slow=True)
        DMA("sp", GQS[:, :], gq_d.to_broadcast([128, 96]), [], R("GQS"), slow=True)
        DMA("sp", GK[:, :], gk_d.to_broadcast([128, 96]), [], R("GK"), slow=True)
        DMA("sp", LGT[:, 0:8], lf_d.to_broadcast([128, 8]), [], R("LGT"), slow=True)
        DMA("sp", LGT[:, 8:16], lb_d.to_broadcast([128, 8]), R("LGT"), R("LGT"), slow=True)
        TS(GQS[:, :], GQS[:, :], 96.0 ** -0.5, None, ALU.mult, None, R("GQS"), R("GQS"))
        ACTV(LG[:, :], LGT[:, :], AF.Exp, R("LGT"), R("LG"), scale=-1.0)
        ACTV(LG[:, :], LG[:, :], AF.Ln, R("LG"), R("LG"), bias=1.0)
        TS(LG[:, :], LG[:, :], -1.0, None, ALU.mult, None, R("LG"), R("LG"))
        IDX = CF[:, 304:308]
        for j, (lo, ic) in enumerate([(0, 0), (8, 1), (0, 2), (8, 3)]):
            TS(ZX[:, j * 8:(j + 1) * 8], LG[:, lo:lo + 8], IDX[:, ic:ic + 1], None, ALU.mult, None, R("LG", "CF"), R("ZX"))
        TS(ZX[:, 32:48], LG[:, 0:16], 128.0, None, ALU.mult, None, R("LG"), R("ZX"))
        ACTV(ZT[:, :], ZX[:, :], AF.Exp, R("ZX"), R("ZT"))
        XFBv = XFB[:, :].rearrange("p (h d) -> p h d", d=2)
        CP("dve", XFBv[:, :, 0], ZT[:, 16:24], R("ZT"), R("XFB"))
        CP("dve", XFBv[:, :, 1], ZT[:, 24:32], R("ZT", "XFB"), R("XFB"))
        ZF, ZB, GCF, GCB = ZT[:, 0:8], ZT[:, 8:16], ZT[:, 32:40], ZT[:, 40:48]

        SF = [arf(i * 5632, 2816) for i in range(2)]
        SBF = [arb(11264 + i * 2816, 2816) for i in range(2)]
        cnt = [0]

        def prep(src, K, N, gain, gtok, dst_fn):
            for kc in range(K // 128):
                s = cnt[0] % 2
                cnt[0] += 1
                DMA("sp", SF[s][:, 0:N], src[kc * 128:(kc + 1) * 128, :], [], R("SF%d" % s))
                if gain is not None:
                    TS(SBF[s][:, 0:N], SF[s][:, 0:N], gain[:, kc:kc + 1], None, ALU.mult, None,
                       R("SF%d" % s, gtok), R("SB%d" % s))
                else:
                    CP("act", SBF[s][:, 0:N], SF[s][:, 0:N], R("SF%d" % s), R("SB%d" % s))
                dst_fn(kc, SBF[s], R("SB%d" % s))

        prep(win_d, D, INW, G1, "G1", lambda kc, t, r: DMA("pool", win_s[kc * 128:(kc + 1) * 128, :], t[:, 0:INW], r, []))
        prep(wuq_d, 256, 768, GQA, "GQA", lambda kc, t, r: DMA("pool", wuq_s[kc * 128:(kc + 1) * 128, :], t[:, 0:768], r, []))
        prep(wukv_d, 128, 1024, GKVA, "GKVA", lambda kc, t, r: DMA("pool", wukv_s[:, :], t[:, 0:1024], r, []))
        prep(wo_d, D, D, None, None, lambda kc, t, r: DMA("pool", wo_s[kc * 128:(kc + 1) * 128, :], t[:, 0:D], r, []))
        for gi, wsrc in enumerate((wg_d, wu_d)):
            prep(wsrc, D, DFF, G2, "G2",
                 lambda kc, t, r, gi=gi: DMA("pool", wgu_s[:, :, gi, kc, :].rearrange("f p n -> p f n"),
                                             t[:, 0:DFF].rearrange("p (f n) -> p f n", n=128), r, []))
        prep(wd_d, DFF, D, None, None, lambda kc, t, r: DMA("pool", wd_s[kc * 128:(kc + 1) * 128, :], t[:, 0:D], r, []))
        P.barrier()

        xc = [0]

        def xnorm(row0, xdst=None, xtok=None, ht_eng="act"):
            s = xc[0] % 2
            xc[0] += 1
            if xdst is None:
                xdst, xtok = XT[s][:, :], "XT%d" % s
                DMA("sp", xdst, x_d[row0:row0 + 128, :], [], R(xtok))
            ACTV(H[:, :], xdst, AF.Square, R(xtok), R("H", "xss"), accum=SM[:, 0:1])
            rsqrt_chain(SM[:, 2:3], SM[:, 0:1], 1.0 / D, EPS, R("xss"), R("xrs"), SM[:, 1:2])
            TS(H[:, :], xdst, SM[:, 2:3], None, ALU.mult, None, R(xtok, "xrs"), R("H"))
            bT = bankb(0)
            for k in range(8):
                TR(bT[:, k * 128:(k + 1) * 128], H[:, k * 128:(k + 1) * 128], R("H"), R("b0"))
            CP(ht_eng, HT[s][:, :], bT[:, :], R("b0"), R("HT%d" % s))
            return HT[s], "HT%d" % s

        def proj(bi, ht, httok, W, wtok, c0, n):
            for k in range(8):
                MM(bank(bi)[:, 0:n], ht[:, k * 128:(k + 1) * 128], W[:, k, c0:c0 + n], k == 0, k == 7,
                   R(httok, wtok), R("b%d" % bi))

        def rotary(src, stok, dst, dtok, cos, sin, tabtok, nh, half, tA, tB, tAtok, tBtok):
            cb = cos.unsqueeze(1).unsqueeze(1).to_broadcast([128, nh, 2, half])
            sbh = sin.unsqueeze(1).to_broadcast([128, nh, half])
            TTo("dve", tA, src, cb, ALU.mult, R(stok, tabtok, tAtok), R(tAtok))
            TTo("pool", tB[:, :, 0, :], src[:, :, 1, :], sbh, ALU.mult, R(stok, tabtok, tBtok), R(tBtok))
            TTo("pool", tB[:, :, 1, :], src[:, :, 0, :], sbh, ALU.mult, R(stok, tabtok, tBtok), R(tBtok))
            TTo("dve", dst[:, :, 0, :], tA[:, :, 0, :], tB[:, :, 0, :], ALU.subtract, R(tAtok, tBtok, dtok), R(dtok))
            TTo("dve", dst[:, :, 1, :], tA[:, :, 1, :], tB[:, :, 1, :], ALU.add, R(tAtok, tBtok, dtok), R(dtok))

        def trig_tables(n, invf, cosd, sind, tmp, tabtok):
            ang, u, r_ = tmp
            TTo("dve", ang.rearrange("p (t f) -> p t f", f=n), POSF[:, :].unsqueeze(2).to_broadcast([128, NT, n]),
                invf.unsqueeze(1).to_broadcast([128, NT, n]), ALU.mult, R("POSF", "CF"), R("ang"))
            for shift, dst in ((0.0, sind), (np.pi / 2, cosd)):
                if shift:
                    TS(ang, ang, shift, None, ALU.add, None, R("ang"), R("ang"))
                TS(u, ang, 1.0 / TWO_PI, MAGIC, ALU.mult, ALU.add, R("ang"), R("trU"))
                TS(u, u, -MAGIC, None, ALU.add, None, R("trU"), R("trU"))
                STT(r_, u, -C1, ang, ALU.mult, ALU.add, R("trU", "ang"), R("trR"))
                STT(r_, u, -C2, r_, ALU.mult, ALU.add, R("trU", "trR"), R("trR"))
                TS(r_, r_, -PI_SAFE, PI_SAFE, ALU.max, ALU.min, R("trR"), R("trR"))
                ACTV(dst, r_, AF.Sin, R("trR"), R(tabtok))

        def checkpoint(k):
            if stop is not None and k == stop:
                raise _Stop()

        try:
            for sq in range(NSEQ):
                base = sq * S
                checkpoint(0)
                DMA("sp", POSI[:, :], pos_d[base:base + S].rearrange("(t p) -> p t", p=128), [], R("POSI"), slow=True)
                CP("dve", POSF[:, :], POSI[:, :], R("POSI"), R("POSF"))

                checkpoint(1)
                WR = arb(0, 12288).rearrange("p (k n) -> p k n", k=8)
                WR1 = arb(0, 8192).rearrange("p (k n) -> p k n", k=8)
                KTOK = arb(12288, NT * 512).rearrange("p (t n) -> p t n", n=512)
                RBC = arb(28672, (NT // 2 + 1) * 512).rearrange("p (t n) -> p t n", n=512)
                RJ = arb(37888, 1024).rearrange("p (t n) -> p t n", n=512)
                DTm = arf(38912, 1024)
                COSR = arf(40960, NT * 32)
                SINR = arf(40960 + NT * 64, NT * 32)
                o = 40960 + NT * 128
                F = [arf(o + i * 1024, 512) for i in range(4)]
                o += 4096
                B = [arb(o + i * 1024, 1024) for i in range(6)]
                o += 6144
                assert o <= ARENA
                tmp3 = [arf(12288 + i * (NT * 64), NT * 32) for i in range(3)]
                trig_tables(32, CF[:, 0:32], COSR, SINR, tmp3, "TABR")
                for h in range(8):
                    TS(F[0][:, 0:128], CF[:, 48:176], LG[:, h:h + 1], None, ALU.mult, None, R("CF", "LG"), R("F0"))
                    STT(F[1][:, 0:128], CF[:, 176:304], LG[:, 8 + h:9 + h], F[0][:, 0:128], ALU.mult, ALU.add,
                        R("CF", "LG", "F0"), R("F1"))
                    ACTV(DTm[:, h * 128:(h + 1) * 128], F[1][:, 0:128], AF.Exp, R("F1"), R("DT"))
                MEMSET("dve", STT_[:, :], 0.0, R("ST"))
                P.barrier()

                def v4(ap, nh, half):
                    return ap.rearrange("p (h d f) -> p h d f", h=nh, d=2)

                def v3(ap, a):
                    return ap.rearrange("p (a b) -> p a b", a=a)

                checkpoint(2)
                DMA("sp", WR1[:, :, :], win_s[:, 512:1536].rearrange("(k p) n -> p k n", p=128), [], R("WR"))
                for t in range(NT - 1, -1, -1):
                    ht, htt = xnorm(base + t * 128)
                    proj(1, ht, htt, WR1, "WR", 0, 512)
                    proj(2, ht, htt, WR1, "WR", 512, 512)
                    ACTV(F[0][:, :], bank(1), AF.Copy, R("b1"), R("F0"), scale=0.125)
                    rotary(v4(F[0][:, :], 8, 32), "F0", v4(KTOK[:, t, :], 8, 32), "KTOK%d" % t,
                           COSR[:, t * 32:(t + 1) * 32], SINR[:, t * 32:(t + 1) * 32], "TABR", 8, 32,
                           v4(F[1][:, :], 8, 32), v4(F[2][:, :], 8, 32), "F1", "F2")
                    TTo("dve", v3(B[0][:, 0:512], 8), v3(bank(2), 8), ZB.unsqueeze(2).to_broadcast([128, 8, 64]), ALU.mult,
                        R("b2", "ZT"), R("B0"))
                    for h in range(8):
                        MM(bank(3)[64:128, h * 64:(h + 1) * 64], KTOK[:, t, h * 64:(h + 1) * 64], B[0][:, h * 64:(h + 1) * 64],
                           True, True, R("KTOK%d" % t, "B0"), R("b3"), tile_position=(0, 64))
                    pr = (t % 2) * 64
                    CP("act", RBC[pr:pr + 64, t // 2, :], STT_[64:128, :], R("ST"), R("RBC%d" % t))
                    TTo("dve", v3(F[3][64:128, :], 8), v3(STT_[64:128, :], 8), GCB[64:128, :].unsqueeze(2).to_broadcast([64, 8, 64]),
                        ALU.mult, R("ST", "ZT"), R("F3"))
                    TTo("dve", STT_[64:128, :], F[3][64:128, :], bank(3)[64:128, :], ALU.add, R("F3", "b3"), R("ST"))
                P.barrier()

                checkpoint(3)
                DMA("sp", WR[:, :, 0:512], win_s[:, 0:512].rearrange("(k p) n -> p k n", p=128), [], R("WR"))
                DMA("sp", WR[:, :, 512:1024], win_s[:, 1536:2048].rearrange("(k p) n -> p k n", p=128), R("WR"), R("WR"))
                DMA("sp", WR[:, :, 1024:1536], win_s[:, 1024:1536].rearrange("(k p) n -> p k n", p=128), R("WR"), R("WR"))
                MEMSET("dve", RJ[0:64, 0, :], 0.0, R("RJ0"))
                CP("dve", RJ[64:128, 0, :], RBC[0:64, 0, :], R("RBC0", "RJ0"), R("RJ0"))
                for t in range(NT):
                    ht, htt = xnorm(base + t * 128)
                    proj(1, ht, htt, WR, "WR", 0, 512)
                    proj(2, ht, htt, WR, "WR", 512, 512)
                    proj(3, ht, htt, WR, "WR", 1024, 512)
                    checkpoint(30)
                    ACTV(F[0][:, :], bank(1), AF.Copy, R("b1"), R("F0"))
                    rotary(v4(F[0][:, :], 8, 32), "F0", v4(F[3][:, :], 8, 32), "F3",
                           COSR[:, t * 32:(t + 1) * 32], SINR[:, t * 32:(t + 1) * 32], "TABR", 8, 32,
                           v4(F[1][:, :], 8, 32), v4(F[2][:, :], 8, 32), "F1", "F2")
                    CP("dve", B[0][:, 0:512], F[3][:, :], R("F3"), R("B0"))
                    TTo("dve", B[1][:, :].rearrange("p (h d e) -> p h d e", h=8, d=2),
                        v3(F[3][:, :], 8).unsqueeze(2).to_broadcast([128, 8, 2, 64]),
                        XFB[:, :].rearrange("p (h d) -> p h d", d=2).unsqueeze(3).to_broadcast([128, 8, 2, 64]),
                        ALU.mult, R("F3", "XFB"), R("B1"))
                    checkpoint(31)
                    CP("act", B[2][:, 0:512], bank(3), R("b3"), R("B2a"))
                    TTo("dve", v3(B[2][:, 512:1024], 8), v3(bank(3), 8), ZF.unsqueeze(2).to_broadcast([128, 8, 64]), ALU.mult,
                        R("b3", "ZT"), R("B2b"))
                    checkpoint(32)
                    b6 = bankb(6)
                    for j in range(4):
                        TR(b6[:, j * 128:(j + 1) * 128], B[0][:, j * 128:(j + 1) * 128], R("B0"), R("b6"))
                    for j in range(4):
                        TR(b6[:, 512 + j * 128:512 + (j + 1) * 128], KTOK[:, t, j * 128:(j + 1) * 128], R("KTOK%d" % t), R("b6"))
                    CP("act", B[3][:, :], b6[:, :], R("b6"), R("B3"))
                    for h in range(8):
                        TR(b6[:, h * 128:(h + 1) * 128], B[1][:, h * 128:(h + 1) * 128], R("B1"), R("b6"))
                    CP("dve", B[4][:, :], b6[:, :], R("b6"), R("B4"))
                    checkpoint(33)
                    for h in range(8):
                        j, hp = h // 2, (h % 2) * 64
                        MM(PS[2][:, (h % 2) * 512 + j * 128:(h % 2) * 512 + (j + 1) * 128],
                           B[3][hp:hp + 64, 512 + j * 128:512 + (j + 1) * 128],
                           B[3][hp:hp + 64, j * 128:(j + 1) * 128], True, True, R("B3"), R("S"))
                    for c2 in range(2):
                        TTo("dve", B[5][:, :].rearrange("p (j q c) -> p j q c", j=4, q=2)[:, :, c2, :],
                            PS[2][:, c2 * 512:(c2 + 1) * 512].rearrange("p (j c) -> p j c", j=4),
                            DTm[:, :].rearrange("p (j q c) -> p j q c", j=4, q=2)[:, :, c2, :], ALU.mult, R("S", "DT", "B5"), R("B5"))
                    checkpoint(34)
                    for h in range(8):
                        MM(bank(7)[0:64, h * 64:(h + 1) * 64], KTOK[:, t, h * 64:(h + 1) * 64], B[2][:, 512 + h * 64:512 + (h + 1) * 64],
                           True, True, R("KTOK%d" % t, "B2b"), R("b7"))
                    rj = t % 2
                    for h in range(8):
                        MM(bank(3)[:, h * 64:(h + 1) * 64], B[5][:, h * 128:(h + 1) * 128], B[2][:, h * 64:(h + 1) * 64],
                           True, False, R("B5", "B2a", "B2b"), R("b3"))
                        MM(bank(3)[:, h * 64:(h + 1) * 64], B[4][:, h * 128:(h + 1) * 128], RJ[:, rj, h * 64:(h + 1) * 64],
                           False, True, R("B4", "RJ%d" % rj), R("b3"))
                    checkpoint(35)
                    if t + 1 < NT:
                        nrj = (t + 1) % 2
                        TTo("dve", v3(F[0][0:64, :], 8), v3(STT_[0:64, :], 8), GCF[0:64, :].unsqueeze(2).to_broadcast([64, 8, 64]),
                            ALU.mult, R("ST", "ZT", "F0"), R("F0"))
                        TTo("dve", STT_[0:64, :], F[0][0:64, :], bank(7)[0:64, :], ALU.add, R("F0", "b7"), R("ST"))
                        CP("act", RJ[0:64, nrj, :], STT_[0:64, :], R("ST"), R("RJ%d" % nrj))
                        pr = ((t + 1) % 2) * 64
                        CP("dve", RJ[64:128, nrj, :], RBC[pr:pr + 64, (t + 1) // 2, :], R("RBC%d" % (t + 1), "RJ%d" % nrj), R("RJ%d" % nrj))
                    checkpoint(36)
                    ACTV(F[0][:, :], bank(3), AF.Copy, R("b3"), R("F0"))
                    RED(SM[:, 8:16], v3(F[0][:, :], 8), R("F0"), R("mu"))
                    ACTV(F[1][:, :], F[0][:, :], AF.Square, R("F0"), R("F1"))
                    RED(SM[:, 16:24], v3(F[1][:, :], 8), R("F1"), R("m2"))
                    TS(SM[:, 8:16], SM[:, 8:16], 1.0 / 64, None, ALU.mult, None, R("mu"), R("mu"))
                    TTo("dve", SM[:, 24:32], SM[:, 8:16], SM[:, 8:16], ALU.mult, R("mu"), R("msq"))
                    STT(SM[:, 16:24], SM[:, 16:24], 1.0 / 64, SM[:, 24:32], ALU.mult, ALU.subtract, R("m2", "msq"), R("m2"))
                    rsqrt_chain(SM[:, 32:40], SM[:, 16:24], 1.0, GN_EPS, R("m2"), R("grs"), SM[:, 40:48])
                    TTo("dve", v3(F[2][:, :], 8), v3(F[0][:, :], 8), SM[:, 8:16].unsqueeze(2).to_broadcast([128, 8, 64]), ALU.subtract,
                        R("F0", "mu"), R("F2"))
                    TTo("dve", v3(F[2][:, :], 8), v3(F[2][:, :], 8), SM[:, 32:40].unsqueeze(2).to_broadcast([128, 8, 64]), ALU.mult,
                        R("F2", "grs"), R("F2"))
                    ACTV(F[1][:, :], bank(2), AF.Exp, R("b2", "F1"), R("F1"), scale=-1.0)
                    ACTV(F[1][:, :], F[1][:, :], AF.Ln, R("F1"), R("F1"), bias=1.0)
                    ACTV(F[1][:, :], F[1][:, :], AF.Exp, R("F1"), R("F1"), scale=-1.0)
                    TTo("dve", F[3][:, :], bank(2), F[1][:, :], ALU.mult, R("b2", "F1", "F3"), R("F3"))
                    TTo("dve", B[0][:, 0:512], F[2][:, :], F[3][:, :], ALU.mult, R("F2", "F3", "B0"), R("B0"))
                    checkpoint(37)
                    bT = bankb(0)
                    for j in range(4):
                        TR(bT[:, j * 128:(j + 1) * 128], B[0][:, j * 128:(j + 1) * 128], R("B0"), R("b0"))
                    CP("act", YT[:, :].rearrange("p (k s) -> p k s", k=8)[:, 0:4, t * 128:(t + 1) * 128],
                       bT[:, 0:512].rearrange("p (k s) -> p k s", k=4), R("b0"), R("YTr%d" % t))
                P.barrier()

                checkpoint(4)
                CQT = arb(0, 2 * S).rearrange("p (k s) -> p k s", k=2)
                CKVT = arb(8192, S)
                KRB = arf(12288, NT * 32)
                KT = arb(14336, 4 * S).rearrange("p (h s) -> p h s", h=4)
                VA = arb(30720, NT * 512).rearrange("p (t h c) -> p t h c", h=4, c=128)
                WM = arb(47104, 3328).rearrange("p (k n) -> p k n", k=8)
                QT = arb(47104, 2048).rearrange("p (h s) -> p h s", h=4)
                RL = arf(47104 + 2048, 512)
                WUQ = arb(50432, 1536).rearrange("p (k n) -> p k n", k=2)
                WUKV = arb(51968, 1024)
                PTA = [arb(52992 + i * 1024, 1024) for i in range(3)]
                COSM = arf(56064, NT * 16)
                SINM = arf(57088, NT * 16)
                MF = [arf(58112 + i * 1024, 512) for i in range(2)]
                tmpm = [arf(52992 + i * 1024, NT * 16) for i in range(3)]
                DMA("sp", WM[:, :, :], win_s[:, 2048:2464].rearrange("(k p) n -> p k n", p=128), [], R("WM"))
                DMA("sp", WUQ[:, :, :], wuq_s.rearrange("(k p) n -> p k n", p=128), [], R("WUQ"))
                DMA("sp", WUKV[:, :], wukv_s[:, :], [], R("WUKV"))
                checkpoint(39)
                trig_tables(16, CF[:, 32:48], COSM, SINM, tmpm, "TABM")
                P.barrier()
                checkpoint(40)
                for t in range(NT):
                    ht, htt = xnorm(base + t * 128)
                    proj(1, ht, htt, WM, "WM", 0, 416)
                    checkpoint(41)
                    b1 = bank(1)
                    ACTV(MF[0][:, 0:256], b1[:, 0:256], AF.Square, R("b1"), R("MF0", "ssq"), accum=SM[:, 8:9])
                    rsqrt_chain(SM[:, 10:11], SM[:, 8:9], 1.0 / 256, EPS, R("ssq"), R("rq"), SM[:, 9:10])
                    TS(B_cq := MF[1][:, 0:128].bitcast(BF16), b1[:, 0:256], SM[:, 10:11], None, ALU.mult, None, R("b1", "rq"), R("cq"))
                    ACTV(MF[0][:, 256:384], b1[:, 256:384], AF.Square, R("b1"), R("MF0b", "sskv"), accum=SM[:, 11:12])
                    rsqrt_chain(SM[:, 13:14], SM[:, 11:12], 1.0 / 128, EPS, R("sskv"), R("rkv"), SM[:, 12:13])
                    TS(B_ckv := MF[1][:, 128:192].bitcast(BF16), b1[:, 256:384], SM[:, 13:14], None, ALU.mult, None, R("b1", "rkv"), R("ckv"))
                    checkpoint(42)
                    b2 = bankb(2)
                    TR(b2[:, 0:128], B_cq[:, 0:128], R("cq"), R("b2"))
                    TR(b2[:, 128:256], B_cq[:, 128:256], R("cq"), R("b2"))
                    TR(b2[:, 256:384], B_ckv[:, 0:128], R("ckv"), R("b2"))
                    checkpoint(45)
                    CP("act", CQT[:, :, t * 128:(t + 1) * 128], b2[:, 0:256].rearrange("p (k s) -> p k s", k=2), R("b2"), R("CQT%d" % t))
                    checkpoint(46)
                    CP("dve", CKVT[:, t * 128:(t + 1) * 128], b2[:, 256:384], R("b2"), R("CKVT%d" % t))
                    checkpoint(43)
                    ACTV(MF[0][:, 384:416], b1[:, 384:416], AF.Square, R("b1"), R("MF0c", "SSR"), accum=SSR[:, t:t + 1])
                    TTo("dve", MF[0][:, 448:480], b1[:, 384:416], GK[:, 64:96], ALU.mult, R("b1", "GK"), R("krg"))
                    checkpoint(44)
                    rotary(MF[0][:, 448:480].rearrange("p (h d f) -> p h d f", h=1, d=2), "krg",
                           KRB[:, t * 32:(t + 1) * 32].rearrange("p (h d f) -> p h d f", h=1, d=2), "KRB%d" % t,
                           COSM[:, t * 16:(t + 1) * 16], SINM[:, t * 16:(t + 1) * 16], "TABM", 1, 16,
                           MF[1][:, 256:288].rearrange("p (h d f) -> p h d f", h=1, d=2),
                           MF[1][:, 288:320].rearrange("p (h d f) -> p h d f", h=1, d=2), "mrA", "mrB")
                P.barrier()
                checkpoint(5)
                for hh in range(2):
                    MEMSET("pool", VA[:, :, :, 64:128], 1.0, R("VAones"))
                    for t in range(NT):
                        MM(bank(3), CKVT[:, t * 128:(t + 1) * 128], WUKV[:, hh * 512:(hh + 1) * 512], True, True,
                           R("CKVT%d" % t, "WUKV"), R("b3"))
                        kv = bank(3).rearrange("p (h c) -> p h c", h=4)
                        ACTV(MF[0][:, 0:256].rearrange("p (h c) -> p h c", h=4), kv[:, :, 0:64], AF.Square, R("b3"), R("MF0"))
                        RED(SM[:, 16:20], MF[0][:, 0:256].rearrange("p (h c) -> p h c", h=4), R("MF0"), R("ssh"))
                        TS(SM[:, 16:20], SM[:, 16:20], SSR[:, t:t + 1], None, ALU.add, None, R("ssh", "SSR"), R("ssh"))
                        rsqrt_chain(SM[:, 24:28], SM[:, 16:20], 1.0 / 96, EPS, R("ssh"), R("rk"), SM[:, 20:24])
                        TTo("dve", MF[1][:, 0:256].rearrange("p (h c) -> p h c", h=4), kv[:, :, 0:64],
                            SM[:, 24:28].unsqueeze(2).to_broadcast([128, 4, 64]), ALU.mult, R("b3", "rk"), R("MF1"))
                        KTM = MF[0][:, 256:448].bitcast(BF16).rearrange("p (h c) -> p h c", h=4)
                        TTo("dve", KTM[:, :, 0:64], MF[1][:, 0:256].rearrange("p (h c) -> p h c", h=4),
                            GK[:, 0:64].unsqueeze(1).to_broadcast([128, 4, 64]), ALU.mult, R("MF1", "GK"), R("KTM"))
                        TTo("dve", KTM[:, :, 64:96], KRB[:, t * 32:(t + 1) * 32].unsqueeze(1).to_broadcast([128, 4, 32]),
                            SM[:, 24:28].unsqueeze(2).to_broadcast([128, 4, 32]), ALU.mult, R("KRB%d" % t, "rk", "KTM"), R("KTM"))
                        b2 = bankb(2)
                        for hl in range(4):
                            TR(b2[0:96, hl * 128:(hl + 1) * 128], KTM[:, hl, :], R("KTM"), R("b2"))
                        CP("act", KT[0:96, :, t * 128:(t + 1) * 128], b2[0:96, 0:512].rearrange("p (h s) -> p h s", h=4),
                           R("b2"), R("KT%d" % t))
                        CP("dve", VA[:, t, :, 0:64], kv[:, :, 64:128], R("b3", "VAones"), R("VA%d" % t))
                    P.barrier()
                    checkpoint(6)
                    NK = NT // 2
                    QTs = [QT, QT2v]

                    def qprepA(qb, s):
                        tq = qb * 4 + s
                        b6 = bank(6)
                        for k in range(2):
                            MM(b6[:, 0:384], CQT[:, k, tq * 128:(tq + 1) * 128], WUQ[:, k, hh * 384:(hh + 1) * 384], k == 0, k == 1,
                               R("CQT%d" % tq, "WUQ"), R("b6"))
                        q3 = b6[:, 0:384].rearrange("p (h c) -> p h c", h=4)
                        m0 = MF[0][:, 0:384].rearrange("p (h c) -> p h c", h=4)
                        m1 = MF[1][:, 0:384].rearrange("p (h c) -> p h c", h=4)
                        ACTV(m0, q3, AF.Square, R("b6"), R("MF0"))
                        RED(SM[:, 16:20], m0, R("MF0"), R("ssh"))
                        rsqrt_chain(SM[:, 24:28], SM[:, 16:20], 1.0 / 96, EPS, R("ssh"), R("rk"), SM[:, 20:24])
                        TTo("dve", m1, q3, SM[:, 24:28].unsqueeze(2).to_broadcast([128, 4, 96]), ALU.mult, R("b6", "rk"), R("MF1"))
                        TTo("dve", m1, m1, GQS[:, :].unsqueeze(1).to_broadcast([128, 4, 96]), ALU.mult, R("MF1", "GQS"), R("MF1"))
                        qtk = H[:, 0:384].rearrange("p (h c) -> p h c", h=4)
                        CP("dve", qtk[:, :, 0:64], m1[:, :, 0:64], R("MF1", "H"), R("H"))
                        src = m1[:, :, 64:96].rearrange("p h (d f) -> p h d f", d=2)
                        dst = qtk[:, :, 64:96].rearrange("p h (d f) -> p h d f", d=2)
                        tA = m0[:, :, 0:32].rearrange("p h (d f) -> p h d f", d=2)
                        tB = m0[:, :, 32:64].rearrange("p h (d f) -> p h d f", d=2)
                        rotary(src, "MF1", dst, "H", COSM[:, tq * 16:(tq + 1) * 16], SINM[:, tq * 16:(tq + 1) * 16], "TABM", 4, 16,
                               tA, tB, "MF0", "MF0")

                    def qprepC(qb, s):
                        qtk = H[:, 0:384].rearrange("p (h c) -> p h c", h=4)
                        b7 = bankb(7)
                        for hl in range(4):
                            TR(b7[0:96, hl * 128:(hl + 1) * 128], qtk[:, hl, :], R("H"), R("b7"))
                        CP("dve", QTs[qb % 2][0:96, :, s * 128:(s + 1) * 128], b7[0:96, 0:512].rearrange("p (h s) -> p h s", h=4),
                           R("b7"), R("QT%d" % (qb % 2)))

                    units = [(qb, hl, kp) for qb in range(NQB) for hl in range(4) for kp in range(NK)]

                    def emitQK(i):
                        qb, hl, kp = units[i]
                        ssl = i % 2
                        for j in range(2):
                            kt = 2 * kp + j
                            MM(PS[ssl][:, j * 512:(j + 1) * 512], KT[0:96, hl, kt * 128:(kt + 1) * 128], QTs[qb % 2][0:96, hl, :], True, True,
                               R("KT%d" % kt, "QT%d" % (qb % 2)), R("S%d" % ssl))

                    for s in range(4):
                        qprepA(0, s)
                        qprepC(0, s)
                    emitQK(0)
                    if len(units) > 1:
                        emitQK(1)
                    UB = 4 * NK
                    sched = {}
                    if UB >= 40:
                        for s in range(4):
                            sched[2 + 9 * s] = ("A", s)
                            sched[2 + 9 * s + 5] = ("C", s)
                    for i, (qb, hl, kp) in enumerate(units):
                        h = hh * 4 + hl
                        osl = (qb * 4 + hl) % 2
                        bO = bank(4 + osl)
                        ssl, psl = i % 2, i % 3
                        ACTV(PTA[psl][:, :], PS[ssl][:, :], AF.Exp, R("S%d" % ssl), R("PTA%d" % psl))
                        for j in range(2):
                            kt = 2 * kp + j
                            MM(bO, VA[:, kt, hl, :], PTA[psl][:, j * 512:(j + 1) * 512], kp == 0 and j == 0,
                               kp == NK - 1 and j == 1, R("VA%d" % kt, "PTA%d" % psl), R("bO%d" % osl))
                        if i + 2 < len(units) and not (UB < 40 and units[i + 2][0] != qb):
                            emitQK(i + 2)
                        ju = i % UB
                        if qb + 1 < NQB:
                            if UB >= 40:
                                ev = sched.get(ju)
                                if ev is not None:
                                    (qprepA if ev[0] == "A" else qprepC)(qb + 1, ev[1])
                        if kp == NK - 1:
                            RECIP(RL[64:128, :], bO[64:128, :], R("bO%d" % osl), R("RL"))
                            pr = (h % 2) * 64
                            TTo("dve", YT[:, :].rearrange("p (k s) -> p k s", k=8)[pr:pr + 64, 4 + h // 2, qb * 512:(qb + 1) * 512],
                                bO[0:64, :], RL[64:128, :], ALU.mult, R("bO%d" % osl, "RL"), R("YTm%d_%d" % (h, qb)))
                        if UB < 40 and ju == UB - 1 and qb + 1 < NQB:
                            for s in range(4):
                                qprepA(qb + 1, s)
                                qprepC(qb + 1, s)
                            emitQK(i + 1)
                            if i + 2 < len(units):
                                emitQK(i + 2)
                    P.barrier()

                checkpoint(7)
                WO = arb(0, 8192).rearrange("p (k n) -> p k n", k=8)
                X1 = arf(8192, 4096).rearrange("p (s n) -> p s n", s=4)
                H2T = arb(16384, 4096).rearrange("p (k s) -> p k s", k=8)
                AT = arb(20480, NF * 512).rearrange("p (f s) -> p f s", f=NF)
                NWG = 4
                WGU = [arb(31744 + i * 2048, 2048).rearrange("p (g k n) -> p g k n", g=2, k=8) for i in range(NWG)]
                WDQ = [arb(39936 + i * 5632, 5632).rearrange("p (f n) -> p f n", f=NF) for i in range(3)]
                SG = [arf(56832 + i * 1024, 512) for i in range(2)]
                DMA("sp", WO[:, :, :], wo_s.rearrange("(k p) n -> p k n", p=128), [], R("WO"))
                YT3 = YT[:, :].rearrange("p (k s) -> p k s", k=8)
                wq = [0]
                for qb in range(NQB):
                    for s in range(4):
                        tq = qb * 4 + s
                        row0 = base + tq * 128
                        xs = "X1_%d" % s
                        DMA("pool", X1[:, s, :], x_d[row0:row0 + 128, :], [], R(xs))
                        for c2 in range(2):
                            for k in range(8):
                                MM(bank(c2), YT3[:, k, tq * 128:(tq + 1) * 128], WO[:, k, c2 * 512:(c2 + 1) * 512], k == 0, k == 7,
                                   R("WO"), R("b%d" % c2))
                            TTo("dve", X1[:, s, c2 * 512:(c2 + 1) * 512], X1[:, s, c2 * 512:(c2 + 1) * 512], bank(c2), ALU.add,
                                R(xs, "b%d" % c2), R(xs))
                        ACTV(H[:, :], X1[:, s, :], AF.Square, R(xs), R("H", "xss"), accum=SM[:, 0:1])
                        rsqrt_chain(SM[:, 2:3], SM[:, 0:1], 1.0 / D, EPS, R("xss"), R("xrs"), SM[:, 1:2])
                        TS(H[:, :], X1[:, s, :], SM[:, 2:3], None, ALU.mult, None, R(xs, "xrs"), R("H"))
                        bT = bankb(2)
                        for k in range(8):
                            TR(bT[:, k * 128:(k + 1) * 128], H[:, k * 128:(k + 1) * 128], R("H"), R("b2"))
                        CP("act", H2T[:, :, s * 128:(s + 1) * 128], bT[:, :].rearrange("p (k s) -> p k s", k=8), R("b2"), R("H2T"))
                    for f in range(NF):
                        sl = f % 2
                        wsl_ = f % NWG
                        DMA("sp", WGU[wsl_][:, :, :, :], wgu_s[f], [], R("WGU%d" % wsl_))
                        for g in range(2):
                            bi = 3 + 2 * g + sl
                            for k in range(8):
                                MM(bank(bi), WGU[wsl_][:, g, k, :], H2T[:, k, :], k == 0, k == 7, R("WGU%d" % wsl_, "H2T"), R("b%d" % bi))
                        bG, bU = bank(3 + sl), bank(5 + sl)
                        ACTV(SG[0][:, :], bG, AF.Exp, R("b%d" % (3 + sl)), R("SG0"), scale=-1.0)
                        ACTV(SG[0][:, :], SG[0][:, :], AF.Ln, R("SG0"), R("SG0"), bias=1.0)
                        ACTV(SG[0][:, :], SG[0][:, :], AF.Exp, R("SG0"), R("SG0"), scale=-1.0)
                        TTo("dve", SG[1][:, :], bG, SG[0][:, :], ALU.mult, R("b%d" % (3 + sl), "SG0"), R("SG1"))
                        TTo("dve", AT[:, f, :], SG[1][:, :], bU, ALU.mult, R("SG1", "b%d" % (5 + sl)), R("AT"))
                    for q4 in range(4):
                        wsl = wq[0] % 3
                        wq[0] += 1
                        DMA("sp", WDQ[wsl][:, :, :], wd_s[:, q4 * 256:(q4 + 1) * 256].rearrange("(f p) n -> p f n", p=128), [],
                            R("WDQ%d" % wsl))
                        for s in range(4):
                            bi = s % 2
                            for f in range(NF):
                                MM(bank(bi)[:, 0:256], AT[:, f, s * 128:(s + 1) * 128], WDQ[wsl][:, f, :], f == 0, f == NF - 1,
                                   R("AT", "WDQ%d" % wsl), R("b%d" % bi))
                            TTo("dve", X1[:, s, q4 * 256:(q4 + 1) * 256], X1[:, s, q4 * 256:(q4 + 1) * 256], bank(bi)[:, 0:256], ALU.add,
                                R("X1_%d" % s, "b%d" % bi), R("X1_%d" % s))
                    for s in range(4):
                        row0 = base + (qb * 4 + s) * 128
                        DMA("pool", out_d[row0:row0 + 128, :], X1[:, s, :], R("X1_%d" % s), [])
                P.barrier()
        except _Stop:
            pass
        P.emit(st)
    return nc


def make_consts():
    cf = np.zeros((128, 308), np.float32)
    cf[:, 0:32] = (10000.0 ** (-np.arange(32, dtype=np.float32) / 32.0))[None, :]
    cf[:, 32:48] = (10000.0 ** (-np.arange(16, dtype=np.float32) / 16.0))[None, :]
    m = np.arange(128, dtype=np.float32)[:, None]
    c = np.arange(128, dtype=np.float32)[None, :]
    cf[:, 48:176] = np.maximum(c - m, 0.0)
    cf[:, 176:304] = np.maximum(m - c, 0.0)
    cf[:, 304] = 127.0 - m[:, 0]
    cf[:, 305] = m[:, 0]
    cf[:, 306] = m[:, 0] + 1.0
    cf[:, 307] = 128.0 - m[:, 0]
    ident = np.eye(128, dtype=np.float32).astype(ml_dtypes.bfloat16)
    return cf, ident


_CACHE = {}


def run(inputs, S, NSEQ, ncores):
    key = (S, NSEQ)
    if key not in _CACHE:
        _CACHE[key] = build(S, NSEQ)
    nc = _CACHE[key]
    cf, ident = make_consts()
    x = np.ascontiguousarray(np.asarray(inputs["x"], dtype=np.float32))
    pos = np.ascontiguousarray(np.asarray(inputs["positions"], dtype=np.int32))
    wnames = ["norm1_g", "w_in", "ret_decay_logit_fwd", "ret_decay_logit_bwd", "q_a_norm_g", "w_uq", "kv_a_norm_g",
              "w_ukv", "q_norm_g", "k_norm_g", "w_o", "norm2_g", "w_gate", "w_up", "w_down"]
    shared = {}
    for n in wnames:
        a = np.asarray(inputs[n], dtype=np.float32)
        shared[n] = np.ascontiguousarray(a[0])
    for n in ["norm1_g", "ret_decay_logit_fwd", "ret_decay_logit_bwd", "q_a_norm_g", "kv_a_norm_g", "q_norm_g", "k_norm_g", "norm2_g"]:
        shared[n] = shared[n].reshape(1, -1)
    shared["cf"] = cf
    shared["ident"] = ident
    in_maps = []
    for c in range(ncores):
        m = dict(shared)
        m["x"] = x[c * NSEQ:(c + 1) * NSEQ].reshape(NSEQ * S, D)
        m["positions"] = pos[c * NSEQ:(c + 1) * NSEQ].reshape(NSEQ * S)
        in_maps.append(m)
    res = run_bass_kernel_spmd(nc, in_maps, core_ids=list(range(ncores)))
    outs = [np.asarray(r["out"]).reshape(NSEQ, S, D) for r in res.results]
    return np.concatenate(outs, axis=0).astype(np.float32)


def kernel(**inputs):
    return run(inputs, 4096, 2, 8)
```

```python
import numpy as np
import concourse.bass as bass
import concourse.mybir as mybir

F32 = mybir.dt.float32
BF16 = mybir.dt.bfloat16
I32 = mybir.dt.int32
ALU = mybir.AluOpType
AF = mybir.ActivationFunctionType
AX = mybir.AxisListType


class Tok:
    __slots__ = ("w", "r", "name", "excl")

    def __init__(self, name="", excl=False):
        self.w = None
        self.r = {}
        self.name = name
        self.excl = excl


class Op:
    __slots__ = ("eng", "fn", "deps", "sig", "sigval", "sem", "dma", "idx")


class Prog:
    ENGS = ("pe", "act", "dve", "pool", "sp")

    def __init__(self, nc, ndma=8):
        self.nc = nc
        self.ops = {e: [] for e in self.ENGS}
        self.ndma = ndma
        self.dma_hist = {e: [] for e in self.ENGS}
        self.nops = 0
        self.cap = None

    def capture(self):
        self.cap = []

    def end_capture(self):
        c, self.cap = self.cap, None
        return c

    def replay_skewed(self, streams):
        allops = []
        for si, (ops, t0, dt) in enumerate(streams):
            for i, o in enumerate(ops):
                allops.append((t0 + i * dt, si, i, o))
        allops.sort(key=lambda z: (z[0], z[1], z[2]))
        for _, _, _, o in allops:
            self.add(*o)

    def add(self, eng, fn, reads=(), writes=(), dma=False):
        if self.cap is not None:
            self.cap.append((eng, fn, list(reads), list(writes), dma))
            return None
        op = Op()
        op.eng = eng
        op.fn = fn
        op.dma = dma
        op.sig = dma
        op.sigval = 0
        op.sem = None
        op.idx = self.nops
        self.nops += 1
        deps = {}
        xr = [t for t in reads if t.excl]
        if xr:
            reads = [t for t in reads if not t.excl]
            writes = list(writes) + [t for t in xr if t not in writes]

        def need(d, war):
            if d is None:
                return
            if (not d.dma) and (not dma) and d.eng == eng:
                if eng == "pe" or war:
                    return
            deps[d.idx] = d

        for t in reads:
            need(t.w, False)
        for t in writes:
            need(t.w, False)
            for r in t.r.values():
                need(r, True)
        if dma:
            h = self.dma_hist[eng]
            k = len(h)
            op.sem = (eng, k % self.ndma)
            op.sigval = 16 * (k // self.ndma + 1)
            if k >= self.ndma:
                d = h[k - self.ndma]
                deps[d.idx] = d
            h.append(op)
        op.deps = list(deps.values())
        for d in op.deps:
            d.sig = True
        for t in reads:
            key = (eng, op.idx) if dma else eng
            t.r[key] = op
        for t in writes:
            t.w = op
            t.r = {}
        self.ops[eng].append(op)
        return op

    def barrier(self):
        lasts = []
        for e in self.ENGS:
            comp = [o for o in self.ops[e] if not o.dma and o.fn is not None]
            if comp:
                lasts.append(comp[-1])
            h = self.dma_hist[e]
            lasts.extend(h[-self.ndma:])
        for e in self.ENGS:
            op = Op()
            op.eng = e
            op.fn = None
            op.dma = False
            op.sig = False
            op.sigval = 0
            op.sem = None
            op.idx = self.nops
            self.nops += 1
            op.deps = [d for d in lasts if not (d.eng == e and not d.dma)]
            for d in op.deps:
                d.sig = True
            self.ops[e].append(op)

    def emit(self, stack):
        nc = self.nc
        sems = {}
        for e in self.ENGS:
            sems[e] = stack.enter_context(nc.semaphore("s_" + e))
            for k in range(self.ndma):
                sems[(e, k)] = stack.enter_context(nc.semaphore("d_%s%d" % (e, k)))
        for e in self.ENGS:
            c = 0
            for op in self.ops[e]:
                if op.dma or op.fn is None:
                    continue
                if op.sig:
                    c += 1
                    op.sigval = c
                    op.sem = e
        block = stack.enter_context(nc.Block())
        prog = self

        def run(engname, eng):
            waited = {}
            for op in prog.ops[engname]:
                for d in op.deps:
                    if waited.get(d.sem, 0) >= d.sigval:
                        continue
                    eng.wait_ge(sems[d.sem], d.sigval)
                    waited[d.sem] = d.sigval
                if op.fn is None:
                    continue
                ins = op.fn(eng)
                if op.sig:
                    ins.then_inc(sems[op.sem], 16 if op.dma else 1)
            for op in prog.dma_hist[engname][-prog.ndma:]:
                if waited.get(op.sem, 0) < op.sigval:
                    eng.wait_ge(sems[op.sem], op.sigval)
                    waited[op.sem] = op.sigval

        @block.tensor
        def _(e):
            run("pe", e)

        @block.scalar
        def _(e):
            run("act", e)

        @block.vector
        def _(e):
            run("dve", e)

        @block.gpsimd
        def _(e):
            run("pool", e)

        @block.sync
        def _(e):
            run("sp", e)


import ml_dtypes
from contextlib import ExitStack
from concourse.bass_utils import run_bass_kernel_spmd

D = 1024
DFF = 2816
NF = DFF // 128
INW = 2464
EPS = 1e-6
GN_EPS = 1e-5
TWO_PI = 6.283185307179586
C1 = 6.28125
C2 = TWO_PI - C1
MAGIC = 12582912.0
PI_SAFE = 3.1415925
ARENA = 60416


class _Stop(Exception):
    pass


def build(S, NSEQ, stop=None):
    NT = S // 128
    NQB = S // 512
    nc = bass.Bass("TRN2", target_bir_lowering=False)

    def din(name, shape, dt=F32):
        return nc.dram_tensor(name, shape, dt, kind="ExternalInput").ap()

    x_d = din("x", [NSEQ * S, D])
    pos_d = din("positions", [NSEQ * S], I32)
    n1_d = din("norm1_g", [1, D])
    win_d = din("w_in", [D, INW])
    lf_d = din("ret_decay_logit_fwd", [1, 8])
    lb_d = din("ret_decay_logit_bwd", [1, 8])
    gqa_d = din("q_a_norm_g", [1, 256])
    wuq_d = din("w_uq", [256, 768])
    gkva_d = din("kv_a_norm_g", [1, 128])
    wukv_d = din("w_ukv", [128, 1024])
    gq_d = din("q_norm_g", [1, 96])
    gk_d = din("k_norm_g", [1, 96])
    wo_d = din("w_o", [D, D])
    n2_d = din("norm2_g", [1, D])
    wg_d = din("w_gate", [D, DFF])
    wu_d = din("w_up", [D, DFF])
    wd_d = din("w_down", [DFF, D])
    cf_d = din("cf", [128, 308])
    id_d = din("ident", [128, 128], BF16)
    out_d = nc.dram_tensor("out", [NSEQ * S, D], F32, kind="ExternalOutput").ap()
    win_s = nc.dram_tensor("win_s", [D, INW], BF16).ap()
    wuq_s = nc.dram_tensor("wuq_s", [256, 768], BF16).ap()
    wukv_s = nc.dram_tensor("wukv_s", [128, 1024], BF16).ap()
    wo_s = nc.dram_tensor("wo_s", [D, D], BF16).ap()
    wgu_s = nc.dram_tensor("wgu_s", [NF, 128, 2, 8, 128], BF16).ap()
    wd_s = nc.dram_tensor("wd_s", [DFF, D], BF16).ap()

    st = ExitStack()
    with st:
        def sb(name, shape, dt):
            return st.enter_context(nc.sbuf_tensor(name, shape, dt))

        AR = sb("arena", [128, ARENA], BF16)
        YT = sb("YT", [128, 8 * S], BF16)
        XT = [sb("XT%d" % i, [128, D], F32) for i in range(2)]
        H = sb("H", [128, D], BF16)
        HT = [sb("HT%d" % i, [128, D], BF16) for i in range(2)]
        CF = sb("CF", [128, 308], F32)
        ID = sb("ID", [128, 128], BF16)
        G1 = sb("G1", [128, 8], F32)
        G2 = sb("G2", [128, 8], F32)
        GQA = sb("GQA", [128, 2], F32)
        GKVA = sb("GKVA", [128, 1], F32)
        GQS = sb("GQS", [128, 96], F32)
        GK = sb("GK", [128, 96], F32)
        LGT = sb("LGT", [128, 16], F32)
        LG = sb("LG", [128, 16], F32)
        ZT = sb("ZT", [128, 48], F32)
        ZX = sb("ZX", [128, 48], F32)
        XFB = sb("XFB", [128, 16], F32)
        STT_ = sb("STATE", [128, 512], F32)
        POSI = sb("POSI", [128, NT], I32)
        POSF = sb("POSF", [128, NT], F32)
        SM = sb("SM", [128, 64], F32)
        SSR = sb("SSR", [128, NT], F32)
        QT2 = sb("QT2", [128, 2048], BF16)
        QT2v = QT2[:, :].rearrange("p (h s) -> p h s", h=4)
        PS = [st.enter_context(nc.psum_tensor("ps%d" % i, [128, 1024], F32)) for i in range(4)]

        def bank(i):
            return PS[i // 2][:, (i % 2) * 512:(i % 2 + 1) * 512]

        def bankb(i):
            return bank(i).bitcast(BF16)

        def arb(off, n):
            return AR[:, off:off + n]

        def arf(off, n):
            return AR[:, off:off + 2 * n].bitcast(F32)

        P = Prog(nc)
        toks = {}

        def T(name):
            t = toks.get(name)
            if t is None:
                excl = (name[0] == "b" and name[1:].isdigit()) or name in ("S", "S0", "S1", "bO0", "bO1")
                t = toks[name] = Tok(name, excl)
            return t

        def R(*names):
            return [T(n) for n in names]

        def MM(out, lhsT, rhs, start, stop, r, w, **kw):
            return P.add("pe", lambda e: e.matmul(out, lhsT=lhsT, rhs=rhs, start=start, stop=stop, **kw), r, w)

        def TR(out, in_, r, w):
            return P.add("pe", lambda e: e.transpose(out=out, in_=in_, identity=ID[:, :]), r + [T("ID")], w)

        def TTo(eng, out, in0, in1, op, r, w):
            return P.add(eng, lambda e: e.tensor_tensor(out=out, in0=in0, in1=in1, op=op), r, w)

        def TS(out, in0, s1, s2, op0, op1, r, w):
            if s2 is None:
                return P.add("dve", lambda e: e.tensor_scalar(out=out, in0=in0, scalar1=s1, scalar2=None, op0=op0), r, w)
            return P.add("dve", lambda e: e.tensor_scalar(out=out, in0=in0, scalar1=s1, scalar2=s2, op0=op0, op1=op1), r, w)

        def STT(out, in0, scalar, in1, op0, op1, r, w):
            return P.add("dve", lambda e: e.scalar_tensor_tensor(out=out, in0=in0, scalar=scalar, in1=in1, op0=op0, op1=op1), r, w)

        def ACTV(out, in_, func, r, w, scale=1.0, bias=0.0, accum=None):
            if accum is None:
                return P.add("act", lambda e: e.activation(out=out, in_=in_, func=func, bias=bias, scale=scale), r, w)
            return P.add("act", lambda e: e.activation(out=out, in_=in_, func=func, bias=bias, scale=scale, accum_out=accum), r, w)

        def CP(eng, out, in_, r, w):
            if eng == "act":
                return P.add("act", lambda e: e.copy(out=out, in_=in_), r, w)
            return P.add(eng, lambda e: e.tensor_copy(out=out, in_=in_), r, w)

        def RED(out, in_, r, w):
            return P.add("dve", lambda e: e.reduce_sum(out=out, in_=in_, axis=AX.X), r, w)

        def RECIP(out, in_, r, w):
            return P.add("dve", lambda e: e.reciprocal(out=out, in_=in_), r, w)

        def MEMSET(eng, ap, val, w):
            return P.add(eng, lambda e: e.memset(ap, val), [], w)

        def DMA(q, out, in_, r, w, slow=False):
            if slow:
                return P.add(q, lambda e: e.dma_start(out=out, in_=in_, allow_slow_non_contiguous=True), r, w, dma=True)
            return P.add(q, lambda e: e.dma_start(out=out, in_=in_), r, w, dma=True)

        def rsqrt_chain(dst, src, n_inv, eps, r, w, tmp):
            ACTV(tmp, src, AF.Ln, r, [T("rs_tmp")], scale=n_inv, bias=eps)
            ACTV(dst, tmp, AF.Exp, [T("rs_tmp")], w, scale=-0.5)

        DMA("sp", CF[:, :], cf_d[:, :], [], R("CF"))
        DMA("sp", ID[:, :], id_d[:, :], [], R("ID"))
        DMA("sp", G1[:, :], n1_d.rearrange("o (k p) -> p (o k)", p=128), [], R("G1"), slow=True)
        DMA("sp", G2[:, :], n2_d.rearrange("o (k p) -> p (o k)", p=128), [], R("G2"), slow=True)
        DMA("sp", GQA[:, :], gqa_d.rearrange("o (k p) -> p (o k)", p=128), [], R("GQA"), slow=True)
        DMA("sp", GKVA[:, :], gkva_d.rearrange("o (k p) -> p (o k)", p=128), [], R("GKVA"), slow=True)
        DMA("sp", GQS[:, :], gq_d.to_broadcast([128, 96]), [], R("GQS"), slow=True)
        DMA("sp", GK[:, :], gk_d.to_broadcast([128, 96]), [], R("GK"), slow=True)
        DMA("sp", LGT[:, 0:8], lf_d.to_broadcast([128, 8]), [], R("LGT"), slow=True)
        DMA("sp", LGT[:, 8:16], lb_d.to_broadcast([128, 8]), R("LGT"), R("LGT"), slow=True)
        TS(GQS[:, :], GQS[:, :], 96.0 ** -0.5, None, ALU.mult, None, R("GQS"), R("GQS"))
        ACTV(LG[:, :], LGT[:, :], AF.Exp, R("LGT"), R("LG"), scale=-1.0)
        ACTV(LG[:, :], LG[:, :], AF.Ln, R("LG"), R("LG"), bias=1.0)
        TS(LG[:, :], LG[:, :], -1.0, None, ALU.mult, None, R("LG"), R("LG"))
        IDX = CF[:, 304:308]
        for j, (lo, ic) in enumerate([(0, 0), (8, 1), (0, 2), (8, 3)]):
            TS(ZX[:, j * 8:(j + 1) * 8], LG[:, lo:lo + 8], IDX[:, ic:ic + 1], None, ALU.mult, None, R("LG", "CF"), R("ZX"))
        TS(ZX[:, 32:48], LG[:, 0:16], 128.0, None, ALU.mult, None, R("LG"), R("ZX"))
        ACTV(ZT[:, :], ZX[:, :], AF.Exp, R("ZX"), R("ZT"))
        XFBv = XFB[:, :].rearrange("p (h d) -> p h d", d=2)
        CP("dve", XFBv[:, :, 0], ZT[:, 16:24], R("ZT"), R("XFB"))
        CP("dve", XFBv[:, :, 1], ZT[:, 24:32], R("ZT", "XFB"), R("XFB"))
        ZF, ZB, GCF, GCB = ZT[:, 0:8], ZT[:, 8:16], ZT[:, 32:40], ZT[:, 40:48]

        SF = [arf(i * 5632, 2816) for i in range(2)]
        SBF = [arb(11264 + i * 2816, 2816) for i in range(2)]
        cnt = [0]

        def prep(src, K, N, gain, gtok, dst_fn):
            for kc in range(K // 128):
                s = cnt[0] % 2
                cnt[0] += 1
                DMA("sp", SF[s][:, 0:N], src[kc * 128:(kc + 1) * 128, :], [], R("SF%d" % s))
                if gain is not None:
                    TS(SBF[s][:, 0:N], SF[s][:, 0:N], gain[:, kc:kc + 1], None, ALU.mult, None,
                       R("SF%d" % s, gtok), R("SB%d" % s))
                else:
                    CP("act", SBF[s][:, 0:N], SF[s][:, 0:N], R("SF%d" % s), R("SB%d" % s))
                dst_fn(kc, SBF[s], R("SB%d" % s))

        prep(win_d, D, INW, G1, "G1", lambda kc, t, r: DMA("pool", win_s[kc * 128:(kc + 1) * 128, :], t[:, 0:INW], r, []))
        prep(wuq_d, 256, 768, GQA, "GQA", lambda kc, t, r: DMA("pool", wuq_s[kc * 128:(kc + 1) * 128, :], t[:, 0:768], r, []))
        prep(wukv_d, 128, 1024, GKVA, "GKVA", lambda kc, t, r: DMA("pool", wukv_s[:, :], t[:, 0:1024], r, []))
        prep(wo_d, D, D, None, None, lambda kc, t, r: DMA("pool", wo_s[kc * 128:(kc + 1) * 128, :], t[:, 0:D], r, []))
        for gi, wsrc in enumerate((wg_d, wu_d)):
            prep(wsrc, D, DFF, G2, "G2",
                 lambda kc, t, r, gi=gi: DMA("pool", wgu_s[:, :, gi, kc, :].rearrange("f p n -> p f n"),
                                             t[:, 0:DFF].rearrange("p (f n) -> p f n", n=128), r, []))
        prep(wd_d, DFF, D, None, None, lambda kc, t, r: DMA("pool", wd_s[kc * 128:(kc + 1) * 128, :], t[:, 0:D], r, []))
        P.barrier()

        xc = [0]

        def xnorm(row0, xdst=None, xtok=None, ht_eng="act"):
            s = xc[0] % 2
            xc[0] += 1
            if xdst is None:
                xdst, xtok = XT[s][:, :], "XT%d" % s
                DMA("sp", xdst, x_d[row0:row0 + 128, :], [], R(xtok))
            ACTV(H[:, :], xdst, AF.Square, R(xtok), R("H", "xss"), accum=SM[:, 0:1])
            rsqrt_chain(SM[:, 2:3], SM[:, 0:1], 1.0 / D, EPS, R("xss"), R("xrs"), SM[:, 1:2])
            TS(H[:, :], xdst, SM[:, 2:3], None, ALU.mult, None, R(xtok, "xrs"), R("H"))
            bT = bankb(0)
            for k in range(8):
                TR(bT[:, k * 128:(k + 1) * 128], H[:, k * 128:(k + 1) * 128], R("H"), R("b0"))
            CP(ht_eng, HT[s][:, :], bT[:, :], R("b0"), R("HT%d" % s))
            return HT[s], "HT%d" % s

        def tile_loop(order, row_of, main):
            P.capture()
            nxt = xnorm(row_of(order[0]))
            P.replay_skewed([(P.end_capture(), 0.0, 1.0)])
            for i, t in enumerate(order):
                cur = nxt
                P.capture()
                main(t, cur[0], cur[1])
                mops = P.end_capture()
                streams = [(mops, 0.0, 1.0)]
                if i + 1 < len(order):
                    P.capture()
                    nxt = xnorm(row_of(order[i + 1]))
                    xops = P.end_capture()
                    streams.append((xops, 0.3 * len(mops), 0.6 * len(mops) / max(1, len(xops))))
                P.replay_skewed(streams)

        def proj(bi, ht, httok, W, wtok, c0, n):
            for k in range(8):
                MM(bank(bi)[:, 0:n], ht[:, k * 128:(k + 1) * 128], W[:, k, c0:c0 + n], k == 0, k == 7,
                   R(httok, wtok), R("b%d" % bi))

        def rotary(src, stok, dst, dtok, cos, sin, tabtok, nh, half, tA, tB, tAtok, tBtok):
            cb = cos.unsqueeze(1).unsqueeze(1).to_broadcast([128, nh, 2, half])
            sbh = sin.unsqueeze(1).to_broadcast([128, nh, half])
            TTo("dve", tA, src, cb, ALU.mult, R(stok, tabtok, tAtok), R(tAtok))
            TTo("pool", tB[:, :, 0, :], src[:, :, 1, :], sbh, ALU.mult, R(stok, tabtok, tBtok), R(tBtok))
            TTo("pool", tB[:, :, 1, :], src[:, :, 0, :], sbh, ALU.mult, R(stok, tabtok, tBtok), R(tBtok))
            TTo("dve", dst[:, :, 0, :], tA[:, :, 0, :], tB[:, :, 0, :], ALU.subtract, R(tAtok, tBtok, dtok), R(dtok))
            TTo("dve", dst[:, :, 1, :], tA[:, :, 1, :], tB[:, :, 1, :], ALU.add, R(tAtok, tBtok, dtok), R(dtok))

        def trig_tables(n, invf, cosd, sind, tmp, tabtok):
            ang, u, r_ = tmp
            TTo("dve", ang.rearrange("p (t f) -> p t f", f=n), POSF[:, :].unsqueeze(2).to_broadcast([128, NT, n]),
                invf.unsqueeze(1).to_broadcast([128, NT, n]), ALU.mult, R("POSF", "CF"), R("ang"))
            for shift, dst in ((0.0, sind), (np.pi / 2, cosd)):
                if shift:
                    TS(ang, ang, shift, None, ALU.add, None, R("ang"), R("ang"))
                TS(u, ang, 1.0 / TWO_PI, MAGIC, ALU.mult, ALU.add, R("ang"), R("trU"))
                TS(u, u, -MAGIC, None, ALU.add, None, R("trU"), R("trU"))
                STT(r_, u, -C1, ang, ALU.mult, ALU.add, R("trU", "ang"), R("trR"))
                STT(r_, u, -C2, r_, ALU.mult, ALU.add, R("trU", "trR"), R("trR"))
                TS(r_, r_, -PI_SAFE, PI_SAFE, ALU.max, ALU.min, R("trR"), R("trR"))
                ACTV(dst, r_, AF.Sin, R("trR"), R(tabtok))

        def checkpoint(k):
            if stop is not None and k == stop:
                raise _Stop()

        try:
            for sq in range(NSEQ):
                base = sq * S
                checkpoint(0)
                DMA("sp", POSI[:, :], pos_d[base:base + S].rearrange("(t p) -> p t", p=128), [], R("POSI"), slow=True)
                CP("dve", POSF[:, :], POSI[:, :], R("POSI"), R("POSF"))

                checkpoint(1)
                WR = arb(0, 12288).rearrange("p (k n) -> p k n", k=8)
                WR1 = arb(0, 8192).rearrange("p (k n) -> p k n", k=8)
                KTOK = arb(12288, NT * 512).rearrange("p (t n) -> p t n", n=512)
                RBC = arb(28672, (NT // 2 + 1) * 512).rearrange("p (t n) -> p t n", n=512)
                RJ = arb(37888, 1024).rearrange("p (t n) -> p t n", n=512)
                DTm = arf(38912, 1024)
                COSR = arf(40960, NT * 32)
                SINR = arf(40960 + NT * 64, NT * 32)
                o = 40960 + NT * 128
                F = [arf(o + i * 1024, 512) for i in range(4)]
                o += 4096
                B = [arb(o + i * 1024, 1024) for i in range(6)]
                o += 6144
                assert o <= ARENA
                tmp3 = [arf(12288 + i * (NT * 64), NT * 32) for i in range(3)]
                trig_tables(32, CF[:, 0:32], COSR, SINR, tmp3, "TABR")
                for h in range(8):
                    TS(F[0][:, 0:128], CF[:, 48:176], LG[:, h:h + 1], None, ALU.mult, None, R("CF", "LG"), R("F0"))
                    STT(F[1][:, 0:128], CF[:, 176:304], LG[:, 8 + h:9 + h], F[0][:, 0:128], ALU.mult, ALU.add,
                        R("CF", "LG", "F0"), R("F1"))
                    ACTV(DTm[:, h * 128:(h + 1) * 128], F[1][:, 0:128], AF.Exp, R("F1"), R("DT"))
                MEMSET("dve", STT_[:, :], 0.0, R("ST"))
                P.barrier()

                def v4(ap, nh, half):
                    return ap.rearrange("p (h d f) -> p h d f", h=nh, d=2)

                def v3(ap, a):
                    return ap.rearrange("p (a b) -> p a b", a=a)

                checkpoint(2)
                DMA("sp", WR1[:, :, :], win_s[:, 512:1536].rearrange("(k p) n -> p k n", p=128), [], R("WR"))
                def sweep1_main(t, ht, htt):
                    proj(1, ht, htt, WR1, "WR", 0, 512)
                    proj(2, ht, htt, WR1, "WR", 512, 512)
                    ACTV(F[0][:, :], bank(1), AF.Copy, R("b1"), R("F0"), scale=0.125)
                    rotary(v4(F[0][:, :], 8, 32), "F0", v4(KTOK[:, t, :], 8, 32), "KTOK%d" % t,
                           COSR[:, t * 32:(t + 1) * 32], SINR[:, t * 32:(t + 1) * 32], "TABR", 8, 32,
                           v4(F[1][:, :], 8, 32), v4(F[2][:, :], 8, 32), "F1", "F2")
                    TTo("dve", v3(B[0][:, 0:512], 8), v3(bank(2), 8), ZB.unsqueeze(2).to_broadcast([128, 8, 64]), ALU.mult,
                        R("b2", "ZT"), R("B0"))
                    for h in range(8):
                        MM(bank(3)[64:128, h * 64:(h + 1) * 64], KTOK[:, t, h * 64:(h + 1) * 64], B[0][:, h * 64:(h + 1) * 64],
                           True, True, R("KTOK%d" % t, "B0"), R("b3"), tile_position=(0, 64))
                    pr = (t % 2) * 64
                    CP("act", RBC[pr:pr + 64, t // 2, :], STT_[64:128, :], R("ST"), R("RBC%d" % t))
                    TTo("dve", v3(F[3][64:128, :], 8), v3(STT_[64:128, :], 8), GCB[64:128, :].unsqueeze(2).to_broadcast([64, 8, 64]),
                        ALU.mult, R("ST", "ZT"), R("F3"))
                    TTo("dve", STT_[64:128, :], F[3][64:128, :], bank(3)[64:128, :], ALU.add, R("F3", "b3"), R("ST"))

                tile_loop(list(range(NT - 1, -1, -1)), lambda t: base + t * 128, sweep1_main)
                P.barrier()

                checkpoint(3)
                DMA("sp", WR[:, :, 0:512], win_s[:, 0:512].rearrange("(k p) n -> p k n", p=128), [], R("WR"))
                DMA("sp", WR[:, :, 512:1024], win_s[:, 1536:2048].rearrange("(k p) n -> p k n", p=128), R("WR"), R("WR"))
                DMA("sp", WR[:, :, 1024:1536], win_s[:, 1024:1536].rearrange("(k p) n -> p k n", p=128), R("WR"), R("WR"))
                MEMSET("dve", RJ[0:64, 0, :], 0.0, R("RJ0"))
                CP("dve", RJ[64:128, 0, :], RBC[0:64, 0, :], R("RBC0", "RJ0"), R("RJ0"))
                def sweep2_main(t, ht, htt):
                    proj(1, ht, htt, WR, "WR", 0, 512)
                    proj(2, ht, htt, WR, "WR", 512, 512)
                    proj(3, ht, htt, WR, "WR", 1024, 512)
                    ACTV(F[0][:, :], bank(1), AF.Copy, R("b1"), R("F0"))
                    rotary(v4(F[0][:, :], 8, 32), "F0", v4(F[3][:, :], 8, 32), "F3",
                           COSR[:, t * 32:(t + 1) * 32], SINR[:, t * 32:(t + 1) * 32], "TABR", 8, 32,
                           v4(F[1][:, :], 8, 32), v4(F[2][:, :], 8, 32), "F1", "F2")
                    CP("dve", B[0][:, 0:512], F[3][:, :], R("F3"), R("B0"))
                    TTo("dve", B[1][:, :].rearrange("p (h d e) -> p h d e", h=8, d=2),
                        v3(F[3][:, :], 8).unsqueeze(2).to_broadcast([128, 8, 2, 64]),
                        XFB[:, :].rearrange("p (h d) -> p h d", d=2).unsqueeze(3).to_broadcast([128, 8, 2, 64]),
                        ALU.mult, R("F3", "XFB"), R("B1"))
                    CP("act", B[2][:, 0:512], bank(3), R("b3"), R("B2a"))
                    TTo("dve", v3(B[2][:, 512:1024], 8), v3(bank(3), 8), ZF.unsqueeze(2).to_broadcast([128, 8, 64]), ALU.mult,
                        R("b3", "ZT"), R("B2b"))
                    b6 = bankb(6)
                    for j in range(4):
                        TR(b6[:, j * 128:(j + 1) * 128], B[0][:, j * 128:(j + 1) * 128], R("B0"), R("b6"))
                    for j in range(4):
                        TR(b6[:, 512 + j * 128:512 + (j + 1) * 128], KTOK[:, t, j * 128:(j + 1) * 128], R("KTOK%d" % t), R("b6"))
                    CP("act", B[3][:, :], b6[:, :], R("b6"), R("B3"))
                    for h in range(8):
                        TR(b6[:, h * 128:(h + 1) * 128], B[1][:, h * 128:(h + 1) * 128], R("B1"), R("b6"))
                    CP("dve", B[4][:, :], b6[:, :], R("b6"), R("B4"))
                    for h in range(8):
                        j, hp = h // 2, (h % 2) * 64
                        MM(PS[2][:, (h % 2) * 512 + j * 128:(h % 2) * 512 + (j + 1) * 128],
                           B[3][hp:hp + 64, 512 + j * 128:512 + (j + 1) * 128],
                           B[3][hp:hp + 64, j * 128:(j + 1) * 128], True, True, R("B3"), R("S"))
                    for c2 in range(2):
                        TTo("dve", B[5][:, :].rearrange("p (j q c) -> p j q c", j=4, q=2)[:, :, c2, :],
                            PS[2][:, c2 * 512:(c2 + 1) * 512].rearrange("p (j c) -> p j c", j=4),
                            DTm[:, :].rearrange("p (j q c) -> p j q c", j=4, q=2)[:, :, c2, :], ALU.mult, R("S", "DT", "B5"), R("B5"))
                    for h in range(8):
                        MM(bank(7)[0:64, h * 64:(h + 1) * 64], KTOK[:, t, h * 64:(h + 1) * 64], B[2][:, 512 + h * 64:512 + (h + 1) * 64],
                           True, True, R("KTOK%d" % t, "B2b"), R("b7"))
                    rj = t % 2
                    for h in range(8):
                        MM(bank(3)[:, h * 64:(h + 1) * 64], B[5][:, h * 128:(h + 1) * 128], B[2][:, h * 64:(h + 1) * 64],
                           True, False, R("B5", "B2a", "B2b"), R("b3"))
                        MM(bank(3)[:, h * 64:(h + 1) * 64], B[4][:, h * 128:(h + 1) * 128], RJ[:, rj, h * 64:(h + 1) * 64],
                           False, True, R("B4", "RJ%d" % rj), R("b3"))
                    if t + 1 < NT:
                        nrj = (t + 1) % 2
                        TTo("dve", v3(F[0][0:64, :], 8), v3(STT_[0:64, :], 8), GCF[0:64, :].unsqueeze(2).to_broadcast([64, 8, 64]),
                            ALU.mult, R("ST", "ZT", "F0"), R("F0"))
                        TTo("dve", STT_[0:64, :], F[0][0:64, :], bank(7)[0:64, :], ALU.add, R("F0", "b7"), R("ST"))
                        CP("act", RJ[0:64, nrj, :], STT_[0:64, :], R("ST"), R("RJ%d" % nrj))
                        pr = ((t + 1) % 2) * 64
                        CP("dve", RJ[64:128, nrj, :], RBC[pr:pr + 64, (t + 1) // 2, :], R("RBC%d" % (t + 1), "RJ%d" % nrj), R("RJ%d" % nrj))
                    ACTV(F[0][:, :], bank(3), AF.Copy, R("b3"), R("F0"))
                    RED(SM[:, 8:16], v3(F[0][:, :], 8), R("F0"), R("mu"))
                    ACTV(F[1][:, :], F[0][:, :], AF.Square, R("F0"), R("F1"))
                    RED(SM[:, 16:24], v3(F[1][:, :], 8), R("F1"), R("m2"))
                    TS(SM[:, 8:16], SM[:, 8:16], 1.0 / 64, None, ALU.mult, None, R("mu"), R("mu"))
                    TTo("dve", SM[:, 24:32], SM[:, 8:16], SM[:, 8:16], ALU.mult, R("mu"), R("msq"))
                    STT(SM[:, 16:24], SM[:, 16:24], 1.0 / 64, SM[:, 24:32], ALU.mult, ALU.subtract, R("m2", "msq"), R("m2"))
                    rsqrt_chain(SM[:, 32:40], SM[:, 16:24], 1.0, GN_EPS, R("m2"), R("grs"), SM[:, 40:48])
                    TTo("dve", v3(F[2][:, :], 8), v3(F[0][:, :], 8), SM[:, 8:16].unsqueeze(2).to_broadcast([128, 8, 64]), ALU.subtract,
                        R("F0", "mu"), R("F2"))
                    TTo("dve", v3(F[2][:, :], 8), v3(F[2][:, :], 8), SM[:, 32:40].unsqueeze(2).to_broadcast([128, 8, 64]), ALU.mult,
                        R("F2", "grs"), R("F2"))
                    ACTV(F[1][:, :], bank(2), AF.Exp, R("b2", "F1"), R("F1"), scale=-1.0)
                    ACTV(F[1][:, :], F[1][:, :], AF.Ln, R("F1"), R("F1"), bias=1.0)
                    ACTV(F[1][:, :], F[1][:, :], AF.Exp, R("F1"), R("F1"), scale=-1.0)
                    TTo("dve", F[3][:, :], bank(2), F[1][:, :], ALU.mult, R("b2", "F1", "F3"), R("F3"))
                    TTo("dve", B[0][:, 0:512], F[2][:, :], F[3][:, :], ALU.mult, R("F2", "F3", "B0"), R("B0"))
                    bT = bankb(6)
                    for j in range(4):
                        TR(bT[:, j * 128:(j + 1) * 128], B[0][:, j * 128:(j + 1) * 128], R("B0"), R("b6"))
                    CP("act", YT[:, :].rearrange("p (k s) -> p k s", k=8)[:, 0:4, t * 128:(t + 1) * 128],
                       bT[:, 0:512].rearrange("p (k s) -> p k s", k=4), R("b6"), R("YTr%d" % t))

                tile_loop(list(range(NT)), lambda t: base + t * 128, sweep2_main)
                P.barrier()

                checkpoint(4)
                CQT = arb(0, 2 * S).rearrange("p (k s) -> p k s", k=2)
                CKVT = arb(8192, S)
                KRB = arf(12288, NT * 32)
                KT = arb(14336, 4 * S).rearrange("p (h s) -> p h s", h=4)
                VA = arb(30720, NT * 512).rearrange("p (t h c) -> p t h c", h=4, c=128)
                WM = arb(47104, 3328).rearrange("p (k n) -> p k n", k=8)
                QT = arb(47104, 2048).rearrange("p (h s) -> p h s", h=4)
                RL = arf(47104 + 2048, 512)
                WUQ = arb(50432, 1536).rearrange("p (k n) -> p k n", k=2)
                WUKV = arb(51968, 1024)
                PTA = [arb(52992 + i * 1024, 1024) for i in range(3)]
                COSM = arf(56064, NT * 16)
                SINM = arf(57088, NT * 16)
                MF = [arf(58112 + i * 1024, 512) for i in range(2)]
                tmpm = [arf(52992 + i * 1024, NT * 16) for i in range(3)]
                DMA("sp", WM[:, :, :], win_s[:, 2048:2464].rearrange("(k p) n -> p k n", p=128), [], R("WM"))
                DMA("sp", WUQ[:, :, :], wuq_s.rearrange("(k p) n -> p k n", p=128), [], R("WUQ"))
                DMA("sp", WUKV[:, :], wukv_s[:, :], [], R("WUKV"))
                trig_tables(16, CF[:, 32:48], COSM, SINM, tmpm, "TABM")
                P.barrier()
                def mla_main(t, ht, htt):
                    proj(1, ht, htt, WM, "WM", 0, 416)
                    b1 = bank(1)
                    ACTV(MF[0][:, 0:256], b1[:, 0:256], AF.Square, R("b1"), R("MF0", "ssq"), accum=SM[:, 8:9])
                    rsqrt_chain(SM[:, 10:11], SM[:, 8:9], 1.0 / 256, EPS, R("ssq"), R("rq"), SM[:, 9:10])
                    TS(B_cq := MF[1][:, 0:128].bitcast(BF16), b1[:, 0:256], SM[:, 10:11], None, ALU.mult, None, R("b1", "rq"), R("cq"))
                    ACTV(MF[0][:, 256:384], b1[:, 256:384], AF.Square, R("b1"), R("MF0b", "sskv"), accum=SM[:, 11:12])
                    rsqrt_chain(SM[:, 13:14], SM[:, 11:12], 1.0 / 128, EPS, R("sskv"), R("rkv"), SM[:, 12:13])
                    TS(B_ckv := MF[1][:, 128:192].bitcast(BF16), b1[:, 256:384], SM[:, 13:14], None, ALU.mult, None, R("b1", "rkv"), R("ckv"))
                    b2 = bankb(2)
                    TR(b2[:, 0:128], B_cq[:, 0:128], R("cq"), R("b2"))
                    TR(b2[:, 128:256], B_cq[:, 128:256], R("cq"), R("b2"))
                    TR(b2[:, 256:384], B_ckv[:, 0:128], R("ckv"), R("b2"))
                    CP("act", CQT[:, :, t * 128:(t + 1) * 128], b2[:, 0:256].rearrange("p (k s) -> p k s", k=2), R("b2"), R("CQT%d" % t))
                    CP("dve", CKVT[:, t * 128:(t + 1) * 128], b2[:, 256:384], R("b2"), R("CKVT%d" % t))
                    ACTV(MF[0][:, 384:416], b1[:, 384:416], AF.Square, R("b1"), R("MF0c", "SSR"), accum=SSR[:, t:t + 1])
                    TTo("dve", MF[0][:, 448:480], b1[:, 384:416], GK[:, 64:96], ALU.mult, R("b1", "GK"), R("krg"))
                    rotary(MF[0][:, 448:480].rearrange("p (h d f) -> p h d f", h=1, d=2), "krg",
                           KRB[:, t * 32:(t + 1) * 32].rearrange("p (h d f) -> p h d f", h=1, d=2), "KRB%d" % t,
                           COSM[:, t * 16:(t + 1) * 16], SINM[:, t * 16:(t + 1) * 16], "TABM", 1, 16,
                           MF[1][:, 256:288].rearrange("p (h d f) -> p h d f", h=1, d=2),
                           MF[1][:, 288:320].rearrange("p (h d f) -> p h d f", h=1, d=2), "mrA", "mrB")

                tile_loop(list(range(NT)), lambda t: base + t * 128, mla_main)
                P.barrier()
                checkpoint(5)
                for hh in range(2):
                    MEMSET("pool", VA[:, :, :, 64:128], 1.0, R("VAones"))
                    kd_lists = []
                    for t in range(NT):
                        p = t % 2
                        bkv, bkvt = bank(3 + p), "b%d" % (3 + p)
                        btr, btrt = (bankb(2), "b2") if p == 0 else (bankb(5), "b5")
                        kb = 52992 + p * 1408
                        k0 = arf(kb, 256).rearrange("p (h c) -> p h c", h=4)
                        k1 = arf(kb + 512, 256).rearrange("p (h c) -> p h c", h=4)
                        KTM = arb(kb + 1024, 384).rearrange("p (h c) -> p h c", h=4)
                        so = 16 if p == 0 else 48
                        ssh, stmp, srk = SM[:, so:so + 4], SM[:, so + 4:so + 8], SM[:, so + 8:so + 12]
                        tn = lambda n, p=p: "%s_%d" % (n, p)
                        P.capture()
                        MM(bkv, CKVT[:, t * 128:(t + 1) * 128], WUKV[:, hh * 512:(hh + 1) * 512], True, True,
                           R("CKVT%d" % t, "WUKV"), R(bkvt))
                        kv = bkv.rearrange("p (h c) -> p h c", h=4)
                        ACTV(k0, kv[:, :, 0:64], AF.Square, R(bkvt), R(tn("KD0")))
                        RED(ssh, k0, R(tn("KD0")), R(tn("ssh")))
                        TS(ssh, ssh, SSR[:, t:t + 1], None, ALU.add, None, R(tn("ssh"), "SSR"), R(tn("ssh")))
                        ACTV(stmp, ssh, AF.Ln, R(tn("ssh")), R(tn("kdtmp")), scale=1.0 / 96, bias=EPS)
                        ACTV(srk, stmp, AF.Exp, R(tn("kdtmp")), R(tn("rk")), scale=-0.5)
                        TTo("dve", k1, kv[:, :, 0:64], srk.unsqueeze(2).to_broadcast([128, 4, 64]), ALU.mult, R(bkvt, tn("rk")), R(tn("KD1")))
                        TTo("dve", KTM[:, :, 0:64], k1, GK[:, 0:64].unsqueeze(1).to_broadcast([128, 4, 64]), ALU.mult,
                            R(tn("KD1"), "GK"), R(tn("KTM")))
                        TTo("dve", KTM[:, :, 64:96], KRB[:, t * 32:(t + 1) * 32].unsqueeze(1).to_broadcast([128, 4, 32]),
                            srk.unsqueeze(2).to_broadcast([128, 4, 32]), ALU.mult, R("KRB%d" % t, tn("rk"), tn("KTM")), R(tn("KTM")))
                        for hl in range(4):
                            TR(btr[0:96, hl * 128:(hl + 1) * 128], KTM[:, hl, :], R(tn("KTM")), R(btrt))
                        CP("act", KT[0:96, :, t * 128:(t + 1) * 128], btr[0:96, 0:512].rearrange("p (h s) -> p h s", h=4),
                           R(btrt), R("KT%d" % t))
                        CP("dve", VA[:, t, :, 0:64], kv[:, :, 64:128], R(bkvt, "VAones"), R("VA%d" % t))
                        kd_lists.append(P.end_capture())
                    Lk = len(kd_lists[0])
                    P.replay_skewed([(ops, t * Lk / 2.0, 1.0) for t, ops in enumerate(kd_lists)])
                    P.barrier()
                    checkpoint(6)
                    NK = NT // 2
                    QTs = [QT, QT2v]

                    def qprepA(qb, s):
                        tq = qb * 4 + s
                        b6 = bank(6)
                        for k in range(2):
                            MM(b6[:, 0:384], CQT[:, k, tq * 128:(tq + 1) * 128], WUQ[:, k, hh * 384:(hh + 1) * 384], k == 0, k == 1,
                               R("CQT%d" % tq, "WUQ"), R("b6"))
                        q3 = b6[:, 0:384].rearrange("p (h c) -> p h c", h=4)
                        m0 = MF[0][:, 0:384].rearrange("p (h c) -> p h c", h=4)
                        m1 = MF[1][:, 0:384].rearrange("p (h c) -> p h c", h=4)
                        ACTV(m0, q3, AF.Square, R("b6"), R("MF0"))
                        RED(SM[:, 16:20], m0, R("MF0"), R("ssh"))
                        rsqrt_chain(SM[:, 24:28], SM[:, 16:20], 1.0 / 96, EPS, R("ssh"), R("rk"), SM[:, 20:24])
                        TTo("dve", m1, q3, SM[:, 24:28].unsqueeze(2).to_broadcast([128, 4, 96]), ALU.mult, R("b6", "rk"), R("MF1"))
                        TTo("dve", m1, m1, GQS[:, :].unsqueeze(1).to_broadcast([128, 4, 96]), ALU.mult, R("MF1", "GQS"), R("MF1"))
                        qtk = H[:, 0:384].rearrange("p (h c) -> p h c", h=4)
                        CP("dve", qtk[:, :, 0:64], m1[:, :, 0:64], R("MF1", "H"), R("H"))
                        src = m1[:, :, 64:96].rearrange("p h (d f) -> p h d f", d=2)
                        dst = qtk[:, :, 64:96].rearrange("p h (d f) -> p h d f", d=2)
                        tA = m0[:, :, 0:32].rearrange("p h (d f) -> p h d f", d=2)
                        tB = m0[:, :, 32:64].rearrange("p h (d f) -> p h d f", d=2)
                        rotary(src, "MF1", dst, "H", COSM[:, tq * 16:(tq + 1) * 16], SINM[:, tq * 16:(tq + 1) * 16], "TABM", 4, 16,
                               tA, tB, "MF0", "MF0")

                    def qprepC(qb, s):
                        qtk = H[:, 0:384].rearrange("p (h c) -> p h c", h=4)
                        b7 = bankb(7)
                        for hl in range(4):
                            TR(b7[0:96, hl * 128:(hl + 1) * 128], qtk[:, hl, :], R("H"), R("b7"))
                        CP("dve", QTs[qb % 2][0:96, :, s * 128:(s + 1) * 128], b7[0:96, 0:512].rearrange("p (h s) -> p h s", h=4),
                           R("b7"), R("QT%d" % (qb % 2)))

                    units = [(qb, hl, kp) for qb in range(NQB) for hl in range(4) for kp in range(NK)]

                    def emitQK(i):
                        qb, hl, kp = units[i]
                        ssl = i % 2
                        for j in range(2):
                            kt = 2 * kp + j
                            MM(PS[ssl][:, j * 512:(j + 1) * 512], KT[0:96, hl, kt * 128:(kt + 1) * 128], QTs[qb % 2][0:96, hl, :], True, True,
                               R("KT%d" % kt, "QT%d" % (qb % 2)), R("S%d" % ssl))

                    for s in range(4):
                        qprepA(0, s)
                        qprepC(0, s)
                    emitQK(0)
                    if len(units) > 1:
                        emitQK(1)
                    UB = 4 * NK
                    sched = {}
                    if UB >= 40:
                        for s in range(4):
                            sched[2 + 9 * s] = ("A", s)
                            sched[2 + 9 * s + 5] = ("C", s)
                    for i, (qb, hl, kp) in enumerate(units):
                        h = hh * 4 + hl
                        osl = (qb * 4 + hl) % 2
                        bO = bank(4 + osl)
                        ssl, psl = i % 2, i % 3
                        ACTV(PTA[psl][:, :], PS[ssl][:, :], AF.Exp, R("S%d" % ssl), R("PTA%d" % psl))
                        for j in range(2):
                            kt = 2 * kp + j
                            MM(bO, VA[:, kt, hl, :], PTA[psl][:, j * 512:(j + 1) * 512], kp == 0 and j == 0,
                               kp == NK - 1 and j == 1, R("VA%d" % kt, "PTA%d" % psl), R("bO%d" % osl))
                        if i + 2 < len(units) and not (UB < 40 and units[i + 2][0] != qb):
                            emitQK(i + 2)
                        ju = i % UB
                        if qb + 1 < NQB:
                            if UB >= 40:
                                ev = sched.get(ju)
                                if ev is not None:
                                    (qprepA if ev[0] == "A" else qprepC)(qb + 1, ev[1])
                        if kp == NK - 1:
                            RECIP(RL[64:128, :], bO[64:128, :], R("bO%d" % osl), R("RL"))
                            pr = (h % 2) * 64
                            TTo("dve", YT[:, :].rearrange("p (k s) -> p k s", k=8)[pr:pr + 64, 4 + h // 2, qb * 512:(qb + 1) * 512],
                                bO[0:64, :], RL[64:128, :], ALU.mult, R("bO%d" % osl, "RL"), R("YTm%d_%d" % (h, qb)))
                        if UB < 40 and ju == UB - 1 and qb + 1 < NQB:
                            for s in range(4):
                                qprepA(qb + 1, s)
                                qprepC(qb + 1, s)
                            emitQK(i + 1)
                            if i + 2 < len(units):
                                emitQK(i + 2)
                    P.barrier()

                checkpoint(7)
                WO = arb(0, 8192).rearrange("p (k n) -> p k n", k=8)
                X1 = arf(8192, 4096).rearrange("p (s n) -> p s n", s=4)
                H2T = arb(16384, 4096).rearrange("p (k s) -> p k s", k=8)
                AT = arb(20480, NF * 512).rearrange("p (f s) -> p f s", f=NF)
                NWG = 4
                WGU = [arb(31744 + i * 2048, 2048).rearrange("p (g k n) -> p g k n", g=2, k=8) for i in range(NWG)]
                WDQ = [arb(39936 + i * 5632, 5632).rearrange("p (f n) -> p f n", f=NF) for i in range(3)]
                SG = [arf(56832 + i * 1024, 512) for i in range(2)]
                DMA("sp", WO[:, :, :], wo_s.rearrange("(k p) n -> p k n", p=128), [], R("WO"))
                YT3 = YT[:, :].rearrange("p (k s) -> p k s", k=8)
                wq = [0]
                for qb in range(NQB):
                    for s in range(4):
                        tq = qb * 4 + s
                        row0 = base + tq * 128
                        xs = "X1_%d" % s
                        DMA("pool", X1[:, s, :], x_d[row0:row0 + 128, :], [], R(xs))
                        for c2 in range(2):
                            for k in range(8):
                                MM(bank(c2), YT3[:, k, tq * 128:(tq + 1) * 128], WO[:, k, c2 * 512:(c2 + 1) * 512], k == 0, k == 7,
                                   R("WO"), R("b%d" % c2))
                            TTo("dve", X1[:, s, c2 * 512:(c2 + 1) * 512], X1[:, s, c2 * 512:(c2 + 1) * 512], bank(c2), ALU.add,
                                R(xs, "b%d" % c2), R(xs))
                        ACTV(H[:, :], X1[:, s, :], AF.Square, R(xs), R("H", "xss"), accum=SM[:, 0:1])
                        rsqrt_chain(SM[:, 2:3], SM[:, 0:1], 1.0 / D, EPS, R("xss"), R("xrs"), SM[:, 1:2])
                        TS(H[:, :], X1[:, s, :], SM[:, 2:3], None, ALU.mult, None, R(xs, "xrs"), R("H"))
                        bT = bankb(2)
                        for k in range(8):
                            TR(bT[:, k * 128:(k + 1) * 128], H[:, k * 128:(k + 1) * 128], R("H"), R("b2"))
                        CP("act", H2T[:, :, s * 128:(s + 1) * 128], bT[:, :].rearrange("p (k s) -> p k s", k=8), R("b2"), R("H2T"))
                    for f in range(NF):
                        sl = f % 2
                        wsl_ = f % NWG
                        DMA("sp", WGU[wsl_][:, :, :, :], wgu_s[f], [], R("WGU%d" % wsl_))
                        for g in range(2):
                            bi = 3 + 2 * g + sl
                            for k in range(8):
                                MM(bank(bi), WGU[wsl_][:, g, k, :], H2T[:, k, :], k == 0, k == 7, R("WGU%d" % wsl_, "H2T"), R("b%d" % bi))
                        bG, bU = bank(3 + sl), bank(5 + sl)
                        ACTV(SG[0][:, :], bG, AF.Exp, R("b%d" % (3 + sl)), R("SG0"), scale=-1.0)
                        ACTV(SG[0][:, :], SG[0][:, :], AF.Ln, R("SG0"), R("SG0"), bias=1.0)
                        ACTV(SG[0][:, :], SG[0][:, :], AF.Exp, R("SG0"), R("SG0"), scale=-1.0)
                        TTo("dve", SG[1][:, :], bG, SG[0][:, :], ALU.mult, R("b%d" % (3 + sl), "SG0"), R("SG1"))
                        TTo("dve", AT[:, f, :], SG[1][:, :], bU, ALU.mult, R("SG1", "b%d" % (5 + sl)), R("AT"))
                    for q4 in range(4):
                        wsl = wq[0] % 3
                        wq[0] += 1
                        DMA("sp", WDQ[wsl][:, :, :], wd_s[:, q4 * 256:(q4 + 1) * 256].rearrange("(f p) n -> p f n", p=128), [],
                            R("WDQ%d" % wsl))
                        for s in range(4):
                            bi = s % 2
                            for f in range(NF):
                                MM(bank(bi)[:, 0:256], AT[:, f, s * 128:(s + 1) * 128], WDQ[wsl][:, f, :], f == 0, f == NF - 1,
                                   R("AT", "WDQ%d" % wsl), R("b%d" % bi))
                            TTo("dve", X1[:, s, q4 * 256:(q4 + 1) * 256], X1[:, s, q4 * 256:(q4 + 1) * 256], bank(bi)[:, 0:256], ALU.add,
                                R("X1_%d" % s, "b%d" % bi), R("X1_%d" % s))
                    for s in range(4):
                        row0 = base + (qb * 4 + s) * 128
                        DMA("pool", out_d[row0:row0 + 128, :], X1[:, s, :], R("X1_%d" % s), [])
                P.barrier()
        except _Stop:
            pass
        P.emit(st)
    return nc


def make_consts():
    cf = np.zeros((128, 308), np.float32)
    cf[:, 0:32] = (10000.0 ** (-np.arange(32, dtype=np.float32) / 32.0))[None, :]
    cf[:, 32:48] = (10000.0 ** (-np.arange(16, dtype=np.float32) / 16.0))[None, :]
    m = np.arange(128, dtype=np.float32)[:, None]
    c = np.arange(128, dtype=np.float32)[None, :]
    cf[:, 48:176] = np.maximum(c - m, 0.0)
    cf[:, 176:304] = np.maximum(m - c, 0.0)
    cf[:, 304] = 127.0 - m[:, 0]
    cf[:, 305] = m[:, 0]
    cf[:, 306] = m[:, 0] + 1.0
    cf[:, 307] = 128.0 - m[:, 0]
    ident = np.eye(128, dtype=np.float32).astype(ml_dtypes.bfloat16)
    return cf, ident


_CACHE = {}


def run(inputs, S, NSEQ, ncores):
    key = (S, NSEQ)
    if key not in _CACHE:
        _CACHE[key] = build(S, NSEQ)
    nc = _CACHE[key]
    cf, ident = make_consts()
    x = np.ascontiguousarray(np.asarray(inputs["x"], dtype=np.float32))
    pos = np.ascontiguousarray(np.asarray(inputs["positions"], dtype=np.int32))
    wnames = ["norm1_g", "w_in", "ret_decay_logit_fwd", "ret_decay_logit_bwd", "q_a_norm_g", "w_uq", "kv_a_norm_g",
              "w_ukv", "q_norm_g", "k_norm_g", "w_o", "norm2_g", "w_gate", "w_up", "w_down"]
    shared = {}
    for n in wnames:
        a = np.asarray(inputs[n], dtype=np.float32)
        shared[n] = np.ascontiguousarray(a[0])
    for n in ["norm1_g", "ret_decay_logit_fwd", "ret_decay_logit_bwd", "q_a_norm_g", "kv_a_norm_g", "q_norm_g", "k_norm_g", "norm2_g"]:
        shared[n] = shared[n].reshape(1, -1)
    shared["cf"] = cf
    shared["ident"] = ident
    in_maps = []
    for c in range(ncores):
        m = dict(shared)
        m["x"] = x[c * NSEQ:(c + 1) * NSEQ].reshape(NSEQ * S, D)
        m["positions"] = pos[c * NSEQ:(c + 1) * NSEQ].reshape(NSEQ * S)
        in_maps.append(m)
    res = run_bass_kernel_spmd(nc, in_maps, core_ids=list(range(ncores)))
    outs = [np.asarray(r["out"]).reshape(NSEQ, S, D) for r in res.results]
    return np.concatenate(outs, axis=0).astype(np.float32)


def kernel(**inputs):
    return run(inputs, 4096, 2, 8)
```

```python
import numpy as np
import concourse.bass as bass
import concourse.mybir as mybir

F32 = mybir.dt.float32
BF16 = mybir.dt.bfloat16
I32 = mybir.dt.int32
ALU = mybir.AluOpType
AF = mybir.ActivationFunctionType
AX = mybir.AxisListType


class Tok:
    __slots__ = ("w", "r", "name", "excl")

    def __init__(self, name="", excl=False):
        self.w = None
        self.r = {}
        self.name = name
        self.excl = excl


class Op:
    __slots__ = ("eng", "fn", "deps", "sig", "sigval", "sem", "dma", "idx")


class Prog:
    ENGS = ("pe", "act", "dve", "pool", "sp")

    def __init__(self, nc, ndma=8):
        self.nc = nc
        self.ops = {e: [] for e in self.ENGS}
        self.ndma = ndma
        self.dma_hist = {e: [] for e in self.ENGS}
        self.nops = 0
        self.cap = None

    def capture(self):
        self.cap = []

    def end_capture(self):
        c, self.cap = self.cap, None
        return c

    def replay_skewed(self, streams):
        allops = []
        for si, (ops, t0, dt) in enumerate(streams):
            for i, o in enumerate(ops):
                allops.append((t0 + i * dt, si, i, o))
        allops.sort(key=lambda z: (z[0], z[1], z[2]))
        for _, _, _, o in allops:
            self.add(*o)

    def add(self, eng, fn, reads=(), writes=(), dma=False):
        if self.cap is not None:
            self.cap.append((eng, fn, list(reads), list(writes), dma))
            return None
        op = Op()
        op.eng = eng
        op.fn = fn
        op.dma = dma
        op.sig = dma
        op.sigval = 0
        op.sem = None
        op.idx = self.nops
        self.nops += 1
        deps = {}
        xr = [t for t in reads if t.excl]
        if xr:
            reads = [t for t in reads if not t.excl]
            writes = list(writes) + [t for t in xr if t not in writes]

        def need(d, war):
            if d is None:
                return
            if (not d.dma) and (not dma) and d.eng == eng:
                if eng == "pe" or war:
                    return
            deps[d.idx] = d

        for t in reads:
            need(t.w, False)
        for t in writes:
            need(t.w, False)
            for r in t.r.values():
                need(r, True)
        if dma:
            h = self.dma_hist[eng]
            k = len(h)
            op.sem = (eng, k % self.ndma)
            op.sigval = 16 * (k // self.ndma + 1)
            if k >= self.ndma:
                d = h[k - self.ndma]
                deps[d.idx] = d
            h.append(op)
        op.deps = list(deps.values())
        for d in op.deps:
            d.sig = True
        for t in reads:
            key = (eng, op.idx) if dma else eng
            t.r[key] = op
        for t in writes:
            t.w = op
            t.r = {}
        self.ops[eng].append(op)
        return op

    def barrier(self):
        lasts = []
        for e in self.ENGS:
            comp = [o for o in self.ops[e] if not o.dma and o.fn is not None]
            if comp:
                lasts.append(comp[-1])
            h = self.dma_hist[e]
            lasts.extend(h[-self.ndma:])
        for e in self.ENGS:
            op = Op()
            op.eng = e
            op.fn = None
            op.dma = False
            op.sig = False
            op.sigval = 0
            op.sem = None
            op.idx = self.nops
            self.nops += 1
            op.deps = [d for d in lasts if not (d.eng == e and not d.dma)]
            for d in op.deps:
                d.sig = True
            self.ops[e].append(op)

    def emit(self, stack):
        nc = self.nc
        sems = {}
        for e in self.ENGS:
            sems[e] = stack.enter_context(nc.semaphore("s_" + e))
            for k in range(self.ndma):
                sems[(e, k)] = stack.enter_context(nc.semaphore("d_%s%d" % (e, k)))
        for e in self.ENGS:
            c = 0
            for op in self.ops[e]:
                if op.dma or op.fn is None:
                    continue
                if op.sig:
                    c += 1
                    op.sigval = c
                    op.sem = e
        block = stack.enter_context(nc.Block())
        prog = self

        def run(engname, eng):
            waited = {}
            for op in prog.ops[engname]:
                for d in op.deps:
                    if waited.get(d.sem, 0) >= d.sigval:
                        continue
                    eng.wait_ge(sems[d.sem], d.sigval)
                    waited[d.sem] = d.sigval
                if op.fn is None:
                    continue
                ins = op.fn(eng)
                if op.sig:
                    ins.then_inc(sems[op.sem], 16 if op.dma else 1)
            for op in prog.dma_hist[engname][-prog.ndma:]:
                if waited.get(op.sem, 0) < op.sigval:
                    eng.wait_ge(sems[op.sem], op.sigval)
                    waited[op.sem] = op.sigval

        @block.tensor
        def _(e):
            run("pe", e)

        @block.scalar
        def _(e):
            run("act", e)

        @block.vector
        def _(e):
            run("dve", e)

        @block.gpsimd
        def _(e):
            run("pool", e)

        @block.sync
        def _(e):
            run("sp", e)


import ml_dtypes
from contextlib import ExitStack
from concourse.bass_utils import run_bass_kernel_spmd

D = 1024
DFF = 2816
NF = DFF // 128
INW = 2464
EPS = 1e-6
GN_EPS = 1e-5
TWO_PI = 6.283185307179586
C1 = 6.28125
C2 = TWO_PI - C1
MAGIC = 12582912.0
PI_SAFE = 3.1415925
ARENA = 60416


class _Stop(Exception):
    pass


def build(S, NSEQ, stop=None):
    NT = S // 128
    NQB = S // 512
    nc = bass.Bass("TRN2", target_bir_lowering=False)

    def din(name, shape, dt=F32):
        return nc.dram_tensor(name, shape, dt, kind="ExternalInput").ap()

    x_d = din("x", [NSEQ * S, D])
    pos_d = din("positions", [NSEQ * S], I32)
    n1_d = din("norm1_g", [1, D])
    win_d = din("w_in", [D, INW])
    lf_d = din("ret_decay_logit_fwd", [1, 8])
    lb_d = din("ret_decay_logit_bwd", [1, 8])
    gqa_d = din("q_a_norm_g", [1, 256])
    wuq_d = din("w_uq", [256, 768])
    gkva_d = din("kv_a_norm_g", [1, 128])
    wukv_d = din("w_ukv", [128, 1024])
    gq_d = din("q_norm_g", [1, 96])
    gk_d = din("k_norm_g", [1, 96])
    wo_d = din("w_o", [D, D])
    n2_d = din("norm2_g", [1, D])
    wg_d = din("w_gate", [D, DFF])
    wu_d = din("w_up", [D, DFF])
    wd_d = din("w_down", [DFF, D])
    cf_d = din("cf", [128, 308])
    id_d = din("ident", [128, 128], BF16)
    out_d = nc.dram_tensor("out", [NSEQ * S, D], F32, kind="ExternalOutput").ap()
    win_s = nc.dram_tensor("win_s", [D, INW], BF16).ap()
    wuq_s = nc.dram_tensor("wuq_s", [256, 768], BF16).ap()
    wukv_s = nc.dram_tensor("wukv_s", [128, 1024], BF16).ap()
    wo_s = nc.dram_tensor("wo_s", [D, D], BF16).ap()
    wgu_s = nc.dram_tensor("wgu_s", [NF, 128, 2, 8, 128], BF16).ap()
    wd_s = nc.dram_tensor("wd_s", [DFF, D], BF16).ap()

    st = ExitStack()
    with st:
        def sb(name, shape, dt):
            return st.enter_context(nc.sbuf_tensor(name, shape, dt))

        AR = sb("arena", [128, ARENA], BF16)
        YT = sb("YT", [128, 8 * S], BF16)
        XT = [sb("XT%d" % i, [128, D], F32) for i in range(2)]
        H = sb("H", [128, D], BF16)
        HT = [sb("HT%d" % i, [128, D], BF16) for i in range(2)]
        CF = sb("CF", [128, 308], F32)
        ID = sb("ID", [128, 128], BF16)
        G1 = sb("G1", [128, 8], F32)
        G2 = sb("G2", [128, 8], F32)
        GQA = sb("GQA", [128, 2], F32)
        GKVA = sb("GKVA", [128, 1], F32)
        GQS = sb("GQS", [128, 96], F32)
        GK = sb("GK", [128, 96], F32)
        LGT = sb("LGT", [128, 16], F32)
        LG = sb("LG", [128, 16], F32)
        ZT = sb("ZT", [128, 48], F32)
        ZX = sb("ZX", [128, 48], F32)
        XFB = sb("XFB", [128, 16], F32)
        STT_ = sb("STATE", [128, 512], F32)
        POSI = sb("POSI", [128, NT], I32)
        POSF = sb("POSF", [128, NT], F32)
        SM = sb("SM", [128, 64], F32)
        SSR = sb("SSR", [128, NT], F32)
        QT2 = sb("QT2", [128, 2048], BF16)
        QT2v = QT2[:, :].rearrange("p (h s) -> p h s", h=4)
        PS = [st.enter_context(nc.psum_tensor("ps%d" % i, [128, 1024], F32)) for i in range(4)]

        def bank(i):
            return PS[i // 2][:, (i % 2) * 512:(i % 2 + 1) * 512]

        def bankb(i):
            return bank(i).bitcast(BF16)

        def arb(off, n):
            return AR[:, off:off + n]

        def arf(off, n):
            return AR[:, off:off + 2 * n].bitcast(F32)

        P = Prog(nc)
        toks = {}

        def T(name):
            t = toks.get(name)
            if t is None:
                excl = (name[0] == "b" and name[1:].isdigit()) or name in ("S", "S0", "S1", "bO0", "bO1")
                t = toks[name] = Tok(name, excl)
            return t

        def R(*names):
            return [T(n) for n in names]

        def MM(out, lhsT, rhs, start, stop, r, w, **kw):
            return P.add("pe", lambda e: e.matmul(out, lhsT=lhsT, rhs=rhs, start=start, stop=stop, **kw), r, w)

        def TR(out, in_, r, w):
            return P.add("pe", lambda e: e.transpose(out=out, in_=in_, identity=ID[:, :]), r + [T("ID")], w)

        def TTo(eng, out, in0, in1, op, r, w):
            return P.add(eng, lambda e: e.tensor_tensor(out=out, in0=in0, in1=in1, op=op), r, w)

        def TS(out, in0, s1, s2, op0, op1, r, w):
            if s2 is None:
                return P.add("dve", lambda e: e.tensor_scalar(out=out, in0=in0, scalar1=s1, scalar2=None, op0=op0), r, w)
            return P.add("dve", lambda e: e.tensor_scalar(out=out, in0=in0, scalar1=s1, scalar2=s2, op0=op0, op1=op1), r, w)

        def STT(out, in0, scalar, in1, op0, op1, r, w):
            return P.add("dve", lambda e: e.scalar_tensor_tensor(out=out, in0=in0, scalar=scalar, in1=in1, op0=op0, op1=op1), r, w)

        def ACTV(out, in_, func, r, w, scale=1.0, bias=0.0, accum=None):
            if accum is None:
                return P.add("act", lambda e: e.activation(out=out, in_=in_, func=func, bias=bias, scale=scale), r, w)
            return P.add("act", lambda e: e.activation(out=out, in_=in_, func=func, bias=bias, scale=scale, accum_out=accum), r, w)

        def CP(eng, out, in_, r, w):
            if eng == "act":
                return P.add("act", lambda e: e.copy(out=out, in_=in_), r, w)
            return P.add(eng, lambda e: e.tensor_copy(out=out, in_=in_), r, w)

        def RED(out, in_, r, w):
            return P.add("dve", lambda e: e.reduce_sum(out=out, in_=in_, axis=AX.X), r, w)

        def RECIP(out, in_, r, w):
            return P.add("dve", lambda e: e.reciprocal(out=out, in_=in_), r, w)

        def MEMSET(eng, ap, val, w):
            return P.add(eng, lambda e: e.memset(ap, val), [], w)

        def DMA(q, out, in_, r, w, slow=False):
            if slow:
                return P.add(q, lambda e: e.dma_start(out=out, in_=in_, allow_slow_non_contiguous=True), r, w, dma=True)
            return P.add(q, lambda e: e.dma_start(out=out, in_=in_), r, w, dma=True)

        def rsqrt_chain(dst, src, n_inv, eps, r, w, tmp):
            ACTV(tmp, src, AF.Ln, r, [T("rs_tmp")], scale=n_inv, bias=eps)
            ACTV(dst, tmp, AF.Exp, [T("rs_tmp")], w, scale=-0.5)

        DMA("sp", CF[:, :], cf_d[:, :], [], R("CF"))
        DMA("sp", ID[:, :], id_d[:, :], [], R("ID"))
        DMA("sp", G1[:, :], n1_d.rearrange("o (k p) -> p (o k)", p=128), [], R("G1"), slow=True)
        DMA("sp", G2[:, :], n2_d.rearrange("o (k p) -> p (o k)", p=128), [], R("G2"), slow=True)
        DMA("sp", GQA[:, :], gqa_d.rearrange("o (k p) -> p (o k)", p=128), [], R("GQA"), slow=True)
        DMA("sp", GKVA[:, :], gkva_d.rearrange("o (k p) -> p (o k)", p=128), [], R("GKVA"), slow=True)
        DMA("sp", GQS[:, :], gq_d.to_broadcast([128, 96]), [], R("GQS"), slow=True)
        DMA("sp", GK[:, :], gk_d.to_broadcast([128, 96]), [], R("GK"), slow=True)
        DMA("sp", LGT[:, 0:8], lf_d.to_broadcast([128, 8]), [], R("LGT"), slow=True)
        DMA("sp", LGT[:, 8:16], lb_d.to_broadcast([128, 8]), R("LGT"), R("LGT"), slow=True)
        TS(GQS[:, :], GQS[:, :], 96.0 ** -0.5, None, ALU.mult, None, R("GQS"), R("GQS"))
        ACTV(LG[:, :], LGT[:, :], AF.Exp, R("LGT"), R("LG"), scale=-1.0)
        ACTV(LG[:, :], LG[:, :], AF.Ln, R("LG"), R("LG"), bias=1.0)
        TS(LG[:, :], LG[:, :], -1.0, None, ALU.mult, None, R("LG"), R("LG"))
        IDX = CF[:, 304:308]
        for j, (lo, ic) in enumerate([(0, 0), (8, 1), (0, 2), (8, 3)]):
            TS(ZX[:, j * 8:(j + 1) * 8], LG[:, lo:lo + 8], IDX[:, ic:ic + 1], None, ALU.mult, None, R("LG", "CF"), R("ZX"))
        TS(ZX[:, 32:48], LG[:, 0:16], 128.0, None, ALU.mult, None, R("LG"), R("ZX"))
        ACTV(ZT[:, :], ZX[:, :], AF.Exp, R("ZX"), R("ZT"))
        XFBv = XFB[:, :].rearrange("p (h d) -> p h d", d=2)
        CP("dve", XFBv[:, :, 0], ZT[:, 16:24], R("ZT"), R("XFB"))
        CP("dve", XFBv[:, :, 1], ZT[:, 24:32], R("ZT", "XFB"), R("XFB"))
        ZF, ZB, GCF, GCB = ZT[:, 0:8], ZT[:, 8:16], ZT[:, 32:40], ZT[:, 40:48]

        SF = [arf(i * 5632, 2816) for i in range(2)]
        SBF = [arb(11264 + i * 2816, 2816) for i in range(2)]
        cnt = [0]

        def prep(src, K, N, gain, gtok, dst_fn):
            for kc in range(K // 128):
                s = cnt[0] % 2
                cnt[0] += 1
                DMA("sp", SF[s][:, 0:N], src[kc * 128:(kc + 1) * 128, :], [], R("SF%d" % s))
                if gain is not None:
                    TS(SBF[s][:, 0:N], SF[s][:, 0:N], gain[:, kc:kc + 1], None, ALU.mult, None,
                       R("SF%d" % s, gtok), R("SB%d" % s))
                else:
                    CP("act", SBF[s][:, 0:N], SF[s][:, 0:N], R("SF%d" % s), R("SB%d" % s))
                dst_fn(kc, SBF[s], R("SB%d" % s))

        prep(win_d, D, INW, G1, "G1", lambda kc, t, r: DMA("pool", win_s[kc * 128:(kc + 1) * 128, :], t[:, 0:INW], r, []))
        prep(wuq_d, 256, 768, GQA, "GQA", lambda kc, t, r: DMA("pool", wuq_s[kc * 128:(kc + 1) * 128, :], t[:, 0:768], r, []))
        prep(wukv_d, 128, 1024, GKVA, "GKVA", lambda kc, t, r: DMA("pool", wukv_s[:, :], t[:, 0:1024], r, []))
        prep(wo_d, D, D, None, None, lambda kc, t, r: DMA("pool", wo_s[kc * 128:(kc + 1) * 128, :], t[:, 0:D], r, []))
        for gi, wsrc in enumerate((wg_d, wu_d)):
            prep(wsrc, D, DFF, G2, "G2",
                 lambda kc, t, r, gi=gi: DMA("pool", wgu_s[:, :, gi, kc, :].rearrange("f p n -> p f n"),
                                             t[:, 0:DFF].rearrange("p (f n) -> p f n", n=128), r, []))
        prep(wd_d, DFF, D, None, None, lambda kc, t, r: DMA("pool", wd_s[kc * 128:(kc + 1) * 128, :], t[:, 0:D], r, []))
        P.barrier()

        xc = [0]

        def xnorm(row0, xdst=None, xtok=None, ht_eng="act"):
            s = xc[0] % 2
            xc[0] += 1
            if xdst is None:
                xdst, xtok = XT[s][:, :], "XT%d" % s
                DMA("sp", xdst, x_d[row0:row0 + 128, :], [], R(xtok))
            ACTV(H[:, :], xdst, AF.Square, R(xtok), R("H", "xss"), accum=SM[:, 0:1])
            rsqrt_chain(SM[:, 2:3], SM[:, 0:1], 1.0 / D, EPS, R("xss"), R("xrs"), SM[:, 1:2])
            TS(H[:, :], xdst, SM[:, 2:3], None, ALU.mult, None, R(xtok, "xrs"), R("H"))
            bT = bankb(0)
            for k in range(8):
                TR(bT[:, k * 128:(k + 1) * 128], H[:, k * 128:(k + 1) * 128], R("H"), R("b0"))
            CP(ht_eng, HT[s][:, :], bT[:, :], R("b0"), R("HT%d" % s))
            return HT[s], "HT%d" % s

        def tile_loop(order, row_of, main):
            P.capture()
            nxt = xnorm(row_of(order[0]))
            P.replay_skewed([(P.end_capture(), 0.0, 1.0)])
            for i, t in enumerate(order):
                cur = nxt
                P.capture()
                main(t, cur[0], cur[1])
                mops = P.end_capture()
                streams = [(mops, 0.0, 1.0)]
                if i + 1 < len(order):
                    P.capture()
                    nxt = xnorm(row_of(order[i + 1]))
                    xops = P.end_capture()
                    streams.append((xops, 0.3 * len(mops), 0.6 * len(mops) / max(1, len(xops))))
                P.replay_skewed(streams)

        def proj(bi, ht, httok, W, wtok, c0, n):
            for k in range(8):
                MM(bank(bi)[:, 0:n], ht[:, k * 128:(k + 1) * 128], W[:, k, c0:c0 + n], k == 0, k == 7,
                   R(httok, wtok), R("b%d" % bi))

        def rotary(src, stok, dst, dtok, cos, sin, tabtok, nh, half, tA, tB, tAtok, tBtok):
            cb = cos.unsqueeze(1).unsqueeze(1).to_broadcast([128, nh, 2, half])
            sbh = sin.unsqueeze(1).to_broadcast([128, nh, half])
            TTo("dve", tA, src, cb, ALU.mult, R(stok, tabtok, tAtok), R(tAtok))
            TTo("pool", tB[:, :, 0, :], src[:, :, 1, :], sbh, ALU.mult, R(stok, tabtok, tBtok), R(tBtok))
            TTo("pool", tB[:, :, 1, :], src[:, :, 0, :], sbh, ALU.mult, R(stok, tabtok, tBtok), R(tBtok))
            TTo("dve", dst[:, :, 0, :], tA[:, :, 0, :], tB[:, :, 0, :], ALU.subtract, R(tAtok, tBtok, dtok), R(dtok))
            TTo("dve", dst[:, :, 1, :], tA[:, :, 1, :], tB[:, :, 1, :], ALU.add, R(tAtok, tBtok, dtok), R(dtok))

        def trig_tables(n, invf, cosd, sind, tmp, tabtok):
            ang, u, r_ = tmp
            TTo("dve", ang.rearrange("p (t f) -> p t f", f=n), POSF[:, :].unsqueeze(2).to_broadcast([128, NT, n]),
                invf.unsqueeze(1).to_broadcast([128, NT, n]), ALU.mult, R("POSF", "CF"), R("ang"))
            for shift, dst in ((0.0, sind), (np.pi / 2, cosd)):
                if shift:
                    TS(ang, ang, shift, None, ALU.add, None, R("ang"), R("ang"))
                TS(u, ang, 1.0 / TWO_PI, MAGIC, ALU.mult, ALU.add, R("ang"), R("trU"))
                TS(u, u, -MAGIC, None, ALU.add, None, R("trU"), R("trU"))
                STT(r_, u, -C1, ang, ALU.mult, ALU.add, R("trU", "ang"), R("trR"))
                STT(r_, u, -C2, r_, ALU.mult, ALU.add, R("trU", "trR"), R("trR"))
                TS(r_, r_, -PI_SAFE, PI_SAFE, ALU.max, ALU.min, R("trR"), R("trR"))
                ACTV(dst, r_, AF.Sin, R("trR"), R(tabtok))

        def checkpoint(k):
            if stop is not None and k == stop:
                raise _Stop()

        try:
            for sq in range(NSEQ):
                base = sq * S
                checkpoint(0)
                DMA("sp", POSI[:, :], pos_d[base:base + S].rearrange("(t p) -> p t", p=128), [], R("POSI"), slow=True)
                CP("dve", POSF[:, :], POSI[:, :], R("POSI"), R("POSF"))

                checkpoint(1)
                WR = arb(0, 12288).rearrange("p (k n) -> p k n", k=8)
                WR1 = arb(0, 8192).rearrange("p (k n) -> p k n", k=8)
                KTOK = arb(12288, NT * 512).rearrange("p (t n) -> p t n", n=512)
                RBC = arb(28672, (NT // 2 + 1) * 512).rearrange("p (t n) -> p t n", n=512)
                RJ = arb(37888, 1024).rearrange("p (t n) -> p t n", n=512)
                DTm = arf(38912, 1024)
                COSR = arf(40960, NT * 32)
                SINR = arf(40960 + NT * 64, NT * 32)
                o = 40960 + NT * 128
                F = [arf(o + i * 1024, 512) for i in range(4)]
                o += 4096
                B = [arb(o + i * 1024, 1024) for i in range(6)]
                o += 6144
                assert o <= ARENA
                tmp3 = [arf(12288 + i * (NT * 64), NT * 32) for i in range(3)]
                trig_tables(32, CF[:, 0:32], COSR, SINR, tmp3, "TABR")
                for h in range(8):
                    TS(F[0][:, 0:128], CF[:, 48:176], LG[:, h:h + 1], None, ALU.mult, None, R("CF", "LG"), R("F0"))
                    STT(F[1][:, 0:128], CF[:, 176:304], LG[:, 8 + h:9 + h], F[0][:, 0:128], ALU.mult, ALU.add,
                        R("CF", "LG", "F0"), R("F1"))
                    ACTV(DTm[:, h * 128:(h + 1) * 128], F[1][:, 0:128], AF.Exp, R("F1"), R("DT"))
                MEMSET("dve", STT_[:, :], 0.0, R("ST"))
                P.barrier()

                def v4(ap, nh, half):
                    return ap.rearrange("p (h d f) -> p h d f", h=nh, d=2)

                def v3(ap, a):
                    return ap.rearrange("p (a b) -> p a b", a=a)

                checkpoint(2)
                DMA("sp", WR1[:, :, :], win_s[:, 512:1536].rearrange("(k p) n -> p k n", p=128), [], R("WR"))
                def sweep1_main(t, ht, htt):
                    proj(1, ht, htt, WR1, "WR", 0, 512)
                    proj(2, ht, htt, WR1, "WR", 512, 512)
                    ACTV(F[0][:, :], bank(1), AF.Copy, R("b1"), R("F0"), scale=0.125)
                    rotary(v4(F[0][:, :], 8, 32), "F0", v4(KTOK[:, t, :], 8, 32), "KTOK%d" % t,
                           COSR[:, t * 32:(t + 1) * 32], SINR[:, t * 32:(t + 1) * 32], "TABR", 8, 32,
                           v4(F[1][:, :], 8, 32), v4(F[2][:, :], 8, 32), "F1", "F2")
                    TTo("dve", v3(B[0][:, 0:512], 8), v3(bank(2), 8), ZB.unsqueeze(2).to_broadcast([128, 8, 64]), ALU.mult,
                        R("b2", "ZT"), R("B0"))
                    for h in range(8):
                        MM(bank(3)[64:128, h * 64:(h + 1) * 64], KTOK[:, t, h * 64:(h + 1) * 64], B[0][:, h * 64:(h + 1) * 64],
                           True, True, R("KTOK%d" % t, "B0"), R("b3"), tile_position=(0, 64))
                    pr = (t % 2) * 64
                    CP("act", RBC[pr:pr + 64, t // 2, :], STT_[64:128, :], R("ST"), R("RBC%d" % t))
                    TTo("dve", v3(F[3][64:128, :], 8), v3(STT_[64:128, :], 8), GCB[64:128, :].unsqueeze(2).to_broadcast([64, 8, 64]),
                        ALU.mult, R("ST", "ZT"), R("F3"))
                    TTo("dve", STT_[64:128, :], F[3][64:128, :], bank(3)[64:128, :], ALU.add, R("F3", "b3"), R("ST"))

                tile_loop(list(range(NT - 1, -1, -1)), lambda t: base + t * 128, sweep1_main)
                P.barrier()

                checkpoint(3)
                DMA("sp", WR[:, :, 0:512], win_s[:, 0:512].rearrange("(k p) n -> p k n", p=128), [], R("WR"))
                DMA("sp", WR[:, :, 512:1024], win_s[:, 1536:2048].rearrange("(k p) n -> p k n", p=128), R("WR"), R("WR"))
                DMA("sp", WR[:, :, 1024:1536], win_s[:, 1024:1536].rearrange("(k p) n -> p k n", p=128), R("WR"), R("WR"))
                MEMSET("dve", RJ[0:64, 0, :], 0.0, R("RJ0"))
                CP("dve", RJ[64:128, 0, :], RBC[0:64, 0, :], R("RBC0", "RJ0"), R("RJ0"))
                def sweep2_main(t, ht, htt):
                    proj(1, ht, htt, WR, "WR", 0, 512)
                    proj(2, ht, htt, WR, "WR", 512, 512)
                    proj(3, ht, htt, WR, "WR", 1024, 512)
                    ACTV(F[0][:, :], bank(1), AF.Copy, R("b1"), R("F0"))
                    rotary(v4(F[0][:, :], 8, 32), "F0", v4(F[3][:, :], 8, 32), "F3",
                           COSR[:, t * 32:(t + 1) * 32], SINR[:, t * 32:(t + 1) * 32], "TABR", 8, 32,
                           v4(F[1][:, :], 8, 32), v4(F[2][:, :], 8, 32), "F1", "F2")
                    CP("dve", B[0][:, 0:512], F[3][:, :], R("F3"), R("B0"))
                    TTo("dve", B[1][:, :].rearrange("p (h d e) -> p h d e", h=8, d=2),
                        v3(F[3][:, :], 8).unsqueeze(2).to_broadcast([128, 8, 2, 64]),
                        XFB[:, :].rearrange("p (h d) -> p h d", d=2).unsqueeze(3).to_broadcast([128, 8, 2, 64]),
                        ALU.mult, R("F3", "XFB"), R("B1"))
                    CP("act", B[2][:, 0:512], bank(3), R("b3"), R("B2a"))
                    TTo("dve", v3(B[2][:, 512:1024], 8), v3(bank(3), 8), ZF.unsqueeze(2).to_broadcast([128, 8, 64]), ALU.mult,
                        R("b3", "ZT"), R("B2b"))
                    b6 = bankb(6)
                    for j in range(4):
                        TR(b6[:, j * 128:(j + 1) * 128], B[0][:, j * 128:(j + 1) * 128], R("B0"), R("b6"))
                    for j in range(4):
                        TR(b6[:, 512 + j * 128:512 + (j + 1) * 128], KTOK[:, t, j * 128:(j + 1) * 128], R("KTOK%d" % t), R("b6"))
                    CP("act", B[3][:, :], b6[:, :], R("b6"), R("B3"))
                    for h in range(8):
                        TR(b6[:, h * 128:(h + 1) * 128], B[1][:, h * 128:(h + 1) * 128], R("B1"), R("b6"))
                    CP("dve", B[4][:, :], b6[:, :], R("b6"), R("B4"))
                    for h in range(8):
                        j, hp = h // 2, (h % 2) * 64
                        MM(PS[2][:, (h % 2) * 512 + j * 128:(h % 2) * 512 + (j + 1) * 128],
                           B[3][hp:hp + 64, 512 + j * 128:512 + (j + 1) * 128],
                           B[3][hp:hp + 64, j * 128:(j + 1) * 128], True, True, R("B3"), R("S"))
                    for c2 in range(2):
                        TTo("dve", B[5][:, :].rearrange("p (j q c) -> p j q c", j=4, q=2)[:, :, c2, :],
                            PS[2][:, c2 * 512:(c2 + 1) * 512].rearrange("p (j c) -> p j c", j=4),
                            DTm[:, :].rearrange("p (j q c) -> p j q c", j=4, q=2)[:, :, c2, :], ALU.mult, R("S", "DT", "B5"), R("B5"))
                    for h in range(8):
                        MM(bank(7)[0:64, h * 64:(h + 1) * 64], KTOK[:, t, h * 64:(h + 1) * 64], B[2][:, 512 + h * 64:512 + (h + 1) * 64],
                           True, True, R("KTOK%d" % t, "B2b"), R("b7"))
                    rj = t % 2
                    for h in range(8):
                        MM(bank(3)[:, h * 64:(h + 1) * 64], B[5][:, h * 128:(h + 1) * 128], B[2][:, h * 64:(h + 1) * 64],
                           True, False, R("B5", "B2a", "B2b"), R("b3"))
                        MM(bank(3)[:, h * 64:(h + 1) * 64], B[4][:, h * 128:(h + 1) * 128], RJ[:, rj, h * 64:(h + 1) * 64],
                           False, True, R("B4", "RJ%d" % rj), R("b3"))
                    if t + 1 < NT:
                        nrj = (t + 1) % 2
                        TTo("dve", v3(F[0][0:64, :], 8), v3(STT_[0:64, :], 8), GCF[0:64, :].unsqueeze(2).to_broadcast([64, 8, 64]),
                            ALU.mult, R("ST", "ZT", "F0"), R("F0"))
                        TTo("dve", STT_[0:64, :], F[0][0:64, :], bank(7)[0:64, :], ALU.add, R("F0", "b7"), R("ST"))
                        CP("act", RJ[0:64, nrj, :], STT_[0:64, :], R("ST"), R("RJ%d" % nrj))
                        pr = ((t + 1) % 2) * 64
                        CP("dve", RJ[64:128, nrj, :], RBC[pr:pr + 64, (t + 1) // 2, :], R("RBC%d" % (t + 1), "RJ%d" % nrj), R("RJ%d" % nrj))
                    ACTV(F[0][:, :], bank(3), AF.Copy, R("b3"), R("F0"))
                    RED(SM[:, 8:16], v3(F[0][:, :], 8), R("F0"), R("mu"))
                    ACTV(F[1][:, :], F[0][:, :], AF.Square, R("F0"), R("F1"))
                    RED(SM[:, 16:24], v3(F[1][:, :], 8), R("F1"), R("m2"))
                    TS(SM[:, 8:16], SM[:, 8:16], 1.0 / 64, None, ALU.mult, None, R("mu"), R("mu"))
                    TTo("dve", SM[:, 24:32], SM[:, 8:16], SM[:, 8:16], ALU.mult, R("mu"), R("msq"))
                    STT(SM[:, 16:24], SM[:, 16:24], 1.0 / 64, SM[:, 24:32], ALU.mult, ALU.subtract, R("m2", "msq"), R("m2"))
                    rsqrt_chain(SM[:, 32:40], SM[:, 16:24], 1.0, GN_EPS, R("m2"), R("grs"), SM[:, 40:48])
                    TTo("dve", v3(F[2][:, :], 8), v3(F[0][:, :], 8), SM[:, 8:16].unsqueeze(2).to_broadcast([128, 8, 64]), ALU.subtract,
                        R("F0", "mu"), R("F2"))
                    TTo("dve", v3(F[2][:, :], 8), v3(F[2][:, :], 8), SM[:, 32:40].unsqueeze(2).to_broadcast([128, 8, 64]), ALU.mult,
                        R("F2", "grs"), R("F2"))
                    ACTV(F[1][:, :], bank(2), AF.Exp, R("b2", "F1"), R("F1"), scale=-1.0)
                    ACTV(F[1][:, :], F[1][:, :], AF.Ln, R("F1"), R("F1"), bias=1.0)
                    ACTV(F[1][:, :], F[1][:, :], AF.Exp, R("F1"), R("F1"), scale=-1.0)
                    TTo("dve", F[3][:, :], bank(2), F[1][:, :], ALU.mult, R("b2", "F1", "F3"), R("F3"))
                    TTo("dve", B[0][:, 0:512], F[2][:, :], F[3][:, :], ALU.mult, R("F2", "F3", "B0"), R("B0"))
                    bT = bankb(6)
                    for j in range(4):
                        TR(bT[:, j * 128:(j + 1) * 128], B[0][:, j * 128:(j + 1) * 128], R("B0"), R("b6"))
                    CP("act", YT[:, :].rearrange("p (k s) -> p k s", k=8)[:, 0:4, t * 128:(t + 1) * 128],
                       bT[:, 0:512].rearrange("p (k s) -> p k s", k=4), R("b6"), R("YTr%d" % t))

                tile_loop(list(range(NT)), lambda t: base + t * 128, sweep2_main)
                P.barrier()

                checkpoint(4)
                CQT = arb(0, 2 * S).rearrange("p (k s) -> p k s", k=2)
                CKVT = arb(8192, S)
                KRB = arf(12288, NT * 32)
                KT = arb(14336, 4 * S).rearrange("p (h s) -> p h s", h=4)
                VA = arb(30720, NT * 512).rearrange("p (t h c) -> p t h c", h=4, c=128)
                WM = arb(47104, 3328).rearrange("p (k n) -> p k n", k=8)
                QT = arb(47104, 2048).rearrange("p (h s) -> p h s", h=4)
                RL = arf(47104 + 2048, 512)
                WUQ = arb(50432, 1536).rearrange("p (k n) -> p k n", k=2)
                WUKV = arb(51968, 1024)
                PTA = [arb(52992 + i * 1024, 1024) for i in range(3)]
                COSM = arf(56064, NT * 16)
                SINM = arf(57088, NT * 16)
                MF = [arf(58112 + i * 1024, 512) for i in range(2)]
                tmpm = [arf(52992 + i * 1024, NT * 16) for i in range(3)]
                DMA("sp", WM[:, :, :], win_s[:, 2048:2464].rearrange("(k p) n -> p k n", p=128), [], R("WM"))
                DMA("sp", WUQ[:, :, :], wuq_s.rearrange("(k p) n -> p k n", p=128), [], R("WUQ"))
                DMA("sp", WUKV[:, :], wukv_s[:, :], [], R("WUKV"))
                trig_tables(16, CF[:, 32:48], COSM, SINM, tmpm, "TABM")
                P.barrier()
                def mla_main(t, ht, htt):
                    proj(1, ht, htt, WM, "WM", 0, 416)
                    b1 = bank(1)
                    ACTV(MF[0][:, 0:256], b1[:, 0:256], AF.Square, R("b1"), R("MF0", "ssq"), accum=SM[:, 8:9])
                    rsqrt_chain(SM[:, 10:11], SM[:, 8:9], 1.0 / 256, EPS, R("ssq"), R("rq"), SM[:, 9:10])
                    TS(B_cq := MF[1][:, 0:128].bitcast(BF16), b1[:, 0:256], SM[:, 10:11], None, ALU.mult, None, R("b1", "rq"), R("cq"))
                    ACTV(MF[0][:, 256:384], b1[:, 256:384], AF.Square, R("b1"), R("MF0b", "sskv"), accum=SM[:, 11:12])
                    rsqrt_chain(SM[:, 13:14], SM[:, 11:12], 1.0 / 128, EPS, R("sskv"), R("rkv"), SM[:, 12:13])
                    TS(B_ckv := MF[1][:, 128:192].bitcast(BF16), b1[:, 256:384], SM[:, 13:14], None, ALU.mult, None, R("b1", "rkv"), R("ckv"))
                    b2 = bankb(2)
                    TR(b2[:, 0:128], B_cq[:, 0:128], R("cq"), R("b2"))
                    TR(b2[:, 128:256], B_cq[:, 128:256], R("cq"), R("b2"))
                    TR(b2[:, 256:384], B_ckv[:, 0:128], R("ckv"), R("b2"))
                    CP("act", CQT[:, :, t * 128:(t + 1) * 128], b2[:, 0:256].rearrange("p (k s) -> p k s", k=2), R("b2"), R("CQT%d" % t))
                    CP("dve", CKVT[:, t * 128:(t + 1) * 128], b2[:, 256:384], R("b2"), R("CKVT%d" % t))
                    ACTV(MF[0][:, 384:416], b1[:, 384:416], AF.Square, R("b1"), R("MF0c", "SSR"), accum=SSR[:, t:t + 1])
                    TTo("dve", MF[0][:, 448:480], b1[:, 384:416], GK[:, 64:96], ALU.mult, R("b1", "GK"), R("krg"))
                    rotary(MF[0][:, 448:480].rearrange("p (h d f) -> p h d f", h=1, d=2), "krg",
                           KRB[:, t * 32:(t + 1) * 32].rearrange("p (h d f) -> p h d f", h=1, d=2), "KRB%d" % t,
                           COSM[:, t * 16:(t + 1) * 16], SINM[:, t * 16:(t + 1) * 16], "TABM", 1, 16,
                           MF[1][:, 256:288].rearrange("p (h d f) -> p h d f", h=1, d=2),
                           MF[1][:, 288:320].rearrange("p (h d f) -> p h d f", h=1, d=2), "mrA", "mrB")

                tile_loop(list(range(NT)), lambda t: base + t * 128, mla_main)
                P.barrier()
                checkpoint(5)
                for hh in range(2):
                    MEMSET("pool", VA[:, :, :, 64:128], 1.0, R("VAones"))
                    kd_lists = []
                    for t in range(NT):
                        p = t % 2
                        bkv, bkvt = bank(3 + p), "b%d" % (3 + p)
                        btr, btrt = (bankb(2), "b2") if p == 0 else (bankb(5), "b5")
                        kb = 52992 + p * 1408
                        k0 = arf(kb, 256).rearrange("p (h c) -> p h c", h=4)
                        k1 = arf(kb + 512, 256).rearrange("p (h c) -> p h c", h=4)
                        KTM = arb(kb + 1024, 384).rearrange("p (h c) -> p h c", h=4)
                        so = 16 if p == 0 else 48
                        ssh, stmp, srk = SM[:, so:so + 4], SM[:, so + 4:so + 8], SM[:, so + 8:so + 12]
                        tn = lambda n, p=p: "%s_%d" % (n, p)
                        P.capture()
                        MM(bkv, CKVT[:, t * 128:(t + 1) * 128], WUKV[:, hh * 512:(hh + 1) * 512], True, True,
                           R("CKVT%d" % t, "WUKV"), R(bkvt))
                        kv = bkv.rearrange("p (h c) -> p h c", h=4)
                        ACTV(k0, kv[:, :, 0:64], AF.Square, R(bkvt), R(tn("KD0")))
                        RED(ssh, k0, R(tn("KD0")), R(tn("ssh")))
                        TS(ssh, ssh, SSR[:, t:t + 1], None, ALU.add, None, R(tn("ssh"), "SSR"), R(tn("ssh")))
                        ACTV(stmp, ssh, AF.Ln, R(tn("ssh")), R(tn("kdtmp")), scale=1.0 / 96, bias=EPS)
                        ACTV(srk, stmp, AF.Exp, R(tn("kdtmp")), R(tn("rk")), scale=-0.5)
                        TTo("dve", k1, kv[:, :, 0:64], srk.unsqueeze(2).to_broadcast([128, 4, 64]), ALU.mult, R(bkvt, tn("rk")), R(tn("KD1")))
                        TTo("dve", KTM[:, :, 0:64], k1, GK[:, 0:64].unsqueeze(1).to_broadcast([128, 4, 64]), ALU.mult,
                            R(tn("KD1"), "GK"), R(tn("KTM")))
                        TTo("dve", KTM[:, :, 64:96], KRB[:, t * 32:(t + 1) * 32].unsqueeze(1).to_broadcast([128, 4, 32]),
                            srk.unsqueeze(2).to_broadcast([128, 4, 32]), ALU.mult, R("KRB%d" % t, tn("rk"), tn("KTM")), R(tn("KTM")))
                        for hl in range(4):
                            TR(btr[0:96, hl * 128:(hl + 1) * 128], KTM[:, hl, :], R(tn("KTM")), R(btrt))
                        CP("act", KT[0:96, :, t * 128:(t + 1) * 128], btr[0:96, 0:512].rearrange("p (h s) -> p h s", h=4),
                           R(btrt), R("KT%d" % t))
                        CP("dve", VA[:, t, :, 0:64], kv[:, :, 64:128], R(bkvt, "VAones"), R("VA%d" % t))
                        kd_lists.append(P.end_capture())
                    Lk = len(kd_lists[0])
                    P.replay_skewed([(ops, t * Lk / 2.0, 1.0) for t, ops in enumerate(kd_lists)])
                    P.barrier()
                    checkpoint(6)
                    NK = NT // 2
                    QTs = [QT, QT2v]

                    def qprepA(qb, s):
                        tq = qb * 4 + s
                        b6 = bank(6)
                        for k in range(2):
                            MM(b6[:, 0:384], CQT[:, k, tq * 128:(tq + 1) * 128], WUQ[:, k, hh * 384:(hh + 1) * 384], k == 0, k == 1,
                               R("CQT%d" % tq, "WUQ"), R("b6"))
                        q3 = b6[:, 0:384].rearrange("p (h c) -> p h c", h=4)
                        m0 = MF[0][:, 0:384].rearrange("p (h c) -> p h c", h=4)
                        m1 = MF[1][:, 0:384].rearrange("p (h c) -> p h c", h=4)
                        ACTV(m0, q3, AF.Square, R("b6"), R("MF0"))
                        RED(SM[:, 16:20], m0, R("MF0"), R("ssh"))
                        rsqrt_chain(SM[:, 24:28], SM[:, 16:20], 1.0 / 96, EPS, R("ssh"), R("rk"), SM[:, 20:24])
                        TTo("dve", m1, q3, SM[:, 24:28].unsqueeze(2).to_broadcast([128, 4, 96]), ALU.mult, R("b6", "rk"), R("MF1"))
                        TTo("dve", m1, m1, GQS[:, :].unsqueeze(1).to_broadcast([128, 4, 96]), ALU.mult, R("MF1", "GQS"), R("MF1"))
                        qtk = H[:, 0:384].rearrange("p (h c) -> p h c", h=4)
                        CP("dve", qtk[:, :, 0:64], m1[:, :, 0:64], R("MF1", "H"), R("H"))
                        src = m1[:, :, 64:96].rearrange("p h (d f) -> p h d f", d=2)
                        dst = qtk[:, :, 64:96].rearrange("p h (d f) -> p h d f", d=2)
                        tA = m0[:, :, 0:32].rearrange("p h (d f) -> p h d f", d=2)
                        tB = m0[:, :, 32:64].rearrange("p h (d f) -> p h d f", d=2)
                        rotary(src, "MF1", dst, "H", COSM[:, tq * 16:(tq + 1) * 16], SINM[:, tq * 16:(tq + 1) * 16], "TABM", 4, 16,
                               tA, tB, "MF0", "MF0")

                    def qprepC(qb, s):
                        qtk = H[:, 0:384].rearrange("p (h c) -> p h c", h=4)
                        b7 = bankb(7)
                        for hl in range(4):
                            TR(b7[0:96, hl * 128:(hl + 1) * 128], qtk[:, hl, :], R("H"), R("b7"))
                        CP("dve", QTs[qb % 2][0:96, :, s * 128:(s + 1) * 128], b7[0:96, 0:512].rearrange("p (h s) -> p h s", h=4),
                           R("b7"), R("QT%d" % (qb % 2)))

                    units = [(qb, hl, kp) for qb in range(NQB) for hl in range(4) for kp in range(NK)]

                    def emitQK(i):
                        qb, hl, kp = units[i]
                        ssl = i % 2
                        for j in range(2):
                            kt = 2 * kp + j
                            MM(PS[ssl][:, j * 512:(j + 1) * 512], KT[0:96, hl, kt * 128:(kt + 1) * 128], QTs[qb % 2][0:96, hl, :], True, True,
                               R("KT%d" % kt, "QT%d" % (qb % 2)), R("S%d" % ssl))

                    for s in range(4):
                        qprepA(0, s)
                        qprepC(0, s)
                    emitQK(0)
                    if len(units) > 1:
                        emitQK(1)
                    UB = 4 * NK
                    sched = {}
                    if UB >= 40:
                        for s in range(4):
                            sched[2 + 9 * s] = ("A", s)
                            sched[2 + 9 * s + 5] = ("C", s)
                    for i, (qb, hl, kp) in enumerate(units):
                        h = hh * 4 + hl
                        osl = (qb * 4 + hl) % 2
                        bO = bank(4 + osl)
                        ssl, psl = i % 2, i % 3
                        ACTV(PTA[psl][:, :], PS[ssl][:, :], AF.Exp, R("S%d" % ssl), R("PTA%d" % psl))
                        if i + 2 < len(units) and not (UB < 40 and units[i + 2][0] != qb):
                            emitQK(i + 2)
                        for j in range(2):
                            kt = 2 * kp + j
                            MM(bO, VA[:, kt, hl, :], PTA[psl][:, j * 512:(j + 1) * 512], kp == 0 and j == 0,
                               kp == NK - 1 and j == 1, R("VA%d" % kt, "PTA%d" % psl), R("bO%d" % osl))
                        ju = i % UB
                        if qb + 1 < NQB:
                            if UB >= 40:
                                ev = sched.get(ju)
                                if ev is not None:
                                    (qprepA if ev[0] == "A" else qprepC)(qb + 1, ev[1])
                        if kp == NK - 1:
                            RECIP(RL[64:128, :], bO[64:128, :], R("bO%d" % osl), R("RL"))
                            pr = (h % 2) * 64
                            TTo("dve", YT[:, :].rearrange("p (k s) -> p k s", k=8)[pr:pr + 64, 4 + h // 2, qb * 512:(qb + 1) * 512],
                                bO[0:64, :], RL[64:128, :], ALU.mult, R("bO%d" % osl, "RL"), R("YTm%d_%d" % (h, qb)))
                        if UB < 40 and ju == UB - 1 and qb + 1 < NQB:
                            for s in range(4):
                                qprepA(qb + 1, s)
                                qprepC(qb + 1, s)
                            emitQK(i + 1)
                            if i + 2 < len(units):
                                emitQK(i + 2)
                    P.barrier()

                checkpoint(7)
                WO = arb(0, 8192).rearrange("p (k n) -> p k n", k=8)
                X1 = arf(8192, 4096).rearrange("p (s n) -> p s n", s=4)
                H2T = arb(16384, 4096).rearrange("p (k s) -> p k s", k=8)
                AT = arb(20480, NF * 512).rearrange("p (f s) -> p f s", f=NF)
                NWG = 4
                WGU = [arb(31744 + i * 2048, 2048).rearrange("p (g k n) -> p g k n", g=2, k=8) for i in range(NWG)]
                WDQ = [arb(39936 + i * 5632, 5632).rearrange("p (f n) -> p f n", f=NF) for i in range(3)]
                SG = [arf(56832 + i * 1024, 512) for i in range(2)]
                DMA("sp", WO[:, :, :], wo_s.rearrange("(k p) n -> p k n", p=128), [], R("WO"))
                YT3 = YT[:, :].rearrange("p (k s) -> p k s", k=8)
                wq = [0]
                for qb in range(NQB):
                    for s in range(4):
                        row0 = base + (qb * 4 + s) * 128
                        DMA("pool", X1[:, s, :], x_d[row0:row0 + 128, :], [], R("X1_%d" % s))
                    plists = []
                    for s in range(4):
                        tq = qb * 4 + s
                        xs = "X1_%d" % s
                        p = s % 2
                        wb = (0, 1) if p == 0 else (3, 4)
                        tbi = 2 if p == 0 else 7
                        Hs, Hst = (H, "H") if p == 0 else (HT[0], "HT0")
                        so = 0 if p == 0 else 4
                        P.capture()
                        for c2 in range(2):
                            for k in range(8):
                                MM(bank(wb[c2]), YT3[:, k, tq * 128:(tq + 1) * 128], WO[:, k, c2 * 512:(c2 + 1) * 512], k == 0, k == 7,
                                   R("WO"), R("b%d" % wb[c2]))
                            TTo("dve", X1[:, s, c2 * 512:(c2 + 1) * 512], X1[:, s, c2 * 512:(c2 + 1) * 512], bank(wb[c2]), ALU.add,
                                R(xs, "b%d" % wb[c2]), R(xs))
                        ACTV(Hs[:, :], X1[:, s, :], AF.Square, R(xs), R(Hst, "xss%d" % p), accum=SM[:, so:so + 1])
                        ACTV(SM[:, so + 1:so + 2], SM[:, so:so + 1], AF.Ln, R("xss%d" % p), R("xtmp%d" % p), scale=1.0 / D, bias=EPS)
                        ACTV(SM[:, so + 2:so + 3], SM[:, so + 1:so + 2], AF.Exp, R("xtmp%d" % p), R("xrs%d" % p), scale=-0.5)
                        TS(Hs[:, :], X1[:, s, :], SM[:, so + 2:so + 3], None, ALU.mult, None, R(xs, "xrs%d" % p), R(Hst))
                        bT = bankb(tbi)
                        for k in range(8):
                            TR(bT[:, k * 128:(k + 1) * 128], Hs[:, k * 128:(k + 1) * 128], R(Hst), R("b%d" % tbi))
                        CP("act", H2T[:, :, s * 128:(s + 1) * 128], bT[:, :].rearrange("p (k s) -> p k s", k=8), R("b%d" % tbi), R("H2T"))
                        plists.append(P.end_capture())
                    Lp = len(plists[0])
                    P.replay_skewed([(ops, si * Lp / 2.0, 1.0) for si, ops in enumerate(plists)])
                    for f in range(NF):
                        sl = f % 2
                        wsl_ = f % NWG
                        DMA("sp", WGU[wsl_][:, :, :, :], wgu_s[f], [], R("WGU%d" % wsl_))
                        for g in range(2):
                            bi = 3 + 2 * g + sl
                            for k in range(8):
                                MM(bank(bi), WGU[wsl_][:, g, k, :], H2T[:, k, :], k == 0, k == 7, R("WGU%d" % wsl_, "H2T"), R("b%d" % bi))
                        bG, bU = bank(3 + sl), bank(5 + sl)
                        ACTV(SG[0][:, :], bG, AF.Exp, R("b%d" % (3 + sl)), R("SG0"), scale=-1.0)
                        ACTV(SG[0][:, :], SG[0][:, :], AF.Ln, R("SG0"), R("SG0"), bias=1.0)
                        ACTV(SG[0][:, :], SG[0][:, :], AF.Exp, R("SG0"), R("SG0"), scale=-1.0)
                        TTo("dve", SG[1][:, :], bG, SG[0][:, :], ALU.mult, R("b%d" % (3 + sl), "SG0"), R("SG1"))
                        TTo("dve", AT[:, f, :], SG[1][:, :], bU, ALU.mult, R("SG1", "b%d" % (5 + sl)), R("AT"))
                    for q4 in range(4):
                        wsl = wq[0] % 3
                        wq[0] += 1
                        DMA("sp", WDQ[wsl][:, :, :], wd_s[:, q4 * 256:(q4 + 1) * 256].rearrange("(f p) n -> p f n", p=128), [],
                            R("WDQ%d" % wsl))
                        for s in range(4):
                            bi = s % 2
                            for f in range(NF):
                                MM(bank(bi)[:, 0:256], AT[:, f, s * 128:(s + 1) * 128], WDQ[wsl][:, f, :], f == 0, f == NF - 1,
                                   R("AT", "WDQ%d" % wsl), R("b%d" % bi))
                            TTo("dve", X1[:, s, q4 * 256:(q4 + 1) * 256], X1[:, s, q4 * 256:(q4 + 1) * 256], bank(bi)[:, 0:256], ALU.add,
                                R("X1_%d" % s, "b%d" % bi), R("X1_%d" % s))
                    for s in range(4):
                        row0 = base + (qb * 4 + s) * 128
                        DMA("pool", out_d[row0:row0 + 128, :], X1[:, s, :], R("X1_%d" % s), [])
                P.barrier()
        except _Stop:
            pass
        P.emit(st)
    return nc


def make_consts():
    cf = np.zeros((128, 308), np.float32)
    cf[:, 0:32] = (10000.0 ** (-np.arange(32, dtype=np.float32) / 32.0))[None, :]
    cf[:, 32:48] = (10000.0 ** (-np.arange(16, dtype=np.float32) / 16.0))[None, :]
    m = np.arange(128, dtype=np.float32)[:, None]
    c = np.arange(128, dtype=np.float32)[None, :]
    cf[:, 48:176] = np.maximum(c - m, 0.0)
    cf[:, 176:304] = np.maximum(m - c, 0.0)
    cf[:, 304] = 127.0 - m[:, 0]
    cf[:, 305] = m[:, 0]
    cf[:, 306] = m[:, 0] + 1.0
    cf[:, 307] = 128.0 - m[:, 0]
    ident = np.eye(128, dtype=np.float32).astype(ml_dtypes.bfloat16)
    return cf, ident


_CACHE = {}


def run(inputs, S, NSEQ, ncores):
    key = (S, NSEQ)
    if key not in _CACHE:
        _CACHE[key] = build(S, NSEQ)
    nc = _CACHE[key]
    cf, ident = make_consts()
    x = np.ascontiguousarray(np.asarray(inputs["x"], dtype=np.float32))
    pos = np.ascontiguousarray(np.asarray(inputs["positions"], dtype=np.int32))
    wnames = ["norm1_g", "w_in", "ret_decay_logit_fwd", "ret_decay_logit_bwd", "q_a_norm_g", "w_uq", "kv_a_norm_g",
              "w_ukv", "q_norm_g", "k_norm_g", "w_o", "norm2_g", "w_gate", "w_up", "w_down"]
    shared = {}
    for n in wnames:
        a = np.asarray(inputs[n], dtype=np.float32)
        shared[n] = np.ascontiguousarray(a[0])
    for n in ["norm1_g", "ret_decay_logit_fwd", "ret_decay_logit_bwd", "q_a_norm_g", "kv_a_norm_g", "q_norm_g", "k_norm_g", "norm2_g"]:
        shared[n] = shared[n].reshape(1, -1)
    shared["cf"] = cf
    shared["ident"] = ident
    in_maps = []
    for c in range(ncores):
        m = dict(shared)
        m["x"] = x[c * NSEQ:(c + 1) * NSEQ].reshape(NSEQ * S, D)
        m["positions"] = pos[c * NSEQ:(c + 1) * NSEQ].reshape(NSEQ * S)
        in_maps.append(m)
    res = run_bass_kernel_spmd(nc, in_maps, core_ids=list(range(ncores)))
    outs = [np.asarray(r["out"]).reshape(NSEQ, S, D) for r in res.results]
    return np.concatenate(outs, axis=0).astype(np.float32)


def kernel(**inputs):
    return run(inputs, 4096, 2, 8)
```

```python
import numpy as np
import concourse.bass as bass
import concourse.mybir as mybir

F32 = mybir.dt.float32
BF16 = mybir.dt.bfloat16
I32 = mybir.dt.int32
ALU = mybir.AluOpType
AF = mybir.ActivationFunctionType
AX = mybir.AxisListType


class Tok:
    __slots__ = ("w", "r", "name", "excl")

    def __init__(self, name="", excl=False):
        self.w = None
        self.r = {}
        self.name = name
        self.excl = excl


class Op:
    __slots__ = ("eng", "fn", "deps", "sig", "sigval", "sem", "dma", "idx")


class Prog:
    ENGS = ("pe", "act", "dve", "pool", "sp")

    def __init__(self, nc, ndma=8):
        self.nc = nc
        self.ops = {e: [] for e in self.ENGS}
        self.ndma = ndma
        self.dma_hist = {e: [] for e in self.ENGS}
        self.nops = 0
        self.caps = []
        self.sched_on = False

    def capture(self):
        self.caps.append([])

    def end_capture(self):
        return self.caps.pop()

    def schedule(self, ops, window=400):
        n = len(ops)
        preds = [None] * n
        lastw = {}
        readers = {}
        for i, (eng, fn, reads, writes, dma, cost) in enumerate(ops):
            ps = set()
            wl = list(writes) + [t for t in reads if t.excl]
            for t in reads:
                w = lastw.get(id(t))
                if w is not None:
                    ps.add(w)
            for t in wl:
                w = lastw.get(id(t))
                if w is not None:
                    ps.add(w)
                for r in readers.get(id(t), ()):
                    ps.add(r)
            for t in reads:
                if not t.excl:
                    readers.setdefault(id(t), []).append(i)
            for t in wl:
                lastw[id(t)] = i
                readers[id(t)] = []
            ps.discard(i)
            preds[i] = ps
        fin = [0.0] * n
        done = [False] * n
        efree = {}
        lastdma = {}
        order = []
        lo = 0
        LAT = 0.15
        while len(order) < n:
            while lo < n and done[lo]:
                lo += 1
            best, bkey = None, None
            seen_eng_block = set()
            hi = min(n, lo + window)
            for i in range(lo, hi):
                if done[i]:
                    continue
                eng = ops[i][0]
                ok = True
                rdy = 0.0
                for p in preds[i]:
                    if not done[p]:
                        ok = False
                        break
                    f = fin[p] + (LAT if ops[p][0] != eng or ops[p][4] else 0.0)
                    if f > rdy:
                        rdy = f
                if not ok:
                    continue
                start = max(rdy, efree.get(eng, 0.0))
                key = (start, i)
                if bkey is None or key < bkey:
                    best, bkey = i, key
            i = best
            eng, fn, reads, writes, dma, cost = ops[i]
            start = bkey[0]
            if dma:
                efree[eng] = start + 0.1
                fin[i] = start + cost
            else:
                efree[eng] = start + cost
                fin[i] = start + cost
            done[i] = True
            order.append(ops[i])
        return order

    def replay_skewed(self, streams):
        allops = []
        G = 1
        for si, (ops, t0, dt) in enumerate(streams):
            for i, o in enumerate(ops):
                allops.append((t0 + (i // G) * G * dt, si, i, o))
        allops.sort(key=lambda z: (z[0], z[1], z[2]))
        for _, _, _, o in allops:
            self.add(*o)

    def add(self, eng, fn, reads=(), writes=(), dma=False, cost=0.3):
        if self.caps:
            self.caps[-1].append((eng, fn, list(reads), list(writes), dma, cost))
            return None
        op = Op()
        op.eng = eng
        op.fn = fn
        op.dma = dma
        op.sig = dma
        op.sigval = 0
        op.sem = None
        op.idx = self.nops
        self.nops += 1
        deps = {}
        xr = [t for t in reads if t.excl]
        if xr:
            reads = [t for t in reads if not t.excl]
            writes = list(writes) + [t for t in xr if t not in writes]

        def need(d, war):
            if d is None:
                return
            if (not d.dma) and (not dma) and d.eng == eng:
                if eng == "pe" or war:
                    return
            deps[d.idx] = d

        for t in reads:
            need(t.w, False)
        for t in writes:
            need(t.w, False)
            for r in t.r.values():
                need(r, True)
        if dma:
            h = self.dma_hist[eng]
            k = len(h)
            op.sem = (eng, k % self.ndma)
            op.sigval = 16 * (k // self.ndma + 1)
            if k >= self.ndma:
                d = h[k - self.ndma]
                deps[d.idx] = d
            h.append(op)
        op.deps = list(deps.values())
        for d in op.deps:
            d.sig = True
        for t in reads:
            key = (eng, op.idx) if dma else eng
            t.r[key] = op
        for t in writes:
            t.w = op
            t.r = {}
        self.ops[eng].append(op)
        return op

    def barrier(self):
        if self.sched_on and len(self.caps) == 1:
            ops = self.caps.pop()
            for o in self.schedule(ops):
                self.add(*o)
            self._barrier()
            self.caps.append([])
            return
        self._barrier()

    def _barrier(self):
        lasts = []
        for e in self.ENGS:
            comp = [o for o in self.ops[e] if not o.dma and o.fn is not None]
            if comp:
                lasts.append(comp[-1])
            h = self.dma_hist[e]
            lasts.extend(h[-self.ndma:])
        for e in self.ENGS:
            op = Op()
            op.eng = e
            op.fn = None
            op.dma = False
            op.sig = False
            op.sigval = 0
            op.sem = None
            op.idx = self.nops
            self.nops += 1
            op.deps = [d for d in lasts if not (d.eng == e and not d.dma)]
            for d in op.deps:
                d.sig = True
            self.ops[e].append(op)

    def emit(self, stack):
        nc = self.nc
        sems = {}
        for e in self.ENGS:
            sems[e] = stack.enter_context(nc.semaphore("s_" + e))
            for k in range(self.ndma):
                sems[(e, k)] = stack.enter_context(nc.semaphore("d_%s%d" % (e, k)))
        for e in self.ENGS:
            c = 0
            for op in self.ops[e]:
                if op.dma or op.fn is None:
                    continue
                if op.sig:
                    c += 1
                    op.sigval = c
                    op.sem = e
        block = stack.enter_context(nc.Block())
        prog = self

        def run(engname, eng):
            waited = {}
            for op in prog.ops[engname]:
                for d in op.deps:
                    if waited.get(d.sem, 0) >= d.sigval:
                        continue
                    eng.wait_ge(sems[d.sem], d.sigval)
                    waited[d.sem] = d.sigval
                if op.fn is None:
                    continue
                ins = op.fn(eng)
                if op.sig:
                    ins.then_inc(sems[op.sem], 16 if op.dma else 1)
            for op in prog.dma_hist[engname][-prog.ndma:]:
                if waited.get(op.sem, 0) < op.sigval:
                    eng.wait_ge(sems[op.sem], op.sigval)
                    waited[op.sem] = op.sigval

        @block.tensor
        def _(e):
            run("pe", e)

        @block.scalar
        def _(e):
            run("act", e)

        @block.vector
        def _(e):
            run("dve", e)

        @block.gpsimd
        def _(e):
            run("pool", e)

        @block.sync
        def _(e):
            run("sp", e)


import ml_dtypes
from contextlib import ExitStack
from concourse.bass_utils import run_bass_kernel_spmd

D = 1024
DFF = 2816
NF = DFF // 128
INW = 2464
EPS = 1e-6
GN_EPS = 1e-5
TWO_PI = 6.283185307179586
C1 = 6.28125
C2 = TWO_PI - C1
MAGIC = 12582912.0
PI_SAFE = 3.1415925
ARENA = 60416
SCHED = True


class _Stop(Exception):
    pass


def build(S, NSEQ, stop=None):
    NT = S // 128
    NQB = S // 512
    nc = bass.Bass("TRN2", target_bir_lowering=False)

    def din(name, shape, dt=F32):
        return nc.dram_tensor(name, shape, dt, kind="ExternalInput").ap()

    x_d = din("x", [NSEQ * S, D])
    pos_d = din("positions", [NSEQ * S], I32)
    n1_d = din("norm1_g", [1, D])
    win_d = din("w_in", [D, INW])
    lf_d = din("ret_decay_logit_fwd", [1, 8])
    lb_d = din("ret_decay_logit_bwd", [1, 8])
    gqa_d = din("q_a_norm_g", [1, 256])
    wuq_d = din("w_uq", [256, 768])
    gkva_d = din("kv_a_norm_g", [1, 128])
    wukv_d = din("w_ukv", [128, 1024])
    gq_d = din("q_norm_g", [1, 96])
    gk_d = din("k_norm_g", [1, 96])
    wo_d = din("w_o", [D, D])
    n2_d = din("norm2_g", [1, D])
    wg_d = din("w_gate", [D, DFF])
    wu_d = din("w_up", [D, DFF])
    wd_d = din("w_down", [DFF, D])
    cf_d = din("cf", [128, 308])
    id_d = din("ident", [128, 128], BF16)
    out_d = nc.dram_tensor("out", [NSEQ * S, D], F32, kind="ExternalOutput").ap()
    win_s = nc.dram_tensor("win_s", [D, INW], BF16).ap()
    wuq_s = nc.dram_tensor("wuq_s", [256, 768], BF16).ap()
    wukv_s = nc.dram_tensor("wukv_s", [128, 1024], BF16).ap()
    wo_s = nc.dram_tensor("wo_s", [D, D], BF16).ap()
    wgu_s = nc.dram_tensor("wgu_s", [NF, 128, 2, 8, 128], BF16).ap()
    wd_s = nc.dram_tensor("wd_s", [DFF, D], BF16).ap()

    st = ExitStack()
    with st:
        def sb(name, shape, dt):
            return st.enter_context(nc.sbuf_tensor(name, shape, dt))

        AR = sb("arena", [128, ARENA], BF16)
        YT = sb("YT", [128, 8 * S], BF16)
        XT = [sb("XT%d" % i, [128, D], F32) for i in range(2)]
        H = sb("H", [128, D], BF16)
        HT = [sb("HT%d" % i, [128, D], BF16) for i in range(2)]
        CF = sb("CF", [128, 308], F32)
        ID = sb("ID", [128, 128], BF16)
        G1 = sb("G1", [128, 8], F32)
        G2 = sb("G2", [128, 8], F32)
        GQA = sb("GQA", [128, 2], F32)
        GKVA = sb("GKVA", [128, 1], F32)
        GQS = sb("GQS", [128, 96], F32)
        GK = sb("GK", [128, 96], F32)
        LGT = sb("LGT", [128, 16], F32)
        LG = sb("LG", [128, 16], F32)
        ZT = sb("ZT", [128, 48], F32)
        ZX = sb("ZX", [128, 48], F32)
        XFB = sb("XFB", [128, 16], F32)
        STT_ = sb("STATE", [128, 512], F32)
        POSI = sb("POSI", [128, NT], I32)
        POSF = sb("POSF", [128, NT], F32)
        SM = sb("SM", [128, 64], F32)
        SSR = sb("SSR", [128, NT], F32)
        QT2 = sb("QT2", [128, 2048], BF16)
        QT2v = QT2[:, :].rearrange("p (h s) -> p h s", h=4)
        PS = [st.enter_context(nc.psum_tensor("ps%d" % i, [128, 1024], F32)) for i in range(4)]

        def bank(i):
            return PS[i // 2][:, (i % 2) * 512:(i % 2 + 1) * 512]

        def bankb(i):
            return bank(i).bitcast(BF16)

        def arb(off, n):
            return AR[:, off:off + n]

        def arf(off, n):
            return AR[:, off:off + 2 * n].bitcast(F32)

        P = Prog(nc)
        toks = {}

        def T(name):
            t = toks.get(name)
            if t is None:
                excl = (name[0] == "b" and name[1:].isdigit()) or name in ("S", "S0", "S1", "bO0", "bO1")
                t = toks[name] = Tok(name, excl)
            return t

        def R(*names):
            return [T(n) for n in names]

        def fsz(ap):
            n = 1
            for d in ap.shape[1:]:
                n *= d
            return n

        def MM(out, lhsT, rhs, start, stop, r, w, **kw):
            return P.add("pe", lambda e: e.matmul(out, lhsT=lhsT, rhs=rhs, start=start, stop=stop, **kw), r, w,
                         cost=0.05 + max(64, fsz(rhs)) / 2400.0)

        def TR(out, in_, r, w):
            return P.add("pe", lambda e: e.transpose(out=out, in_=in_, identity=ID[:, :]), r + [T("ID")], w, cost=0.12)

        def TTo(eng, out, in0, in1, op, r, w):
            return P.add(eng, lambda e: e.tensor_tensor(out=out, in0=in0, in1=in1, op=op), r, w,
                         cost=(0.15 + fsz(out) / 960.0) if eng == "dve" else (0.3 + fsz(out) / 500.0))

        def TS(out, in0, s1, s2, op0, op1, r, w):
            if s2 is None:
                return P.add("dve", lambda e: e.tensor_scalar(out=out, in0=in0, scalar1=s1, scalar2=None, op0=op0), r, w,
                             cost=0.15 + fsz(out) / 1400.0)
            return P.add("dve", lambda e: e.tensor_scalar(out=out, in0=in0, scalar1=s1, scalar2=s2, op0=op0, op1=op1), r, w,
                         cost=0.15 + fsz(out) / 1400.0)

        def STT(out, in0, scalar, in1, op0, op1, r, w):
            return P.add("dve", lambda e: e.scalar_tensor_tensor(out=out, in0=in0, scalar=scalar, in1=in1, op0=op0, op1=op1), r, w,
                         cost=0.15 + fsz(out) / 960.0)

        def ACTV(out, in_, func, r, w, scale=1.0, bias=0.0, accum=None):
            if accum is None:
                return P.add("act", lambda e: e.activation(out=out, in_=in_, func=func, bias=bias, scale=scale), r, w,
                             cost=0.2 + fsz(in_) / 1200.0)
            return P.add("act", lambda e: e.activation(out=out, in_=in_, func=func, bias=bias, scale=scale, accum_out=accum), r, w,
                         cost=0.3 + fsz(in_) / 1200.0)

        def CP(eng, out, in_, r, w):
            if eng == "act":
                return P.add("act", lambda e: e.copy(out=out, in_=in_), r, w, cost=0.2 + fsz(in_) / 1200.0)
            return P.add(eng, lambda e: e.tensor_copy(out=out, in_=in_), r, w,
                         cost=(0.15 + fsz(out) / 1400.0) if eng == "dve" else (0.3 + fsz(out) / 500.0))

        def RED(out, in_, r, w):
            return P.add("dve", lambda e: e.reduce_sum(out=out, in_=in_, axis=AX.X), r, w, cost=0.15 + fsz(in_) / 960.0)

        def RECIP(out, in_, r, w):
            return P.add("dve", lambda e: e.reciprocal(out=out, in_=in_), r, w, cost=0.15 + fsz(in_) / 960.0)

        def MEMSET(eng, ap, val, w):
            return P.add(eng, lambda e: e.memset(ap, val), [], w)

        def DMA(q, out, in_, r, w, slow=False):
            if slow:
                return P.add(q, lambda e: e.dma_start(out=out, in_=in_, allow_slow_non_contiguous=True), r, w, dma=True, cost=3.0)
            return P.add(q, lambda e: e.dma_start(out=out, in_=in_), r, w, dma=True, cost=2.5 + fsz(out) / 1500.0)

        def rsqrt_chain(dst, src, n_inv, eps, r, w, tmp):
            ACTV(tmp, src, AF.Ln, r, [T("rs_tmp")], scale=n_inv, bias=eps)
            ACTV(dst, tmp, AF.Exp, [T("rs_tmp")], w, scale=-0.5)

        DMA("sp", CF[:, :], cf_d[:, :], [], R("CF"))
        DMA("sp", ID[:, :], id_d[:, :], [], R("ID"))
        DMA("sp", G1[:, :], n1_d.rearrange("o (k p) -> p (o k)", p=128), [], R("G1"), slow=True)
        DMA("sp", G2[:, :], n2_d.rearrange("o (k p) -> p (o k)", p=128), [], R("G2"), slow=True)
        DMA("sp", GQA[:, :], gqa_d.rearrange("o (k p) -> p (o k)", p=128), [], R("GQA"), slow=True)
        DMA("sp", GKVA[:, :], gkva_d.rearrange("o (k p) -> p (o k)", p=128), [], R("GKVA"), slow=True)
        DMA("sp", GQS[:, :], gq_d.to_broadcast([128, 96]), [], R("GQS"), slow=True)
        DMA("sp", GK[:, :], gk_d.to_broadcast([128, 96]), [], R("GK"), slow=True)
        DMA("sp", LGT[:, 0:8], lf_d.to_broadcast([128, 8]), [], R("LGT"), slow=True)
        DMA("sp", LGT[:, 8:16], lb_d.to_broadcast([128, 8]), R("LGT"), R("LGT"), slow=True)
        TS(GQS[:, :], GQS[:, :], 96.0 ** -0.5, None, ALU.mult, None, R("GQS"), R("GQS"))
        ACTV(LG[:, :], LGT[:, :], AF.Exp, R("LGT"), R("LG"), scale=-1.0)
        ACTV(LG[:, :], LG[:, :], AF.Ln, R("LG"), R("LG"), bias=1.0)
        TS(LG[:, :], LG[:, :], -1.0, None, ALU.mult, None, R("LG"), R("LG"))
        IDX = CF[:, 304:308]
        for j, (lo, ic) in enumerate([(0, 0), (8, 1), (0, 2), (8, 3)]):
            TS(ZX[:, j * 8:(j + 1) * 8], LG[:, lo:lo + 8], IDX[:, ic:ic + 1], None, ALU.mult, None, R("LG", "CF"), R("ZX"))
        TS(ZX[:, 32:48], LG[:, 0:16], 128.0, None, ALU.mult, None, R("LG"), R("ZX"))
        ACTV(ZT[:, :], ZX[:, :], AF.Exp, R("ZX"), R("ZT"))
        XFBv = XFB[:, :].rearrange("p (h d) -> p h d", d=2)
        CP("dve", XFBv[:, :, 0], ZT[:, 16:24], R("ZT"), R("XFB"))
        CP("dve", XFBv[:, :, 1], ZT[:, 24:32], R("ZT", "XFB"), R("XFB"))
        ZF, ZB, GCF, GCB = ZT[:, 0:8], ZT[:, 8:16], ZT[:, 32:40], ZT[:, 40:48]

        SF = [arf(i * 5632, 2816) for i in range(2)]
        SBF = [arb(11264 + i * 2816, 2816) for i in range(2)]
        cnt = [0]

        def prep(src, K, N, gain, gtok, dst_fn):
            for kc in range(K // 128):
                s = cnt[0] % 2
                cnt[0] += 1
                DMA("sp", SF[s][:, 0:N], src[kc * 128:(kc + 1) * 128, :], [], R("SF%d" % s))
                if gain is not None:
                    TS(SBF[s][:, 0:N], SF[s][:, 0:N], gain[:, kc:kc + 1], None, ALU.mult, None,
                       R("SF%d" % s, gtok), R("SB%d" % s))
                else:
                    CP("act", SBF[s][:, 0:N], SF[s][:, 0:N], R("SF%d" % s), R("SB%d" % s))
                dst_fn(kc, SBF[s], R("SB%d" % s))

        prep(win_d, D, INW, G1, "G1", lambda kc, t, r: DMA("pool", win_s[kc * 128:(kc + 1) * 128, :], t[:, 0:INW], r, []))
        prep(wuq_d, 256, 768, GQA, "GQA", lambda kc, t, r: DMA("pool", wuq_s[kc * 128:(kc + 1) * 128, :], t[:, 0:768], r, []))
        prep(wukv_d, 128, 1024, GKVA, "GKVA", lambda kc, t, r: DMA("pool", wukv_s[:, :], t[:, 0:1024], r, []))
        prep(wo_d, D, D, None, None, lambda kc, t, r: DMA("pool", wo_s[kc * 128:(kc + 1) * 128, :], t[:, 0:D], r, []))
        for gi, wsrc in enumerate((wg_d, wu_d)):
            prep(wsrc, D, DFF, G2, "G2",
                 lambda kc, t, r, gi=gi: DMA("pool", wgu_s[:, :, gi, kc, :].rearrange("f p n -> p f n"),
                                             t[:, 0:DFF].rearrange("p (f n) -> p f n", n=128), r, []))
        prep(wd_d, DFF, D, None, None, lambda kc, t, r: DMA("pool", wd_s[kc * 128:(kc + 1) * 128, :], t[:, 0:D], r, []))
        P.barrier()
        if SCHED:
            P.sched_on = True
            P.capture()

        xc = [0]

        def xnorm(row0, xdst=None, xtok=None, ht_eng="act"):
            s = xc[0] % 2
            xc[0] += 1
            if xdst is None:
                xdst, xtok = XT[s][:, :], "XT%d" % s
                DMA("sp", xdst, x_d[row0:row0 + 128, :], [], R(xtok))
            ACTV(H[:, :], xdst, AF.Square, R(xtok), R("H", "xss"), accum=SM[:, 0:1])
            rsqrt_chain(SM[:, 2:3], SM[:, 0:1], 1.0 / D, EPS, R("xss"), R("xrs"), SM[:, 1:2])
            TS(H[:, :], xdst, SM[:, 2:3], None, ALU.mult, None, R(xtok, "xrs"), R("H"))
            bT = bankb(0)
            for k in range(8):
                TR(bT[:, k * 128:(k + 1) * 128], H[:, k * 128:(k + 1) * 128], R("H"), R("b0"))
            CP(ht_eng, HT[s][:, :], bT[:, :], R("b0"), R("HT%d" % s))
            return HT[s], "HT%d" % s

        def tile_loop(order, row_of, main):
            P.capture()
            nxt = xnorm(row_of(order[0]))
            P.replay_skewed([(P.end_capture(), 0.0, 1.0)])
            for i, t in enumerate(order):
                cur = nxt
                P.capture()
                main(t, cur[0], cur[1])
                mops = P.end_capture()
                streams = [(mops, 0.0, 1.0)]
                if i + 1 < len(order):
                    P.capture()
                    nxt = xnorm(row_of(order[i + 1]))
                    xops = P.end_capture()
                    streams.append((xops, 0.3 * len(mops), 0.6 * len(mops) / max(1, len(xops))))
                P.replay_skewed(streams)

        def tile_loop2(order, row_of, main, nproj):
            n = len(order)
            xl, hts = [], []
            for i in range(n):
                P.capture()
                hts.append(xnorm(row_of(order[i])))
                xl.append(P.end_capture())
            ml = []
            for i, t in enumerate(order):
                P.capture()
                main(t, i % 2, hts[i][0], hts[i][1])
                ml.append(P.end_capture())
            Lm = float(len(ml[0]))
            assert nproj < 0.5 * Lm, (nproj, Lm)
            Lx = len(xl[0])
            streams = []
            for i in range(n):
                streams.append((ml[i], i * Lm / 2.0, 1.0))
                streams.append((xl[i], (i - 1) * Lm / 2.0 + 0.05 * Lm, 0.4 * Lm / Lx))
            P.replay_skewed(streams)

        def proj(bi, ht, httok, W, wtok, c0, n):
            for k in range(8):
                MM(bank(bi)[:, 0:n], ht[:, k * 128:(k + 1) * 128], W[:, k, c0:c0 + n], k == 0, k == 7,
                   R(httok, wtok), R("b%d" % bi))

        def rotary(src, stok, dst, dtok, cos, sin, tabtok, nh, half, tA, tB, tAtok, tBtok):
            cb = cos.unsqueeze(1).unsqueeze(1).to_broadcast([128, nh, 2, half])
            sbh = sin.unsqueeze(1).to_broadcast([128, nh, half])
            TTo("dve", tA, src, cb, ALU.mult, R(stok, tabtok, tAtok), R(tAtok))
            TTo("pool", tB[:, :, 0, :], src[:, :, 1, :], sbh, ALU.mult, R(stok, tabtok, tBtok), R(tBtok))
            TTo("pool", tB[:, :, 1, :], src[:, :, 0, :], sbh, ALU.mult, R(stok, tabtok, tBtok), R(tBtok))
            TTo("dve", dst[:, :, 0, :], tA[:, :, 0, :], tB[:, :, 0, :], ALU.subtract, R(tAtok, tBtok, dtok), R(dtok))
            TTo("dve", dst[:, :, 1, :], tA[:, :, 1, :], tB[:, :, 1, :], ALU.add, R(tAtok, tBtok, dtok), R(dtok))

        def trig_tables(n, invf, cosd, sind, tmp, tabtok):
            ang, u, r_ = tmp
            TTo("dve", ang.rearrange("p (t f) -> p t f", f=n), POSF[:, :].unsqueeze(2).to_broadcast([128, NT, n]),
                invf.unsqueeze(1).to_broadcast([128, NT, n]), ALU.mult, R("POSF", "CF"), R("ang"))
            for shift, dst in ((0.0, sind), (np.pi / 2, cosd)):
                if shift:
                    TS(ang, ang, shift, None, ALU.add, None, R("ang"), R("ang"))
                TS(u, ang, 1.0 / TWO_PI, MAGIC, ALU.mult, ALU.add, R("ang"), R("trU"))
                TS(u, u, -MAGIC, None, ALU.add, None, R("trU"), R("trU"))
                STT(r_, u, -C1, ang, ALU.mult, ALU.add, R("trU", "ang"), R("trR"))
                STT(r_, u, -C2, r_, ALU.mult, ALU.add, R("trU", "trR"), R("trR"))
                TS(r_, r_, -PI_SAFE, PI_SAFE, ALU.max, ALU.min, R("trR"), R("trR"))
                ACTV(dst, r_, AF.Sin, R("trR"), R(tabtok))

        def checkpoint(k):
            if stop is not None and k == stop:
                raise _Stop()

        try:
            for sq in range(NSEQ):
                base = sq * S
                checkpoint(0)
                DMA("sp", POSI[:, :], pos_d[base:base + S].rearrange("(t p) -> p t", p=128), [], R("POSI"), slow=True)
                CP("dve", POSF[:, :], POSI[:, :], R("POSI"), R("POSF"))

                checkpoint(1)
                WR = arb(0, 12288).rearrange("p (k n) -> p k n", k=8)
                WR1 = arb(0, 8192).rearrange("p (k n) -> p k n", k=8)
                KTOK = arb(12288, NT * 512).rearrange("p (t n) -> p t n", n=512)
                RBC = arb(28672, (NT // 2 + 1) * 512).rearrange("p (t n) -> p t n", n=512)
                RJ = arb(37888, 1024).rearrange("p (t n) -> p t n", n=512)
                DTm = arf(38912, 1024)
                COSR = arf(40960, NT * 32)
                SINR = arf(40960 + NT * 64, NT * 32)
                o = 40960 + NT * 128
                F = [arf(o + i * 1024, 512) for i in range(4)]
                o += 4096
                B = [arb(o + i * 1024, 1024) for i in range(6)]
                o += 6144
                assert o <= ARENA
                tmp3 = [arf(12288 + i * (NT * 64), NT * 32) for i in range(3)]
                trig_tables(32, CF[:, 0:32], COSR, SINR, tmp3, "TABR")
                for h in range(8):
                    TS(F[0][:, 0:128], CF[:, 48:176], LG[:, h:h + 1], None, ALU.mult, None, R("CF", "LG"), R("F0"))
                    STT(F[1][:, 0:128], CF[:, 176:304], LG[:, 8 + h:9 + h], F[0][:, 0:128], ALU.mult, ALU.add,
                        R("CF", "LG", "F0"), R("F1"))
                    ACTV(DTm[:, h * 128:(h + 1) * 128], F[1][:, 0:128], AF.Exp, R("F1"), R("DT"))
                MEMSET("dve", STT_[:, :], 0.0, R("ST"))
                P.barrier()

                def v4(ap, nh, half):
                    return ap.rearrange("p (h d f) -> p h d f", h=nh, d=2)

                def v3(ap, a):
                    return ap.rearrange("p (a b) -> p a b", a=a)

                checkpoint(2)
                DMA("sp", WR1[:, :, :], win_s[:, 512:1536].rearrange("(k p) n -> p k n", p=128), [], R("WR"))
                F2 = [arf(o + i * 1024, 512) for i in range(4)]
                assert o + 4096 <= ARENA

                def sweep1_main(t, p, ht, htt):
                    bk, bv, bkv = (1, 2, 3) if p == 0 else (4, 5, 6)
                    Fp = F if p == 0 else F2
                    fn = (lambda i: "F%d" % i) if p == 0 else (lambda i: "G%d" % i)
                    Bp, Bpt = (B[0], "B0") if p == 0 else (B[1], "B1")
                    proj(bk, ht, htt, WR1, "WR", 0, 512)
                    proj(bv, ht, htt, WR1, "WR", 512, 512)
                    ACTV(Fp[0][:, :], bank(bk), AF.Copy, R("b%d" % bk), R(fn(0)), scale=0.125)
                    rotary(v4(Fp[0][:, :], 8, 32), fn(0), v4(KTOK[:, t, :], 8, 32), "KTOK%d" % t,
                           COSR[:, t * 32:(t + 1) * 32], SINR[:, t * 32:(t + 1) * 32], "TABR", 8, 32,
                           v4(Fp[1][:, :], 8, 32), v4(Fp[2][:, :], 8, 32), fn(1), fn(2))
                    TTo("dve", v3(Bp[:, 0:512], 8), v3(bank(bv), 8), ZB.unsqueeze(2).to_broadcast([128, 8, 64]), ALU.mult,
                        R("b%d" % bv, "ZT"), R(Bpt))
                    for h in range(8):
                        MM(bank(bkv)[64:128, h * 64:(h + 1) * 64], KTOK[:, t, h * 64:(h + 1) * 64], Bp[:, h * 64:(h + 1) * 64],
                           True, True, R("KTOK%d" % t, Bpt), R("b%d" % bkv), tile_position=(0, 64))
                    pr = (t % 2) * 64
                    CP("act", RBC[pr:pr + 64, t // 2, :], STT_[64:128, :], R("ST"), R("RBC%d" % t))
                    TTo("dve", v3(Fp[3][64:128, :], 8), v3(STT_[64:128, :], 8), GCB[64:128, :].unsqueeze(2).to_broadcast([64, 8, 64]),
                        ALU.mult, R("ST", "ZT"), R(fn(3)))
                    TTo("dve", STT_[64:128, :], Fp[3][64:128, :], bank(bkv)[64:128, :], ALU.add, R(fn(3), "b%d" % bkv), R("ST"))

                tile_loop2(list(range(NT - 1, -1, -1)), lambda t: base + t * 128, sweep1_main, 16)
                P.barrier()

                checkpoint(3)
                DMA("sp", WR[:, :, 0:512], win_s[:, 0:512].rearrange("(k p) n -> p k n", p=128), [], R("WR"))
                DMA("sp", WR[:, :, 512:1024], win_s[:, 1536:2048].rearrange("(k p) n -> p k n", p=128), R("WR"), R("WR"))
                DMA("sp", WR[:, :, 1024:1536], win_s[:, 1024:1536].rearrange("(k p) n -> p k n", p=128), R("WR"), R("WR"))
                MEMSET("dve", RJ[0:64, 0, :], 0.0, R("RJ0"))
                CP("dve", RJ[64:128, 0, :], RBC[0:64, 0, :], R("RBC0", "RJ0"), R("RJ0"))
                def sweep2_main(t, ht, htt):
                    proj(1, ht, htt, WR, "WR", 0, 512)
                    proj(2, ht, htt, WR, "WR", 512, 512)
                    proj(3, ht, htt, WR, "WR", 1024, 512)
                    ACTV(F[0][:, :], bank(1), AF.Copy, R("b1"), R("F0"))
                    rotary(v4(F[0][:, :], 8, 32), "F0", v4(F[3][:, :], 8, 32), "F3",
                           COSR[:, t * 32:(t + 1) * 32], SINR[:, t * 32:(t + 1) * 32], "TABR", 8, 32,
                           v4(F[1][:, :], 8, 32), v4(F[2][:, :], 8, 32), "F1", "F2")
                    CP("dve", B[0][:, 0:512], F[3][:, :], R("F3"), R("B0"))
                    TTo("dve", B[1][:, :].rearrange("p (h d e) -> p h d e", h=8, d=2),
                        v3(F[3][:, :], 8).unsqueeze(2).to_broadcast([128, 8, 2, 64]),
                        XFB[:, :].rearrange("p (h d) -> p h d", d=2).unsqueeze(3).to_broadcast([128, 8, 2, 64]),
                        ALU.mult, R("F3", "XFB"), R("B1"))
                    CP("act", B[2][:, 0:512], bank(3), R("b3"), R("B2a"))
                    TTo("dve", v3(B[2][:, 512:1024], 8), v3(bank(3), 8), ZF.unsqueeze(2).to_broadcast([128, 8, 64]), ALU.mult,
                        R("b3", "ZT"), R("B2b"))
                    b6 = bankb(6)
                    for j in range(4):
                        TR(b6[:, j * 128:(j + 1) * 128], B[0][:, j * 128:(j + 1) * 128], R("B0"), R("b6"))
                    for j in range(4):
                        TR(b6[:, 512 + j * 128:512 + (j + 1) * 128], KTOK[:, t, j * 128:(j + 1) * 128], R("KTOK%d" % t), R("b6"))
                    CP("act", B[3][:, :], b6[:, :], R("b6"), R("B3"))
                    for h in range(8):
                        TR(b6[:, h * 128:(h + 1) * 128], B[1][:, h * 128:(h + 1) * 128], R("B1"), R("b6"))
                    CP("dve", B[4][:, :], b6[:, :], R("b6"), R("B4"))
                    for h in range(8):
                        j, hp = h // 2, (h % 2) * 64
                        MM(PS[2][:, (h % 2) * 512 + j * 128:(h % 2) * 512 + (j + 1) * 128],
                           B[3][hp:hp + 64, 512 + j * 128:512 + (j + 1) * 128],
                           B[3][hp:hp + 64, j * 128:(j + 1) * 128], True, True, R("B3"), R("S"))
                    for c2 in range(2):
                        TTo("dve", B[5][:, :].rearrange("p (j q c) -> p j q c", j=4, q=2)[:, :, c2, :],
                            PS[2][:, c2 * 512:(c2 + 1) * 512].rearrange("p (j c) -> p j c", j=4),
                            DTm[:, :].rearrange("p (j q c) -> p j q c", j=4, q=2)[:, :, c2, :], ALU.mult, R("S", "DT", "B5"), R("B5"))
                    for h in range(8):
                        MM(bank(7)[0:64, h * 64:(h + 1) * 64], KTOK[:, t, h * 64:(h + 1) * 64], B[2][:, 512 + h * 64:512 + (h + 1) * 64],
                           True, True, R("KTOK%d" % t, "B2b"), R("b7"))
                    rj = t % 2
                    for h in range(8):
                        MM(bank(3)[:, h * 64:(h + 1) * 64], B[5][:, h * 128:(h + 1) * 128], B[2][:, h * 64:(h + 1) * 64],
                           True, False, R("B5", "B2a", "B2b"), R("b3"))
                        MM(bank(3)[:, h * 64:(h + 1) * 64], B[4][:, h * 128:(h + 1) * 128], RJ[:, rj, h * 64:(h + 1) * 64],
                           False, True, R("B4", "RJ%d" % rj), R("b3"))
                    if t + 1 < NT:
                        nrj = (t + 1) % 2
                        TTo("dve", v3(F[0][0:64, :], 8), v3(STT_[0:64, :], 8), GCF[0:64, :].unsqueeze(2).to_broadcast([64, 8, 64]),
                            ALU.mult, R("ST", "ZT", "F0"), R("F0"))
                        TTo("dve", STT_[0:64, :], F[0][0:64, :], bank(7)[0:64, :], ALU.add, R("F0", "b7"), R("ST"))
                        CP("act", RJ[0:64, nrj, :], STT_[0:64, :], R("ST"), R("RJ%d" % nrj))
                        pr = ((t + 1) % 2) * 64
                        CP("dve", RJ[64:128, nrj, :], RBC[pr:pr + 64, (t + 1) // 2, :], R("RBC%d" % (t + 1), "RJ%d" % nrj), R("RJ%d" % nrj))
                    ACTV(F[0][:, :], bank(3), AF.Copy, R("b3"), R("F0"))
                    RED(SM[:, 8:16], v3(F[0][:, :], 8), R("F0"), R("mu"))
                    ACTV(F[1][:, :], F[0][:, :], AF.Square, R("F0"), R("F1"))
                    RED(SM[:, 16:24], v3(F[1][:, :], 8), R("F1"), R("m2"))
                    TS(SM[:, 8:16], SM[:, 8:16], 1.0 / 64, None, ALU.mult, None, R("mu"), R("mu"))
                    TTo("dve", SM[:, 24:32], SM[:, 8:16], SM[:, 8:16], ALU.mult, R("mu"), R("msq"))
                    STT(SM[:, 16:24], SM[:, 16:24], 1.0 / 64, SM[:, 24:32], ALU.mult, ALU.subtract, R("m2", "msq"), R("m2"))
                    rsqrt_chain(SM[:, 32:40], SM[:, 16:24], 1.0, GN_EPS, R("m2"), R("grs"), SM[:, 40:48])
                    TTo("dve", v3(F[2][:, :], 8), v3(F[0][:, :], 8), SM[:, 8:16].unsqueeze(2).to_broadcast([128, 8, 64]), ALU.subtract,
                        R("F0", "mu"), R("F2"))
                    TTo("dve", v3(F[2][:, :], 8), v3(F[2][:, :], 8), SM[:, 32:40].unsqueeze(2).to_broadcast([128, 8, 64]), ALU.mult,
                        R("F2", "grs"), R("F2"))
                    ACTV(F[1][:, :], bank(2), AF.Exp, R("b2", "F1"), R("F1"), scale=-1.0)
                    ACTV(F[1][:, :], F[1][:, :], AF.Ln, R("F1"), R("F1"), bias=1.0)
                    ACTV(F[1][:, :], F[1][:, :], AF.Exp, R("F1"), R("F1"), scale=-1.0)
                    TTo("dve", F[3][:, :], bank(2), F[1][:, :], ALU.mult, R("b2", "F1", "F3"), R("F3"))
                    TTo("dve", B[0][:, 0:512], F[2][:, :], F[3][:, :], ALU.mult, R("F2", "F3", "B0"), R("B0"))
                    bT = bankb(6)
                    for j in range(4):
                        TR(bT[:, j * 128:(j + 1) * 128], B[0][:, j * 128:(j + 1) * 128], R("B0"), R("b6"))
                    CP("act", YT[:, :].rearrange("p (k s) -> p k s", k=8)[:, 0:4, t * 128:(t + 1) * 128],
                       bT[:, 0:512].rearrange("p (k s) -> p k s", k=4), R("b6"), R("YTr%d" % t))

                tile_loop(list(range(NT)), lambda t: base + t * 128, sweep2_main)
                P.barrier()

                checkpoint(4)
                CQT = arb(0, 2 * S).rearrange("p (k s) -> p k s", k=2)
                CKVT = arb(8192, S)
                KRB = arf(12288, NT * 32)
                KT = arb(14336, 4 * S).rearrange("p (h s) -> p h s", h=4)
                VA = arb(30720, NT * 512).rearrange("p (t h c) -> p t h c", h=4, c=128)
                WM = arb(47104, 3328).rearrange("p (k n) -> p k n", k=8)
                QT = arb(47104, 2048).rearrange("p (h s) -> p h s", h=4)
                RL = arf(47104 + 2048, 512)
                WUQ = arb(50432, 1536).rearrange("p (k n) -> p k n", k=2)
                WUKV = arb(51968, 1024)
                PTA = [arb(52992 + i * 1024, 1024) for i in range(3)]
                COSM = arf(56064, NT * 16)
                SINM = arf(57088, NT * 16)
                MF = [arf(58112 + i * 1024, 512) for i in range(2)]
                tmpm = [arf(52992 + i * 1024, NT * 16) for i in range(3)]
                DMA("sp", WM[:, :, :], win_s[:, 2048:2464].rearrange("(k p) n -> p k n", p=128), [], R("WM"))
                DMA("sp", WUQ[:, :, :], wuq_s.rearrange("(k p) n -> p k n", p=128), [], R("WUQ"))
                DMA("sp", WUKV[:, :], wukv_s[:, :], [], R("WUKV"))
                trig_tables(16, CF[:, 32:48], COSM, SINM, tmpm, "TABM")
                P.barrier()
                MFb = [arf(52992 + i * 1024, 512) for i in range(2)]

                def mla_main(t, p, ht, htt):
                    bm, bt2 = (1, 2) if p == 0 else (3, 4)
                    M0, M1 = (MF[0], MF[1]) if p == 0 else (MFb[0], MFb[1])
                    so = 8 if p == 0 else 40
                    tn = lambda n, p=p: "%s_%d" % (n, p)
                    proj(bm, ht, htt, WM, "WM", 0, 416)
                    b1 = bank(bm)
                    b1t = "b%d" % bm
                    ACTV(M0[:, 0:256], b1[:, 0:256], AF.Square, R(b1t), R(tn("MF0"), tn("ssq")), accum=SM[:, so:so + 1])
                    ACTV(SM[:, so + 1:so + 2], SM[:, so:so + 1], AF.Ln, R(tn("ssq")), R(tn("t1")), scale=1.0 / 256, bias=EPS)
                    ACTV(SM[:, so + 2:so + 3], SM[:, so + 1:so + 2], AF.Exp, R(tn("t1")), R(tn("rq")), scale=-0.5)
                    B_cq = M1[:, 0:128].bitcast(BF16)
                    TS(B_cq, b1[:, 0:256], SM[:, so + 2:so + 3], None, ALU.mult, None, R(b1t, tn("rq")), R(tn("cq")))
                    ACTV(M0[:, 256:384], b1[:, 256:384], AF.Square, R(b1t), R(tn("MF0b"), tn("sskv")), accum=SM[:, so + 3:so + 4])
                    ACTV(SM[:, so + 4:so + 5], SM[:, so + 3:so + 4], AF.Ln, R(tn("sskv")), R(tn("t2")), scale=1.0 / 128, bias=EPS)
                    ACTV(SM[:, so + 5:so + 6], SM[:, so + 4:so + 5], AF.Exp, R(tn("t2")), R(tn("rkv")), scale=-0.5)
                    B_ckv = M1[:, 128:192].bitcast(BF16)
                    TS(B_ckv, b1[:, 256:384], SM[:, so + 5:so + 6], None, ALU.mult, None, R(b1t, tn("rkv")), R(tn("ckv")))
                    b2 = bankb(bt2)
                    b2t = "b%d" % bt2
                    TR(b2[:, 0:128], B_cq[:, 0:128], R(tn("cq")), R(b2t))
                    TR(b2[:, 128:256], B_cq[:, 128:256], R(tn("cq")), R(b2t))
                    TR(b2[:, 256:384], B_ckv[:, 0:128], R(tn("ckv")), R(b2t))
                    CP("act", CQT[:, :, t * 128:(t + 1) * 128], b2[:, 0:256].rearrange("p (k s) -> p k s", k=2), R(b2t), R("CQT%d" % t))
                    CP("dve", CKVT[:, t * 128:(t + 1) * 128], b2[:, 256:384], R(b2t), R("CKVT%d" % t))
                    ACTV(M0[:, 384:416], b1[:, 384:416], AF.Square, R(b1t), R(tn("MF0c"), "SSR"), accum=SSR[:, t:t + 1])
                    TTo("dve", M0[:, 448:480], b1[:, 384:416], GK[:, 64:96], ALU.mult, R(b1t, "GK"), R(tn("krg")))
                    rotary(M0[:, 448:480].rearrange("p (h d f) -> p h d f", h=1, d=2), tn("krg"),
                           KRB[:, t * 32:(t + 1) * 32].rearrange("p (h d f) -> p h d f", h=1, d=2), "KRB%d" % t,
                           COSM[:, t * 16:(t + 1) * 16], SINM[:, t * 16:(t + 1) * 16], "TABM", 1, 16,
                           M1[:, 256:288].rearrange("p (h d f) -> p h d f", h=1, d=2),
                           M1[:, 288:320].rearrange("p (h d f) -> p h d f", h=1, d=2), tn("mrA"), tn("mrB"))

                tile_loop2(list(range(NT)), lambda t: base + t * 128, mla_main, 8)
                P.barrier()
                checkpoint(5)
                for hh in range(2):
                    MEMSET("pool", VA[:, :, :, 64:128], 1.0, R("VAones"))
                    kd_lists = []
                    for t in range(NT):
                        p = t % 2
                        bkv, bkvt = bank(3 + p), "b%d" % (3 + p)
                        btr, btrt = (bankb(2), "b2") if p == 0 else (bankb(5), "b5")
                        kb = 52992 + p * 1408
                        k0 = arf(kb, 256).rearrange("p (h c) -> p h c", h=4)
                        k1 = arf(kb + 512, 256).rearrange("p (h c) -> p h c", h=4)
                        KTM = arb(kb + 1024, 384).rearrange("p (h c) -> p h c", h=4)
                        so = 16 if p == 0 else 48
                        ssh, stmp, srk = SM[:, so:so + 4], SM[:, so + 4:so + 8], SM[:, so + 8:so + 12]
                        tn = lambda n, p=p: "%s_%d" % (n, p)
                        P.capture()
                        MM(bkv, CKVT[:, t * 128:(t + 1) * 128], WUKV[:, hh * 512:(hh + 1) * 512], True, True,
                           R("CKVT%d" % t, "WUKV"), R(bkvt))
                        kv = bkv.rearrange("p (h c) -> p h c", h=4)
                        ACTV(k0, kv[:, :, 0:64], AF.Square, R(bkvt), R(tn("KD0")))
                        RED(ssh, k0, R(tn("KD0")), R(tn("ssh")))
                        TS(ssh, ssh, SSR[:, t:t + 1], None, ALU.add, None, R(tn("ssh"), "SSR"), R(tn("ssh")))
                        ACTV(stmp, ssh, AF.Ln, R(tn("ssh")), R(tn("kdtmp")), scale=1.0 / 96, bias=EPS)
                        ACTV(srk, stmp, AF.Exp, R(tn("kdtmp")), R(tn("rk")), scale=-0.5)
                        TTo("dve", k1, kv[:, :, 0:64], srk.unsqueeze(2).to_broadcast([128, 4, 64]), ALU.mult, R(bkvt, tn("rk")), R(tn("KD1")))
                        TTo("dve", KTM[:, :, 0:64], k1, GK[:, 0:64].unsqueeze(1).to_broadcast([128, 4, 64]), ALU.mult,
                            R(tn("KD1"), "GK"), R(tn("KTM")))
                        TTo("dve", KTM[:, :, 64:96], KRB[:, t * 32:(t + 1) * 32].unsqueeze(1).to_broadcast([128, 4, 32]),
                            srk.unsqueeze(2).to_broadcast([128, 4, 32]), ALU.mult, R("KRB%d" % t, tn("rk"), tn("KTM")), R(tn("KTM")))
                        for hl in range(4):
                            TR(btr[0:96, hl * 128:(hl + 1) * 128], KTM[:, hl, :], R(tn("KTM")), R(btrt))
                        CP("act", KT[0:96, :, t * 128:(t + 1) * 128], btr[0:96, 0:512].rearrange("p (h s) -> p h s", h=4),
                           R(btrt), R("KT%d" % t))
                        CP("dve", VA[:, t, :, 0:64], kv[:, :, 64:128], R(bkvt, "VAones"), R("VA%d" % t))
                        kd_lists.append(P.end_capture())
                    Lk = len(kd_lists[0])
                    P.replay_skewed([(ops, t * Lk / 2.0, 1.0) for t, ops in enumerate(kd_lists)])
                    P.barrier()
                    checkpoint(6)
                    NK = NT // 2
                    QTs = [QT, QT2v]

                    def qprepA(qb, s):
                        tq = qb * 4 + s
                        b6 = bank(6)
                        for k in range(2):
                            MM(b6[:, 0:384], CQT[:, k, tq * 128:(tq + 1) * 128], WUQ[:, k, hh * 384:(hh + 1) * 384], k == 0, k == 1,
                               R("CQT%d" % tq, "WUQ"), R("b6"))
                        q3 = b6[:, 0:384].rearrange("p (h c) -> p h c", h=4)
                        m0 = MF[0][:, 0:384].rearrange("p (h c) -> p h c", h=4)
                        m1 = MF[1][:, 0:384].rearrange("p (h c) -> p h c", h=4)
                        ACTV(m0, q3, AF.Square, R("b6"), R("MF0"))
                        RED(SM[:, 16:20], m0, R("MF0"), R("ssh"))
                        rsqrt_chain(SM[:, 24:28], SM[:, 16:20], 1.0 / 96, EPS, R("ssh"), R("rk"), SM[:, 20:24])
                        TTo("dve", m1, q3, SM[:, 24:28].unsqueeze(2).to_broadcast([128, 4, 96]), ALU.mult, R("b6", "rk"), R("MF1"))
                        TTo("dve", m1, m1, GQS[:, :].unsqueeze(1).to_broadcast([128, 4, 96]), ALU.mult, R("MF1", "GQS"), R("MF1"))
                        qtk = H[:, 0:384].rearrange("p (h c) -> p h c", h=4)
                        CP("dve", qtk[:, :, 0:64], m1[:, :, 0:64], R("MF1", "H"), R("H"))
                        src = m1[:, :, 64:96].rearrange("p h (d f) -> p h d f", d=2)
                        dst = qtk[:, :, 64:96].rearrange("p h (d f) -> p h d f", d=2)
                        tA = m0[:, :, 0:32].rearrange("p h (d f) -> p h d f", d=2)
                        tB = m0[:, :, 32:64].rearrange("p h (d f) -> p h d f", d=2)
                        rotary(src, "MF1", dst, "H", COSM[:, tq * 16:(tq + 1) * 16], SINM[:, tq * 16:(tq + 1) * 16], "TABM", 4, 16,
                               tA, tB, "MF0", "MF0")

                    def qprepC(qb, s):
                        qtk = H[:, 0:384].rearrange("p (h c) -> p h c", h=4)
                        b7 = bankb(7)
                        for hl in range(4):
                            TR(b7[0:96, hl * 128:(hl + 1) * 128], qtk[:, hl, :], R("H"), R("b7"))
                        CP("dve", QTs[qb % 2][0:96, :, s * 128:(s + 1) * 128], b7[0:96, 0:512].rearrange("p (h s) -> p h s", h=4),
                           R("b7"), R("QT%d" % (qb % 2)))

                    units = [(qb, hl, kp) for qb in range(NQB) for hl in range(4) for kp in range(NK)]

                    def emitQK(i):
                        qb, hl, kp = units[i]
                        ssl = i % 2
                        for j in range(2):
                            kt = 2 * kp + j
                            MM(PS[ssl][:, j * 512:(j + 1) * 512], KT[0:96, hl, kt * 128:(kt + 1) * 128], QTs[qb % 2][0:96, hl, :], True, True,
                               R("KT%d" % kt, "QT%d" % (qb % 2)), R("S%d" % ssl))

                    for s in range(4):
                        qprepA(0, s)
                        qprepC(0, s)
                    emitQK(0)
                    if len(units) > 1:
                        emitQK(1)
                    UB = 4 * NK
                    sched = {}
                    if UB >= 40:
                        for s in range(4):
                            sched[2 + 9 * s] = ("A", s)
                            sched[2 + 9 * s + 5] = ("C", s)
                    for i, (qb, hl, kp) in enumerate(units):
                        h = hh * 4 + hl
                        osl = (qb * 4 + hl) % 2
                        bO = bank(4 + osl)
                        ssl, psl = i % 2, i % 3
                        ACTV(PTA[psl][:, :], PS[ssl][:, :], AF.Exp, R("S%d" % ssl), R("PTA%d" % psl))
                        if i + 2 < len(units) and not (UB < 40 and units[i + 2][0] != qb):
                            emitQK(i + 2)
                        for j in range(2):
                            kt = 2 * kp + j
                            MM(bO, VA[:, kt, hl, :], PTA[psl][:, j * 512:(j + 1) * 512], kp == 0 and j == 0,
                               kp == NK - 1 and j == 1, R("VA%d" % kt, "PTA%d" % psl), R("bO%d" % osl))
                        ju = i % UB
                        if qb + 1 < NQB:
                            if UB >= 40:
                                ev = sched.get(ju)
                                if ev is not None:
                                    (qprepA if ev[0] == "A" else qprepC)(qb + 1, ev[1])
                        if kp == NK - 1:
                            RECIP(RL[64:128, :], bO[64:128, :], R("bO%d" % osl), R("RL"))
                            pr = (h % 2) * 64
                            TTo("dve", YT[:, :].rearrange("p (k s) -> p k s", k=8)[pr:pr + 64, 4 + h // 2, qb * 512:(qb + 1) * 512],
                                bO[0:64, :], RL[64:128, :], ALU.mult, R("bO%d" % osl, "RL"), R("YTm%d_%d" % (h, qb)))
                        if UB < 40 and ju == UB - 1 and qb + 1 < NQB:
                            for s in range(4):
                                qprepA(qb + 1, s)
                                qprepC(qb + 1, s)
                            emitQK(i + 1)
                            if i + 2 < len(units):
                                emitQK(i + 2)
                    P.barrier()

                checkpoint(7)
                WO = arb(0, 8192).rearrange("p (k n) -> p k n", k=8)
                X1 = arf(8192, 4096).rearrange("p (s n) -> p s n", s=4)
                H2T = arb(16384, 4096).rearrange("p (k s) -> p k s", k=8)
                AT = arb(20480, NF * 512).rearrange("p (f s) -> p f s", f=NF)
                NWG = 4
                WGU = [arb(31744 + i * 2048, 2048).rearrange("p (g k n) -> p g k n", g=2, k=8) for i in range(NWG)]
                WDQ = [arb(39936 + i * 5632, 5632).rearrange("p (f n) -> p f n", f=NF) for i in range(3)]
                SG = [arf(56832 + i * 1024, 512) for i in range(2)]
                DMA("sp", WO[:, :, :], wo_s.rearrange("(k p) n -> p k n", p=128), [], R("WO"))
                YT3 = YT[:, :].rearrange("p (k s) -> p k s", k=8)
                wq = [0]
                for qb in range(NQB):
                    for s in range(4):
                        row0 = base + (qb * 4 + s) * 128
                        DMA("pool", X1[:, s, :], x_d[row0:row0 + 128, :], [], R("X1_%d" % s))
                    plists = []
                    for s in range(4):
                        tq = qb * 4 + s
                        xs = "X1_%d" % s
                        p = s % 2
                        wb = (0, 1) if p == 0 else (3, 4)
                        tbi = 2 if p == 0 else 7
                        Hs, Hst = (H, "H") if p == 0 else (HT[0], "HT0")
                        so = 0 if p == 0 else 4
                        P.capture()
                        for c2 in range(2):
                            for k in range(8):
                                MM(bank(wb[c2]), YT3[:, k, tq * 128:(tq + 1) * 128], WO[:, k, c2 * 512:(c2 + 1) * 512], k == 0, k == 7,
                                   R("WO"), R("b%d" % wb[c2]))
                            TTo("dve", X1[:, s, c2 * 512:(c2 + 1) * 512], X1[:, s, c2 * 512:(c2 + 1) * 512], bank(wb[c2]), ALU.add,
                                R(xs, "b%d" % wb[c2]), R(xs))
                        ACTV(Hs[:, :], X1[:, s, :], AF.Square, R(xs), R(Hst, "xss%d" % p), accum=SM[:, so:so + 1])
                        ACTV(SM[:, so + 1:so + 2], SM[:, so:so + 1], AF.Ln, R("xss%d" % p), R("xtmp%d" % p), scale=1.0 / D, bias=EPS)
                        ACTV(SM[:, so + 2:so + 3], SM[:, so + 1:so + 2], AF.Exp, R("xtmp%d" % p), R("xrs%d" % p), scale=-0.5)
                        TS(Hs[:, :], X1[:, s, :], SM[:, so + 2:so + 3], None, ALU.mult, None, R(xs, "xrs%d" % p), R(Hst))
                        bT = bankb(tbi)
                        for k in range(8):
                            TR(bT[:, k * 128:(k + 1) * 128], Hs[:, k * 128:(k + 1) * 128], R(Hst), R("b%d" % tbi))
                        CP("act", H2T[:, :, s * 128:(s + 1) * 128], bT[:, :].rearrange("p (k s) -> p k s", k=8), R("b%d" % tbi), R("H2T"))
                        plists.append(P.end_capture())
                    Lp = len(plists[0])
                    P.replay_skewed([(ops, si * Lp / 2.0, 1.0) for si, ops in enumerate(plists)])
                    for f in range(NF):
                        sl = f % 2
                        wsl_ = f % NWG
                        DMA("sp", WGU[wsl_][:, :, :, :], wgu_s[f], [], R("WGU%d" % wsl_))
                        for g in range(2):
                            bi = 3 + 2 * g + sl
                            for k in range(8):
                                MM(bank(bi), WGU[wsl_][:, g, k, :], H2T[:, k, :], k == 0, k == 7, R("WGU%d" % wsl_, "H2T"), R("b%d" % bi))
                        bG, bU = bank(3 + sl), bank(5 + sl)
                        ACTV(SG[0][:, :], bG, AF.Exp, R("b%d" % (3 + sl)), R("SG0"), scale=-1.0)
                        ACTV(SG[0][:, :], SG[0][:, :], AF.Ln, R("SG0"), R("SG0"), bias=1.0)
                        ACTV(SG[0][:, :], SG[0][:, :], AF.Exp, R("SG0"), R("SG0"), scale=-1.0)
                        TTo("dve", SG[1][:, :], bG, SG[0][:, :], ALU.mult, R("b%d" % (3 + sl), "SG0"), R("SG1"))
                        TTo("dve", AT[:, f, :], SG[1][:, :], bU, ALU.mult, R("SG1", "b%d" % (5 + sl)), R("AT"))
                    for q4 in range(4):
                        wsl = wq[0] % 3
                        wq[0] += 1
                        DMA("sp", WDQ[wsl][:, :, :], wd_s[:, q4 * 256:(q4 + 1) * 256].rearrange("(f p) n -> p f n", p=128), [],
                            R("WDQ%d" % wsl))
                        for s in range(4):
                            bi = s % 2
                            for f in range(NF):
                                MM(bank(bi)[:, 0:256], AT[:, f, s * 128:(s + 1) * 128], WDQ[wsl][:, f, :], f == 0, f == NF - 1,
                                   R("AT", "WDQ%d" % wsl), R("b%d" % bi))
                            TTo("dve", X1[:, s, q4 * 256:(q4 + 1) * 256], X1[:, s, q4 * 256:(q4 + 1) * 256], bank(bi)[:, 0:256], ALU.add,
                                R("X1_%d" % s, "b%d" % bi), R("X1_%d" % s))
                    for s in range(4):
                        row0 = base + (qb * 4 + s) * 128
                        DMA("pool", out_d[row0:row0 + 128, :], X1[:, s, :], R("X1_%d" % s), [])
                P.barrier()
        except _Stop:
            pass
        if P.sched_on:
            while len(P.caps) > 1:
                inner = P.caps.pop()
                P.caps[-1].extend(inner)
            ops = P.caps.pop()
            P.sched_on = False
            for o in P.schedule(ops):
                P.add(*o)
        P.emit(st)
    return nc


def make_consts():
    cf = np.zeros((128, 308), np.float32)
    cf[:, 0:32] = (10000.0 ** (-np.arange(32, dtype=np.float32) / 32.0))[None, :]
    cf[:, 32:48] = (10000.0 ** (-np.arange(16, dtype=np.float32) / 16.0))[None, :]
    m = np.arange(128, dtype=np.float32)[:, None]
    c = np.arange(128, dtype=np.float32)[None, :]
    cf[:, 48:176] = np.maximum(c - m, 0.0)
    cf[:, 176:304] = np.maximum(m - c, 0.0)
    cf[:, 304] = 127.0 - m[:, 0]
    cf[:, 305] = m[:, 0]
    cf[:, 306] = m[:, 0] + 1.0
    cf[:, 307] = 128.0 - m[:, 0]
    ident = np.eye(128, dtype=np.float32).astype(ml_dtypes.bfloat16)
    return cf, ident


_CACHE = {}


def run(inputs, S, NSEQ, ncores):
    key = (S, NSEQ)
    if key not in _CACHE:
        _CACHE[key] = build(S, NSEQ)
    nc = _CACHE[key]
    cf, ident = make_consts()
    x = np.ascontiguousarray(np.asarray(inputs["x"], dtype=np.float32))
    pos = np.ascontiguousarray(np.asarray(inputs["positions"], dtype=np.int32))
    wnames = ["norm1_g", "w_in", "ret_decay_logit_fwd", "ret_decay_logit_bwd", "q_a_norm_g", "w_uq", "kv_a_norm_g",
              "w_ukv", "q_norm_g", "k_norm_g", "w_o", "norm2_g", "w_gate", "w_up", "w_down"]
    shared = {}
    for n in wnames:
        a = np.asarray(inputs[n], dtype=np.float32)
        shared[n] = np.ascontiguousarray(a[0])
    for n in ["norm1_g", "ret_decay_logit_fwd", "ret_decay_logit_bwd", "q_a_norm_g", "kv_a_norm_g", "q_norm_g", "k_norm_g", "norm2_g"]:
        shared[n] = shared[n].reshape(1, -1)
    shared["cf"] = cf
    shared["ident"] = ident
    in_maps = []
    for c in range(ncores):
        m = dict(shared)
        m["x"] = x[c * NSEQ:(c + 1) * NSEQ].reshape(NSEQ * S, D)
        m["positions"] = pos[c * NSEQ:(c + 1) * NSEQ].reshape(NSEQ * S)
        in_maps.append(m)
    res = run_bass_kernel_spmd(nc, in_maps, core_ids=list(range(ncores)))
    outs = [np.asarray(r["out"]).reshape(NSEQ, S, D) for r in res.results]
    return np.concatenate(outs, axis=0).astype(np.float32)


def kernel(**inputs):
    return run(inputs, 4096, 2, 8)
```

```python
import numpy as np
import concourse.bass as bass
import concourse.mybir as mybir

F32 = mybir.dt.float32
BF16 = mybir.dt.bfloat16
I32 = mybir.dt.int32
ALU = mybir.AluOpType
AF = mybir.ActivationFunctionType
AX = mybir.AxisListType


class Tok:
    __slots__ = ("w", "r", "name", "excl")

    def __init__(self, name="", excl=False):
        self.w = None
        self.r = {}
        self.name = name
        self.excl = excl


class Op:
    __slots__ = ("eng", "fn", "deps", "sig", "sigval", "sem", "dma", "idx")


class Prog:
    ENGS = ("pe", "act", "dve", "pool", "sp")

    def __init__(self, nc, ndma=8):
        self.nc = nc
        self.ops = {e: [] for e in self.ENGS}
        self.ndma = ndma
        self.dma_hist = {e: [] for e in self.ENGS}
        self.nops = 0
        self.caps = []
        self.sched_on = False

    def capture(self):
        self.caps.append([])

    def end_capture(self):
        return self.caps.pop()

    def schedule(self, ops, window=None):
        import os
        if window is None:
            window = int(os.environ.get("K_WINDOW", "400"))
        n = len(ops)
        preds = [None] * n
        lastw = {}
        readers = {}
        for i, (eng, fn, reads, writes, dma, cost) in enumerate(ops):
            ps = set()
            wl = list(writes) + [t for t in reads if t.excl]
            for t in reads:
                w = lastw.get(id(t))
                if w is not None:
                    ps.add(w)
            for t in wl:
                w = lastw.get(id(t))
                if w is not None:
                    ps.add(w)
                for r in readers.get(id(t), ()):
                    ps.add(r)
            for t in reads:
                if not t.excl:
                    readers.setdefault(id(t), []).append(i)
            for t in wl:
                lastw[id(t)] = i
                readers[id(t)] = []
            ps.discard(i)
            preds[i] = ps
        fin = [0.0] * n
        done = [False] * n
        efree = {}
        lastdma = {}
        order = []
        lo = 0
        LAT = float(os.environ.get("K_LAT", "0.3"))
        while len(order) < n:
            while lo < n and done[lo]:
                lo += 1
            best, bkey = None, None
            seen_eng_block = set()
            hi = min(n, lo + window)
            for i in range(lo, hi):
                if done[i]:
                    continue
                eng = ops[i][0]
                ok = True
                rdy = 0.0
                for p in preds[i]:
                    if not done[p]:
                        ok = False
                        break
                    f = fin[p] + (LAT if ops[p][0] != eng or ops[p][4] else 0.0)
                    if f > rdy:
                        rdy = f
                if not ok:
                    continue
                start = max(rdy, efree.get(eng, 0.0))
                key = (start, i)
                if bkey is None or key < bkey:
                    best, bkey = i, key
            i = best
            eng, fn, reads, writes, dma, cost = ops[i]
            start = bkey[0]
            if dma:
                efree[eng] = start + 0.1
                fin[i] = start + cost
            else:
                efree[eng] = start + cost
                fin[i] = start + cost
            done[i] = True
            order.append(ops[i])
        return order

    def replay_skewed(self, streams):
        allops = []
        G = 1
        for si, (ops, t0, dt) in enumerate(streams):
            for i, o in enumerate(ops):
                allops.append((t0 + (i // G) * G * dt, si, i, o))
        allops.sort(key=lambda z: (z[0], z[1], z[2]))
        for _, _, _, o in allops:
            self.add(*o)

    def add(self, eng, fn, reads=(), writes=(), dma=False, cost=0.3):
        if self.caps:
            self.caps[-1].append((eng, fn, list(reads), list(writes), dma, cost))
            return None
        op = Op()
        op.eng = eng
        op.fn = fn
        op.dma = dma
        op.sig = dma
        op.sigval = 0
        op.sem = None
        op.idx = self.nops
        self.nops += 1
        deps = {}
        xr = [t for t in reads if t.excl]
        if xr:
            reads = [t for t in reads if not t.excl]
            writes = list(writes) + [t for t in xr if t not in writes]

        def need(d, war):
            if d is None:
                return
            if (not d.dma) and (not dma) and d.eng == eng:
                if eng == "pe" or war:
                    return
            deps[d.idx] = d

        for t in reads:
            need(t.w, False)
        for t in writes:
            need(t.w, False)
            for r in t.r.values():
                need(r, True)
        if dma:
            h = self.dma_hist[eng]
            k = len(h)
            op.sem = (eng, k % self.ndma)
            op.sigval = 16 * (k // self.ndma + 1)
            if k >= self.ndma:
                d = h[k - self.ndma]
                deps[d.idx] = d
            h.append(op)
        op.deps = list(deps.values())
        for d in op.deps:
            d.sig = True
        for t in reads:
            key = (eng, op.idx) if dma else eng
            t.r[key] = op
        for t in writes:
            t.w = op
            t.r = {}
        self.ops[eng].append(op)
        return op

    def barrier(self):
        if self.sched_on and len(self.caps) == 1:
            ops = self.caps.pop()
            for o in self.schedule(ops):
                self.add(*o)
            self._barrier()
            self.caps.append([])
            return
        self._barrier()

    def _barrier(self):
        lasts = []
        for e in self.ENGS:
            comp = [o for o in self.ops[e] if not o.dma and o.fn is not None]
            if comp:
                lasts.append(comp[-1])
            h = self.dma_hist[e]
            lasts.extend(h[-self.ndma:])
        for e in self.ENGS:
            op = Op()
            op.eng = e
            op.fn = None
            op.dma = False
            op.sig = False
            op.sigval = 0
            op.sem = None
            op.idx = self.nops
            self.nops += 1
            op.deps = [d for d in lasts if not (d.eng == e and not d.dma)]
            for d in op.deps:
                d.sig = True
            self.ops[e].append(op)

    def emit(self, stack):
        nc = self.nc
        sems = {}
        for e in self.ENGS:
            sems[e] = stack.enter_context(nc.semaphore("s_" + e))
            for k in range(self.ndma):
                sems[(e, k)] = stack.enter_context(nc.semaphore("d_%s%d" % (e, k)))
        for e in self.ENGS:
            c = 0
            for op in self.ops[e]:
                if op.dma or op.fn is None:
                    continue
                if op.sig:
                    c += 1
                    op.sigval = c
                    op.sem = e
        block = stack.enter_context(nc.Block())
        prog = self

        def run(engname, eng):
            waited = {}
            for op in prog.ops[engname]:
                for d in op.deps:
                    if waited.get(d.sem, 0) >= d.sigval:
                        continue
                    eng.wait_ge(sems[d.sem], d.sigval)
                    waited[d.sem] = d.sigval
                if op.fn is None:
                    continue
                ins = op.fn(eng)
                if op.sig:
                    ins.then_inc(sems[op.sem], 16 if op.dma else 1)
            for op in prog.dma_hist[engname][-prog.ndma:]:
                if waited.get(op.sem, 0) < op.sigval:
                    eng.wait_ge(sems[op.sem], op.sigval)
                    waited[op.sem] = op.sigval

        @block.tensor
        def _(e):
            run("pe", e)

        @block.scalar
        def _(e):
            run("act", e)

        @block.vector
        def _(e):
            run("dve", e)

        @block.gpsimd
        def _(e):
            run("pool", e)

        @block.sync
        def _(e):
            run("sp", e)


import ml_dtypes
from contextlib import ExitStack
from concourse.bass_utils import run_bass_kernel_spmd

D = 1024
DFF = 2816
NF = DFF // 128
INW = 2464
EPS = 1e-6
GN_EPS = 1e-5
TWO_PI = 6.283185307179586
C1 = 6.28125
C2 = TWO_PI - C1
MAGIC = 12582912.0
PI_SAFE = 3.1415925
ARENA = 60416
SCHED = True


class _Stop(Exception):
    pass


def build(S, NSEQ, stop=None):
    NT = S // 128
    NQB = S // 512
    nc = bass.Bass("TRN2", target_bir_lowering=False)

    def din(name, shape, dt=F32):
        return nc.dram_tensor(name, shape, dt, kind="ExternalInput").ap()

    x_d = din("x", [NSEQ * S, D])
    pos_d = din("positions", [NSEQ * S], I32)
    n1_d = din("norm1_g", [1, D])
    win_d = din("w_in", [D, INW])
    lf_d = din("ret_decay_logit_fwd", [1, 8])
    lb_d = din("ret_decay_logit_bwd", [1, 8])
    gqa_d = din("q_a_norm_g", [1, 256])
    wuq_d = din("w_uq", [256, 768])
    gkva_d = din("kv_a_norm_g", [1, 128])
    wukv_d = din("w_ukv", [128, 1024])
    gq_d = din("q_norm_g", [1, 96])
    gk_d = din("k_norm_g", [1, 96])
    wo_d = din("w_o", [D, D])
    n2_d = din("norm2_g", [1, D])
    wg_d = din("w_gate", [D, DFF])
    wu_d = din("w_up", [D, DFF])
    wd_d = din("w_down", [DFF, D])
    cf_d = din("cf", [128, 308])
    id_d = din("ident", [128, 128], BF16)
    out_d = nc.dram_tensor("out", [NSEQ * S, D], F32, kind="ExternalOutput").ap()
    win_s = nc.dram_tensor("win_s", [D, INW], BF16).ap()
    wuq_s = nc.dram_tensor("wuq_s", [256, 768], BF16).ap()
    wukv_s = nc.dram_tensor("wukv_s", [128, 1024], BF16).ap()
    wo_s = nc.dram_tensor("wo_s", [D, D], BF16).ap()
    wgu_s = nc.dram_tensor("wgu_s", [NF, 128, 2, 8, 128], BF16).ap()
    wd_s = nc.dram_tensor("wd_s", [DFF, D], BF16).ap()

    st = ExitStack()
    with st:
        def sb(name, shape, dt):
            return st.enter_context(nc.sbuf_tensor(name, shape, dt))

        AR = sb("arena", [128, ARENA], BF16)
        YT = sb("YT", [128, 8 * S], BF16)
        XT = [sb("XT%d" % i, [128, D], F32) for i in range(2)]
        H = sb("H", [128, D], BF16)
        HT = [sb("HT%d" % i, [128, D], BF16) for i in range(2)]
        CF = sb("CF", [128, 308], F32)
        ID = sb("ID", [128, 128], BF16)
        G1 = sb("G1", [128, 8], F32)
        G2 = sb("G2", [128, 8], F32)
        GQA = sb("GQA", [128, 2], F32)
        GKVA = sb("GKVA", [128, 1], F32)
        GQS = sb("GQS", [128, 96], F32)
        GK = sb("GK", [128, 96], F32)
        LGT = sb("LGT", [128, 16], F32)
        LG = sb("LG", [128, 16], F32)
        ZT = sb("ZT", [128, 48], F32)
        ZX = sb("ZX", [128, 48], F32)
        XFB = sb("XFB", [128, 16], F32)
        STT_ = sb("STATE", [128, 512], F32)
        POSI = sb("POSI", [128, NT], I32)
        POSF = sb("POSF", [128, NT], F32)
        SM = sb("SM", [128, 128], F32)
        SSR = sb("SSR", [128, NT], F32)
        QT2 = sb("QT2", [128, 2048], BF16)
        QT2v = QT2[:, :].rearrange("p (h s) -> p h s", h=4)
        PS = [st.enter_context(nc.psum_tensor("ps%d" % i, [128, 1024], F32)) for i in range(4)]

        def bank(i):
            return PS[i // 2][:, (i % 2) * 512:(i % 2 + 1) * 512]

        def bankb(i):
            return bank(i).bitcast(BF16)

        def arb(off, n):
            return AR[:, off:off + n]

        def arf(off, n):
            return AR[:, off:off + 2 * n].bitcast(F32)

        P = Prog(nc)
        toks = {}

        def T(name):
            t = toks.get(name)
            if t is None:
                excl = (name[0] == "b" and name[1:].isdigit()) or name in ("S", "S0", "S1", "bO0", "bO1")
                t = toks[name] = Tok(name, excl)
            return t

        def R(*names):
            return [T(n) for n in names]

        def fsz(ap):
            n = 1
            for d in ap.shape[1:]:
                n *= d
            return n

        def MM(out, lhsT, rhs, start, stop, r, w, **kw):
            return P.add("pe", lambda e: e.matmul(out, lhsT=lhsT, rhs=rhs, start=start, stop=stop, **kw), r, w,
                         cost=0.05 + max(64, fsz(rhs)) / 2400.0)

        def TR(out, in_, r, w):
            return P.add("pe", lambda e: e.transpose(out=out, in_=in_, identity=ID[:, :]), r + [T("ID")], w, cost=0.12)

        def TTo(eng, out, in0, in1, op, r, w):
            return P.add(eng, lambda e: e.tensor_tensor(out=out, in0=in0, in1=in1, op=op), r, w,
                         cost=(0.15 + fsz(out) / 960.0) if eng == "dve" else (0.3 + fsz(out) / 500.0))

        def TS(out, in0, s1, s2, op0, op1, r, w):
            if s2 is None:
                return P.add("dve", lambda e: e.tensor_scalar(out=out, in0=in0, scalar1=s1, scalar2=None, op0=op0), r, w,
                             cost=0.15 + fsz(out) / 1400.0)
            return P.add("dve", lambda e: e.tensor_scalar(out=out, in0=in0, scalar1=s1, scalar2=s2, op0=op0, op1=op1), r, w,
                         cost=0.15 + fsz(out) / 1400.0)

        def STT(out, in0, scalar, in1, op0, op1, r, w):
            return P.add("dve", lambda e: e.scalar_tensor_tensor(out=out, in0=in0, scalar=scalar, in1=in1, op0=op0, op1=op1), r, w,
                         cost=0.15 + fsz(out) / 960.0)

        def ACTV(out, in_, func, r, w, scale=1.0, bias=0.0, accum=None):
            if accum is None:
                return P.add("act", lambda e: e.activation(out=out, in_=in_, func=func, bias=bias, scale=scale), r, w,
                             cost=0.2 + fsz(in_) / 1200.0)
            return P.add("act", lambda e: e.activation(out=out, in_=in_, func=func, bias=bias, scale=scale, accum_out=accum), r, w,
                         cost=0.3 + fsz(in_) / 1200.0)

        def CP(eng, out, in_, r, w):
            if eng == "act":
                return P.add("act", lambda e: e.copy(out=out, in_=in_), r, w, cost=0.2 + fsz(in_) / 1200.0)
            return P.add(eng, lambda e: e.tensor_copy(out=out, in_=in_), r, w,
                         cost=(0.15 + fsz(out) / 1400.0) if eng == "dve" else (0.3 + fsz(out) / 500.0))

        def RED(out, in_, r, w):
            return P.add("dve", lambda e: e.reduce_sum(out=out, in_=in_, axis=AX.X), r, w, cost=0.15 + fsz(in_) / 960.0)

        def RECIP(out, in_, r, w):
            return P.add("dve", lambda e: e.reciprocal(out=out, in_=in_), r, w, cost=0.15 + fsz(in_) / 960.0)

        def MEMSET(eng, ap, val, w):
            return P.add(eng, lambda e: e.memset(ap, val), [], w)

        def DMA(q, out, in_, r, w, slow=False):
            if slow:
                return P.add(q, lambda e: e.dma_start(out=out, in_=in_, allow_slow_non_contiguous=True), r, w, dma=True, cost=3.0)
            return P.add(q, lambda e: e.dma_start(out=out, in_=in_), r, w, dma=True, cost=2.5 + fsz(out) / 1500.0)

        def rsqrt_chain(dst, src, n_inv, eps, r, w, tmp):
            ACTV(tmp, src, AF.Ln, r, [T("rs_tmp")], scale=n_inv, bias=eps)
            ACTV(dst, tmp, AF.Exp, [T("rs_tmp")], w, scale=-0.5)

        DMA("sp", CF[:, :], cf_d[:, :], [], R("CF"))
        DMA("sp", ID[:, :], id_d[:, :], [], R("ID"))
        DMA("sp", G1[:, :], n1_d.rearrange("o (k p) -> p (o k)", p=128), [], R("G1"), slow=True)
        DMA("sp", G2[:, :], n2_d.rearrange("o (k p) -> p (o k)", p=128), [], R("G2"), slow=True)
        DMA("sp", GQA[:, :], gqa_d.rearrange("o (k p) -> p (o k)", p=128), [], R("GQA"), slow=True)
        DMA("sp", GKVA[:, :], gkva_d.rearrange("o (k p) -> p (o k)", p=128), [], R("GKVA"), slow=True)
        DMA("sp", GQS[:, :], gq_d.to_broadcast([128, 96]), [], R("GQS"), slow=True)
        DMA("sp", GK[:, :], gk_d.to_broadcast([128, 96]), [], R("GK"), slow=True)
        DMA("sp", LGT[:, 0:8], lf_d.to_broadcast([128, 8]), [], R("LGT"), slow=True)
        DMA("sp", LGT[:, 8:16], lb_d.to_broadcast([128, 8]), R("LGT"), R("LGT"), slow=True)
        TS(GQS[:, :], GQS[:, :], 96.0 ** -0.5, None, ALU.mult, None, R("GQS"), R("GQS"))
        ACTV(LG[:, :], LGT[:, :], AF.Exp, R("LGT"), R("LG"), scale=-1.0)
        ACTV(LG[:, :], LG[:, :], AF.Ln, R("LG"), R("LG"), bias=1.0)
        TS(LG[:, :], LG[:, :], -1.0, None, ALU.mult, None, R("LG"), R("LG"))
        IDX = CF[:, 304:308]
        for j, (lo, ic) in enumerate([(0, 0), (8, 1), (0, 2), (8, 3)]):
            TS(ZX[:, j * 8:(j + 1) * 8], LG[:, lo:lo + 8], IDX[:, ic:ic + 1], None, ALU.mult, None, R("LG", "CF"), R("ZX"))
        TS(ZX[:, 32:48], LG[:, 0:16], 128.0, None, ALU.mult, None, R("LG"), R("ZX"))
        ACTV(ZT[:, :], ZX[:, :], AF.Exp, R("ZX"), R("ZT"))
        XFBv = XFB[:, :].rearrange("p (h d) -> p h d", d=2)
        CP("dve", XFBv[:, :, 0], ZT[:, 16:24], R("ZT"), R("XFB"))
        CP("dve", XFBv[:, :, 1], ZT[:, 24:32], R("ZT", "XFB"), R("XFB"))
        ZF, ZB, GCF, GCB = ZT[:, 0:8], ZT[:, 8:16], ZT[:, 32:40], ZT[:, 40:48]

        SF = [arf(i * 5632, 2816) for i in range(2)]
        SBF = [arb(11264 + i * 2816, 2816) for i in range(2)]
        cnt = [0]

        def prep(src, K, N, gain, gtok, dst_fn):
            for kc in range(K // 128):
                s = cnt[0] % 2
                cnt[0] += 1
                DMA("sp", SF[s][:, 0:N], src[kc * 128:(kc + 1) * 128, :], [], R("SF%d" % s))
                if gain is not None:
                    TS(SBF[s][:, 0:N], SF[s][:, 0:N], gain[:, kc:kc + 1], None, ALU.mult, None,
                       R("SF%d" % s, gtok), R("SB%d" % s))
                else:
                    CP("act", SBF[s][:, 0:N], SF[s][:, 0:N], R("SF%d" % s), R("SB%d" % s))
                dst_fn(kc, SBF[s], R("SB%d" % s))

        prep(win_d, D, INW, G1, "G1", lambda kc, t, r: DMA("pool", win_s[kc * 128:(kc + 1) * 128, :], t[:, 0:INW], r, []))
        prep(wuq_d, 256, 768, GQA, "GQA", lambda kc, t, r: DMA("pool", wuq_s[kc * 128:(kc + 1) * 128, :], t[:, 0:768], r, []))
        prep(wukv_d, 128, 1024, GKVA, "GKVA", lambda kc, t, r: DMA("pool", wukv_s[:, :], t[:, 0:1024], r, []))
        prep(wo_d, D, D, None, None, lambda kc, t, r: DMA("pool", wo_s[kc * 128:(kc + 1) * 128, :], t[:, 0:D], r, []))
        for gi, wsrc in enumerate((wg_d, wu_d)):
            prep(wsrc, D, DFF, G2, "G2",
                 lambda kc, t, r, gi=gi: DMA("pool", wgu_s[:, :, gi, kc, :].rearrange("f p n -> p f n"),
                                             t[:, 0:DFF].rearrange("p (f n) -> p f n", n=128), r, []))
        prep(wd_d, DFF, D, None, None, lambda kc, t, r: DMA("pool", wd_s[kc * 128:(kc + 1) * 128, :], t[:, 0:D], r, []))
        P.barrier()
        if SCHED:
            P.sched_on = True
            P.capture()

        xc = [0]

        def xnorm(row0, xdst=None, xtok=None, ht_eng="act"):
            s = xc[0] % 2
            xc[0] += 1
            if xdst is None:
                xdst, xtok = XT[s][:, :], "XT%d" % s
                DMA("sp", xdst, x_d[row0:row0 + 128, :], [], R(xtok))
            ACTV(H[:, :], xdst, AF.Square, R(xtok), R("H", "xss"), accum=SM[:, 0:1])
            rsqrt_chain(SM[:, 2:3], SM[:, 0:1], 1.0 / D, EPS, R("xss"), R("xrs"), SM[:, 1:2])
            TS(H[:, :], xdst, SM[:, 2:3], None, ALU.mult, None, R(xtok, "xrs"), R("H"))
            bT = bankb(0)
            for k in range(8):
                TR(bT[:, k * 128:(k + 1) * 128], H[:, k * 128:(k + 1) * 128], R("H"), R("b0"))
            CP(ht_eng, HT[s][:, :], bT[:, :], R("b0"), R("HT%d" % s))
            return HT[s], "HT%d" % s

        def tile_loop(order, row_of, main):
            P.capture()
            nxt = xnorm(row_of(order[0]))
            P.replay_skewed([(P.end_capture(), 0.0, 1.0)])
            for i, t in enumerate(order):
                cur = nxt
                P.capture()
                main(t, cur[0], cur[1])
                mops = P.end_capture()
                streams = [(mops, 0.0, 1.0)]
                if i + 1 < len(order):
                    P.capture()
                    nxt = xnorm(row_of(order[i + 1]))
                    xops = P.end_capture()
                    streams.append((xops, 0.3 * len(mops), 0.6 * len(mops) / max(1, len(xops))))
                P.replay_skewed(streams)

        def tile_loop2(order, row_of, main, nproj):
            n = len(order)
            xl, hts = [], []
            for i in range(n):
                P.capture()
                hts.append(xnorm(row_of(order[i])))
                xl.append(P.end_capture())
            ml = []
            for i, t in enumerate(order):
                P.capture()
                main(t, i % 2, hts[i][0], hts[i][1])
                ml.append(P.end_capture())
            Lm = float(len(ml[0]))
            assert nproj < 0.5 * Lm, (nproj, Lm)
            Lx = len(xl[0])
            streams = []
            for i in range(n):
                streams.append((ml[i], i * Lm / 2.0, 1.0))
                streams.append((xl[i], (i - 1) * Lm / 2.0 + 0.05 * Lm, 0.4 * Lm / Lx))
            P.replay_skewed(streams)

        def proj(bi, ht, httok, W, wtok, c0, n):
            for k in range(8):
                MM(bank(bi)[:, 0:n], ht[:, k * 128:(k + 1) * 128], W[:, k, c0:c0 + n], k == 0, k == 7,
                   R(httok, wtok), R("b%d" % bi))

        def rotary(src, stok, dst, dtok, cos, sin, tabtok, nh, half, tA, tB, tAtok, tBtok):
            cb = cos.unsqueeze(1).unsqueeze(1).to_broadcast([128, nh, 2, half])
            sbh = sin.unsqueeze(1).to_broadcast([128, nh, half])
            TTo("dve", tA, src, cb, ALU.mult, R(stok, tabtok, tAtok), R(tAtok))
            TTo("pool", tB[:, :, 0, :], src[:, :, 1, :], sbh, ALU.mult, R(stok, tabtok, tBtok), R(tBtok))
            TTo("pool", tB[:, :, 1, :], src[:, :, 0, :], sbh, ALU.mult, R(stok, tabtok, tBtok), R(tBtok))
            TTo("dve", dst[:, :, 0, :], tA[:, :, 0, :], tB[:, :, 0, :], ALU.subtract, R(tAtok, tBtok, dtok), R(dtok))
            TTo("dve", dst[:, :, 1, :], tA[:, :, 1, :], tB[:, :, 1, :], ALU.add, R(tAtok, tBtok, dtok), R(dtok))

        def trig_tables(n, invf, cosd, sind, tmp, tabtok):
            ang, u, r_ = tmp
            TTo("dve", ang.rearrange("p (t f) -> p t f", f=n), POSF[:, :].unsqueeze(2).to_broadcast([128, NT, n]),
                invf.unsqueeze(1).to_broadcast([128, NT, n]), ALU.mult, R("POSF", "CF"), R("ang"))
            for shift, dst in ((0.0, sind), (np.pi / 2, cosd)):
                if shift:
                    TS(ang, ang, shift, None, ALU.add, None, R("ang"), R("ang"))
                TS(u, ang, 1.0 / TWO_PI, MAGIC, ALU.mult, ALU.add, R("ang"), R("trU"))
                TS(u, u, -MAGIC, None, ALU.add, None, R("trU"), R("trU"))
                STT(r_, u, -C1, ang, ALU.mult, ALU.add, R("trU", "ang"), R("trR"))
                STT(r_, u, -C2, r_, ALU.mult, ALU.add, R("trU", "trR"), R("trR"))
                TS(r_, r_, -PI_SAFE, PI_SAFE, ALU.max, ALU.min, R("trR"), R("trR"))
                ACTV(dst, r_, AF.Sin, R("trR"), R(tabtok))

        def checkpoint(k):
            if stop is not None and k == stop:
                raise _Stop()

        try:
            for sq in range(NSEQ):
                base = sq * S
                checkpoint(0)
                DMA("sp", POSI[:, :], pos_d[base:base + S].rearrange("(t p) -> p t", p=128), [], R("POSI"), slow=True)
                CP("dve", POSF[:, :], POSI[:, :], R("POSI"), R("POSF"))

                checkpoint(1)
                WR = arb(0, 12288).rearrange("p (k n) -> p k n", k=8)
                WR1 = arb(0, 8192).rearrange("p (k n) -> p k n", k=8)
                KTOK = arb(12288, NT * 512).rearrange("p (t n) -> p t n", n=512)
                RBC = arb(28672, (NT // 2 + 1) * 512).rearrange("p (t n) -> p t n", n=512)
                RJ = arb(37888, 1024).rearrange("p (t n) -> p t n", n=512)
                DTm = arf(38912, 1024)
                COSR = arf(40960, NT * 32)
                SINR = arf(40960 + NT * 64, NT * 32)
                o = 40960 + NT * 128
                F = [arf(o + i * 1024, 512) for i in range(4)]
                o += 4096
                B = [arb(o + i * 1024, 1024) for i in range(6)]
                o += 6144
                assert o <= ARENA
                tmp3 = [arf(12288 + i * (NT * 64), NT * 32) for i in range(3)]
                trig_tables(32, CF[:, 0:32], COSR, SINR, tmp3, "TABR")
                for h in range(8):
                    TS(F[0][:, 0:128], CF[:, 48:176], LG[:, h:h + 1], None, ALU.mult, None, R("CF", "LG"), R("F0"))
                    STT(F[1][:, 0:128], CF[:, 176:304], LG[:, 8 + h:9 + h], F[0][:, 0:128], ALU.mult, ALU.add,
                        R("CF", "LG", "F0"), R("F1"))
                    ACTV(DTm[:, h * 128:(h + 1) * 128], F[1][:, 0:128], AF.Exp, R("F1"), R("DT"))
                MEMSET("dve", STT_[:, :], 0.0, R("ST"))
                P.barrier()

                def v4(ap, nh, half):
                    return ap.rearrange("p (h d f) -> p h d f", h=nh, d=2)

                def v3(ap, a):
                    return ap.rearrange("p (a b) -> p a b", a=a)

                checkpoint(2)
                DMA("sp", WR1[:, :, :], win_s[:, 512:1536].rearrange("(k p) n -> p k n", p=128), [], R("WR"))
                F2 = [arf(o + i * 1024, 512) for i in range(4)]
                assert o + 4096 <= ARENA

                def sweep1_main(t, p, ht, htt):
                    bk, bv, bkv = (1, 2, 3) if p == 0 else (4, 5, 6)
                    Fp = F if p == 0 else F2
                    fn = (lambda i: "F%d" % i) if p == 0 else (lambda i: "G%d" % i)
                    Bp, Bpt = (B[0], "B0") if p == 0 else (B[1], "B1")
                    proj(bk, ht, htt, WR1, "WR", 0, 512)
                    proj(bv, ht, htt, WR1, "WR", 512, 512)
                    ACTV(Fp[0][:, :], bank(bk), AF.Copy, R("b%d" % bk), R(fn(0)), scale=0.125)
                    rotary(v4(Fp[0][:, :], 8, 32), fn(0), v4(KTOK[:, t, :], 8, 32), "KTOK%d" % t,
                           COSR[:, t * 32:(t + 1) * 32], SINR[:, t * 32:(t + 1) * 32], "TABR", 8, 32,
                           v4(Fp[1][:, :], 8, 32), v4(Fp[2][:, :], 8, 32), fn(1), fn(2))
                    TTo("dve", v3(Bp[:, 0:512], 8), v3(bank(bv), 8), ZB.unsqueeze(2).to_broadcast([128, 8, 64]), ALU.mult,
                        R("b%d" % bv, "ZT"), R(Bpt))
                    for h in range(8):
                        MM(bank(bkv)[64:128, h * 64:(h + 1) * 64], KTOK[:, t, h * 64:(h + 1) * 64], Bp[:, h * 64:(h + 1) * 64],
                           True, True, R("KTOK%d" % t, Bpt), R("b%d" % bkv), tile_position=(0, 64))
                    pr = (t % 2) * 64
                    CP("act", RBC[pr:pr + 64, t // 2, :], STT_[64:128, :], R("ST"), R("RBC%d" % t))
                    TTo("dve", v3(Fp[3][64:128, :], 8), v3(STT_[64:128, :], 8), GCB[64:128, :].unsqueeze(2).to_broadcast([64, 8, 64]),
                        ALU.mult, R("ST", "ZT"), R(fn(3)))
                    TTo("dve", STT_[64:128, :], Fp[3][64:128, :], bank(bkv)[64:128, :], ALU.add, R(fn(3), "b%d" % bkv), R("ST"))

                tile_loop2(list(range(NT - 1, -1, -1)), lambda t: base + t * 128, sweep1_main, 16)
                P.barrier()

                checkpoint(3)
                DMA("sp", WR[:, :, 0:512], win_s[:, 0:512].rearrange("(k p) n -> p k n", p=128), [], R("WR"))
                DMA("sp", WR[:, :, 512:1024], win_s[:, 1536:2048].rearrange("(k p) n -> p k n", p=128), R("WR"), R("WR"))
                DMA("sp", WR[:, :, 1024:1536], win_s[:, 1024:1536].rearrange("(k p) n -> p k n", p=128), R("WR"), R("WR"))
                MEMSET("dve", RJ[0:64, 0, :], 0.0, R("RJ0"))
                CP("dve", RJ[64:128, 0, :], RBC[0:64, 0, :], R("RBC0", "RJ0"), R("RJ0"))
                if 4 * S >= 8192:
                    YTm = YT[:, 4 * S:8 * S]
                else:
                    YTm = arb(16384, 8192)
                Bb = [YTm[:, i * 1024:(i + 1) * 1024] for i in range(6)]
                SGs = [YTm[:, 6144 + i * 1024:6144 + (i + 1) * 1024].bitcast(F32) for i in range(2)]

                def sweep2_main(t, ht, htt):
                    p = t % 2
                    Fp = F if p == 0 else F2
                    Bp = B if p == 0 else Bb
                    SGt = SGs[p]
                    so = 0 if p == 0 else 64
                    tn = lambda n, p=p: "%s_%d" % (n, p)
                    proj(1, ht, htt, WR, "WR", 0, 512)
                    proj(2, ht, htt, WR, "WR", 512, 512)
                    proj(3, ht, htt, WR, "WR", 1024, 512)
                    ACTV(SGt, bank(2), AF.Exp, R("b2"), R(tn("SG")), scale=-1.0)
                    ACTV(SGt, SGt, AF.Ln, R(tn("SG")), R(tn("SG")), bias=1.0)
                    ACTV(SGt, SGt, AF.Exp, R(tn("SG")), R(tn("SG")), scale=-1.0)
                    TTo("dve", SGt, bank(2), SGt, ALU.mult, R("b2", tn("SG")), R(tn("SG")))
                    ACTV(Fp[0][:, :], bank(1), AF.Copy, R("b1"), R(tn("F0")))
                    rotary(v4(Fp[0][:, :], 8, 32), tn("F0"), v4(Fp[3][:, :], 8, 32), tn("F3"),
                           COSR[:, t * 32:(t + 1) * 32], SINR[:, t * 32:(t + 1) * 32], "TABR", 8, 32,
                           v4(Fp[1][:, :], 8, 32), v4(Fp[2][:, :], 8, 32), tn("F1"), tn("F2"))
                    CP("dve", Bp[0][:, 0:512], Fp[3][:, :], R(tn("F3")), R(tn("B0")))
                    TTo("dve", Bp[1][:, :].rearrange("p (h d e) -> p h d e", h=8, d=2),
                        v3(Fp[3][:, :], 8).unsqueeze(2).to_broadcast([128, 8, 2, 64]),
                        XFB[:, :].rearrange("p (h d) -> p h d", d=2).unsqueeze(3).to_broadcast([128, 8, 2, 64]),
                        ALU.mult, R(tn("F3"), "XFB"), R(tn("B1")))
                    CP("act", Bp[2][:, 0:512], bank(3), R("b3"), R(tn("B2a")))
                    TTo("dve", v3(Bp[2][:, 512:1024], 8), v3(bank(3), 8), ZF.unsqueeze(2).to_broadcast([128, 8, 64]), ALU.mult,
                        R("b3", "ZT"), R(tn("B2b")))
                    b6 = bankb(6)
                    for j in range(4):
                        TR(b6[:, j * 128:(j + 1) * 128], Bp[0][:, j * 128:(j + 1) * 128], R(tn("B0")), R("b6"))
                    for j in range(4):
                        TR(b6[:, 512 + j * 128:512 + (j + 1) * 128], KTOK[:, t, j * 128:(j + 1) * 128], R("KTOK%d" % t), R("b6"))
                    CP("act", Bp[3][:, :], b6[:, :], R("b6"), R(tn("B3")))
                    for h in range(8):
                        TR(b6[:, h * 128:(h + 1) * 128], Bp[1][:, h * 128:(h + 1) * 128], R(tn("B1")), R("b6"))
                    CP("dve", Bp[4][:, :], b6[:, :], R("b6"), R(tn("B4")))
                    for h in range(8):
                        j, hp = h // 2, (h % 2) * 64
                        MM(PS[2][:, (h % 2) * 512 + j * 128:(h % 2) * 512 + (j + 1) * 128],
                           Bp[3][hp:hp + 64, 512 + j * 128:512 + (j + 1) * 128],
                           Bp[3][hp:hp + 64, j * 128:(j + 1) * 128], True, True, R(tn("B3")), R("S"))
                    for c2 in range(2):
                        TTo("dve", Bp[5][:, :].rearrange("p (j q c) -> p j q c", j=4, q=2)[:, :, c2, :],
                            PS[2][:, c2 * 512:(c2 + 1) * 512].rearrange("p (j c) -> p j c", j=4),
                            DTm[:, :].rearrange("p (j q c) -> p j q c", j=4, q=2)[:, :, c2, :], ALU.mult,
                            R("S", "DT", tn("B5")), R(tn("B5")))
                    for h in range(8):
                        MM(bank(7)[0:64, h * 64:(h + 1) * 64], KTOK[:, t, h * 64:(h + 1) * 64], Bp[2][:, 512 + h * 64:512 + (h + 1) * 64],
                           True, True, R("KTOK%d" % t, tn("B2b")), R("b7"))
                    rj = t % 2
                    for h in range(8):
                        MM(bank(4)[:, h * 64:(h + 1) * 64], Bp[5][:, h * 128:(h + 1) * 128], Bp[2][:, h * 64:(h + 1) * 64],
                           True, False, R(tn("B5"), tn("B2a")), R("S"))
                        MM(bank(4)[:, h * 64:(h + 1) * 64], Bp[4][:, h * 128:(h + 1) * 128], RJ[:, rj, h * 64:(h + 1) * 64],
                           False, True, R(tn("B4"), "RJ%d" % rj), R("S"))
                    if t + 1 < NT:
                        nrj = (t + 1) % 2
                        TTo("dve", v3(Fp[1][0:64, :], 8), v3(STT_[0:64, :], 8), GCF[0:64, :].unsqueeze(2).to_broadcast([64, 8, 64]),
                            ALU.mult, R("ST", "ZT", tn("F1")), R(tn("F1")))
                        TTo("dve", STT_[0:64, :], Fp[1][0:64, :], bank(7)[0:64, :], ALU.add, R(tn("F1"), "b7"), R("ST"))
                        CP("act", RJ[0:64, nrj, :], STT_[0:64, :], R("ST"), R("RJ%d" % nrj))
                        pr = ((t + 1) % 2) * 64
                        CP("dve", RJ[64:128, nrj, :], RBC[pr:pr + 64, (t + 1) // 2, :], R("RBC%d" % (t + 1), "RJ%d" % nrj), R("RJ%d" % nrj))
                    ACTV(Fp[0][:, :], bank(4), AF.Copy, R("S"), R(tn("F0")))
                    RED(SM[:, so + 8:so + 16], v3(Fp[0][:, :], 8), R(tn("F0")), R(tn("mu")))
                    ACTV(Fp[2][:, :], Fp[0][:, :], AF.Square, R(tn("F0")), R(tn("F2")))
                    RED(SM[:, so + 16:so + 24], v3(Fp[2][:, :], 8), R(tn("F2")), R(tn("m2")))
                    TS(SM[:, so + 8:so + 16], SM[:, so + 8:so + 16], 1.0 / 64, None, ALU.mult, None, R(tn("mu")), R(tn("mu")))
                    TTo("dve", SM[:, so + 24:so + 32], SM[:, so + 8:so + 16], SM[:, so + 8:so + 16], ALU.mult, R(tn("mu")), R(tn("msq")))
                    STT(SM[:, so + 16:so + 24], SM[:, so + 16:so + 24], 1.0 / 64, SM[:, so + 24:so + 32], ALU.mult, ALU.subtract,
                        R(tn("m2"), tn("msq")), R(tn("m2")))
                    ACTV(SM[:, so + 40:so + 48], SM[:, so + 16:so + 24], AF.Ln, R(tn("m2")), R(tn("gtmp")), scale=1.0, bias=GN_EPS)
                    ACTV(SM[:, so + 32:so + 40], SM[:, so + 40:so + 48], AF.Exp, R(tn("gtmp")), R(tn("grs")), scale=-0.5)
                    TTo("dve", v3(Fp[2][:, :], 8), v3(Fp[0][:, :], 8), SM[:, so + 8:so + 16].unsqueeze(2).to_broadcast([128, 8, 64]),
                        ALU.subtract, R(tn("F0"), tn("mu"), tn("F2")), R(tn("F2")))
                    TTo("dve", v3(Fp[2][:, :], 8), v3(Fp[2][:, :], 8), SM[:, so + 32:so + 40].unsqueeze(2).to_broadcast([128, 8, 64]),
                        ALU.mult, R(tn("F2"), tn("grs")), R(tn("F2")))
                    TTo("dve", Bp[0][:, 0:512], Fp[2][:, :], SGt, ALU.mult, R(tn("F2"), tn("SG"), tn("B0")), R(tn("B0")))
                    bT = bankb(0)
                    for j in range(4):
                        TR(bT[:, j * 128:(j + 1) * 128], Bp[0][:, j * 128:(j + 1) * 128], R(tn("B0")), R("b0"))
                    CP("act", YT[:, :].rearrange("p (k s) -> p k s", k=8)[:, 0:4, t * 128:(t + 1) * 128],
                       bT[:, 0:512].rearrange("p (k s) -> p k s", k=4), R("b0"), R("YTr%d" % t))

                tile_loop(list(range(NT)), lambda t: base + t * 128, sweep2_main)
                P.barrier()

                checkpoint(4)
                CQT = arb(0, 2 * S).rearrange("p (k s) -> p k s", k=2)
                CKVT = arb(8192, S)
                KRB = arf(12288, NT * 32)
                KT = arb(14336, 4 * S).rearrange("p (h s) -> p h s", h=4)
                VA = arb(30720, NT * 512).rearrange("p (t h c) -> p t h c", h=4, c=128)
                WM = arb(47104, 3328).rearrange("p (k n) -> p k n", k=8)
                QT = arb(47104, 2048).rearrange("p (h s) -> p h s", h=4)
                RL = arf(47104 + 2048, 512)
                WUQ = arb(50432, 1536).rearrange("p (k n) -> p k n", k=2)
                WUKV = arb(51968, 1024)
                PTA = [arb(52992 + i * 1024, 1024) for i in range(3)]
                COSM = arf(56064, NT * 16)
                SINM = arf(57088, NT * 16)
                MF = [arf(58112 + i * 1024, 512) for i in range(2)]
                tmpm = [arf(52992 + i * 1024, NT * 16) for i in range(3)]
                DMA("sp", WM[:, :, :], win_s[:, 2048:2464].rearrange("(k p) n -> p k n", p=128), [], R("WM"))
                DMA("sp", WUQ[:, :, :], wuq_s.rearrange("(k p) n -> p k n", p=128), [], R("WUQ"))
                DMA("sp", WUKV[:, :], wukv_s[:, :], [], R("WUKV"))
                trig_tables(16, CF[:, 32:48], COSM, SINM, tmpm, "TABM")
                P.barrier()
                MFb = [arf(52992 + i * 1024, 512) for i in range(2)]

                def mla_main(t, p, ht, htt):
                    bm, bt2 = (1, 2) if p == 0 else (3, 4)
                    M0, M1 = (MF[0], MF[1]) if p == 0 else (MFb[0], MFb[1])
                    so = 8 if p == 0 else 40
                    tn = lambda n, p=p: "%s_%d" % (n, p)
                    proj(bm, ht, htt, WM, "WM", 0, 416)
                    b1 = bank(bm)
                    b1t = "b%d" % bm
                    ACTV(M0[:, 0:256], b1[:, 0:256], AF.Square, R(b1t), R(tn("MF0"), tn("ssq")), accum=SM[:, so:so + 1])
                    ACTV(SM[:, so + 1:so + 2], SM[:, so:so + 1], AF.Ln, R(tn("ssq")), R(tn("t1")), scale=1.0 / 256, bias=EPS)
                    ACTV(SM[:, so + 2:so + 3], SM[:, so + 1:so + 2], AF.Exp, R(tn("t1")), R(tn("rq")), scale=-0.5)
                    B_cq = M1[:, 0:128].bitcast(BF16)
                    TS(B_cq, b1[:, 0:256], SM[:, so + 2:so + 3], None, ALU.mult, None, R(b1t, tn("rq")), R(tn("cq")))
                    ACTV(M0[:, 256:384], b1[:, 256:384], AF.Square, R(b1t), R(tn("MF0b"), tn("sskv")), accum=SM[:, so + 3:so + 4])
                    ACTV(SM[:, so + 4:so + 5], SM[:, so + 3:so + 4], AF.Ln, R(tn("sskv")), R(tn("t2")), scale=1.0 / 128, bias=EPS)
                    ACTV(SM[:, so + 5:so + 6], SM[:, so + 4:so + 5], AF.Exp, R(tn("t2")), R(tn("rkv")), scale=-0.5)
                    B_ckv = M1[:, 128:192].bitcast(BF16)
                    TS(B_ckv, b1[:, 256:384], SM[:, so + 5:so + 6], None, ALU.mult, None, R(b1t, tn("rkv")), R(tn("ckv")))
                    b2 = bankb(bt2)
                    b2t = "b%d" % bt2
                    TR(b2[:, 0:128], B_cq[:, 0:128], R(tn("cq")), R(b2t))
                    TR(b2[:, 128:256], B_cq[:, 128:256], R(tn("cq")), R(b2t))
                    TR(b2[:, 256:384], B_ckv[:, 0:128], R(tn("ckv")), R(b2t))
                    CP("act", CQT[:, :, t * 128:(t + 1) * 128], b2[:, 0:256].rearrange("p (k s) -> p k s", k=2), R(b2t), R("CQT%d" % t))
                    CP("dve", CKVT[:, t * 128:(t + 1) * 128], b2[:, 256:384], R(b2t), R("CKVT%d" % t))
                    ACTV(M0[:, 384:416], b1[:, 384:416], AF.Square, R(b1t), R(tn("MF0c"), "SSR"), accum=SSR[:, t:t + 1])
                    TTo("dve", M0[:, 448:480], b1[:, 384:416], GK[:, 64:96], ALU.mult, R(b1t, "GK"), R(tn("krg")))
                    rotary(M0[:, 448:480].rearrange("p (h d f) -> p h d f", h=1, d=2), tn("krg"),
                           KRB[:, t * 32:(t + 1) * 32].rearrange("p (h d f) -> p h d f", h=1, d=2), "KRB%d" % t,
                           COSM[:, t * 16:(t + 1) * 16], SINM[:, t * 16:(t + 1) * 16], "TABM", 1, 16,
                           M1[:, 256:288].rearrange("p (h d f) -> p h d f", h=1, d=2),
                           M1[:, 288:320].rearrange("p (h d f) -> p h d f", h=1, d=2), tn("mrA"), tn("mrB"))

                tile_loop2(list(range(NT)), lambda t: base + t * 128, mla_main, 8)
                P.barrier()
                checkpoint(5)
                for hh in range(2):
                    MEMSET("pool", VA[:, :, :, 64:128], 1.0, R("VAones"))
                    kd_lists = []
                    for t in range(NT):
                        p = t % 2
                        bkv, bkvt = bank(3 + p), "b%d" % (3 + p)
                        btr, btrt = (bankb(2), "b2") if p == 0 else (bankb(5), "b5")
                        kb = 52992 + p * 1408
                        k0 = arf(kb, 256).rearrange("p (h c) -> p h c", h=4)
                        k1 = arf(kb + 512, 256).rearrange("p (h c) -> p h c", h=4)
                        KTM = arb(kb + 1024, 384).rearrange("p (h c) -> p h c", h=4)
                        so = 16 if p == 0 else 48
                        ssh, stmp, srk = SM[:, so:so + 4], SM[:, so + 4:so + 8], SM[:, so + 8:so + 12]
                        tn = lambda n, p=p: "%s_%d" % (n, p)
                        P.capture()
                        MM(bkv, CKVT[:, t * 128:(t + 1) * 128], WUKV[:, hh * 512:(hh + 1) * 512], True, True,
                           R("CKVT%d" % t, "WUKV"), R(bkvt))
                        kv = bkv.rearrange("p (h c) -> p h c", h=4)
                        ACTV(k0, kv[:, :, 0:64], AF.Square, R(bkvt), R(tn("KD0")))
                        RED(ssh, k0, R(tn("KD0")), R(tn("ssh")))
                        TS(ssh, ssh, SSR[:, t:t + 1], None, ALU.add, None, R(tn("ssh"), "SSR"), R(tn("ssh")))
                        ACTV(stmp, ssh, AF.Ln, R(tn("ssh")), R(tn("kdtmp")), scale=1.0 / 96, bias=EPS)
                        ACTV(srk, stmp, AF.Exp, R(tn("kdtmp")), R(tn("rk")), scale=-0.5)
                        TTo("dve", k1, kv[:, :, 0:64], srk.unsqueeze(2).to_broadcast([128, 4, 64]), ALU.mult, R(bkvt, tn("rk")), R(tn("KD1")))
                        TTo("dve", KTM[:, :, 0:64], k1, GK[:, 0:64].unsqueeze(1).to_broadcast([128, 4, 64]), ALU.mult,
                            R(tn("KD1"), "GK"), R(tn("KTM")))
                        TTo("dve", KTM[:, :, 64:96], KRB[:, t * 32:(t + 1) * 32].unsqueeze(1).to_broadcast([128, 4, 32]),
                            srk.unsqueeze(2).to_broadcast([128, 4, 32]), ALU.mult, R("KRB%d" % t, tn("rk"), tn("KTM")), R(tn("KTM")))
                        for hl in range(4):
                            TR(btr[0:96, hl * 128:(hl + 1) * 128], KTM[:, hl, :], R(tn("KTM")), R(btrt))
                        CP("act", KT[0:96, :, t * 128:(t + 1) * 128], btr[0:96, 0:512].rearrange("p (h s) -> p h s", h=4),
                           R(btrt), R("KT%d" % t))
                        CP("dve", VA[:, t, :, 0:64], kv[:, :, 64:128], R(bkvt, "VAones"), R("VA%d" % t))
                        kd_lists.append(P.end_capture())
                    Lk = len(kd_lists[0])
                    P.replay_skewed([(ops, t * Lk / 2.0, 1.0) for t, ops in enumerate(kd_lists)])
                    P.barrier()
                    checkpoint(6)
                    NK = NT // 2
                    QTs = [QT, QT2v]

                    def qprepA(qb, s):
                        tq = qb * 4 + s
                        b6 = bank(6)
                        for k in range(2):
                            MM(b6[:, 0:384], CQT[:, k, tq * 128:(tq + 1) * 128], WUQ[:, k, hh * 384:(hh + 1) * 384], k == 0, k == 1,
                               R("CQT%d" % tq, "WUQ"), R("b6"))
                        q3 = b6[:, 0:384].rearrange("p (h c) -> p h c", h=4)
                        m0 = MF[0][:, 0:384].rearrange("p (h c) -> p h c", h=4)
                        m1 = MF[1][:, 0:384].rearrange("p (h c) -> p h c", h=4)
                        ACTV(m0, q3, AF.Square, R("b6"), R("MF0"))
                        RED(SM[:, 16:20], m0, R("MF0"), R("ssh"))
                        rsqrt_chain(SM[:, 24:28], SM[:, 16:20], 1.0 / 96, EPS, R("ssh"), R("rk"), SM[:, 20:24])
                        TTo("dve", m1, q3, SM[:, 24:28].unsqueeze(2).to_broadcast([128, 4, 96]), ALU.mult, R("b6", "rk"), R("MF1"))
                        TTo("dve", m1, m1, GQS[:, :].unsqueeze(1).to_broadcast([128, 4, 96]), ALU.mult, R("MF1", "GQS"), R("MF1"))
                        qtk = H[:, 0:384].rearrange("p (h c) -> p h c", h=4)
                        CP("dve", qtk[:, :, 0:64], m1[:, :, 0:64], R("MF1", "H"), R("H"))
                        src = m1[:, :, 64:96].rearrange("p h (d f) -> p h d f", d=2)
                        dst = qtk[:, :, 64:96].rearrange("p h (d f) -> p h d f", d=2)
                        tA = m0[:, :, 0:32].rearrange("p h (d f) -> p h d f", d=2)
                        tB = m0[:, :, 32:64].rearrange("p h (d f) -> p h d f", d=2)
                        rotary(src, "MF1", dst, "H", COSM[:, tq * 16:(tq + 1) * 16], SINM[:, tq * 16:(tq + 1) * 16], "TABM", 4, 16,
                               tA, tB, "MF0", "MF0")

                    def qprepC(qb, s):
                        qtk = H[:, 0:384].rearrange("p (h c) -> p h c", h=4)
                        b7 = bankb(7)
                        for hl in range(4):
                            TR(b7[0:96, hl * 128:(hl + 1) * 128], qtk[:, hl, :], R("H"), R("b7"))
                        CP("dve", QTs[qb % 2][0:96, :, s * 128:(s + 1) * 128], b7[0:96, 0:512].rearrange("p (h s) -> p h s", h=4),
                           R("b7"), R("QT%d" % (qb % 2)))

                    units = [(qb, hl, kp) for qb in range(NQB) for hl in range(4) for kp in range(NK)]

                    def emitQK(i):
                        qb, hl, kp = units[i]
                        ssl = i % 2
                        for j in range(2):
                            kt = 2 * kp + j
                            MM(PS[ssl][:, j * 512:(j + 1) * 512], KT[0:96, hl, kt * 128:(kt + 1) * 128], QTs[qb % 2][0:96, hl, :], True, True,
                               R("KT%d" % kt, "QT%d" % (qb % 2)), R("S%d" % ssl))

                    for s in range(4):
                        qprepA(0, s)
                        qprepC(0, s)
                    emitQK(0)
                    if len(units) > 1:
                        emitQK(1)
                    UB = 4 * NK
                    sched = {}
                    if UB >= 40:
                        for s in range(4):
                            sched[2 + 9 * s] = ("A", s)
                            sched[2 + 9 * s + 5] = ("C", s)
                    for i, (qb, hl, kp) in enumerate(units):
                        h = hh * 4 + hl
                        osl = (qb * 4 + hl) % 2
                        bO = bank(4 + osl)
                        ssl, psl = i % 2, i % 3
                        ACTV(PTA[psl][:, :], PS[ssl][:, :], AF.Exp, R("S%d" % ssl), R("PTA%d" % psl))
                        if i + 2 < len(units) and not (UB < 40 and units[i + 2][0] != qb):
                            emitQK(i + 2)
                        for j in range(2):
                            kt = 2 * kp + j
                            MM(bO, VA[:, kt, hl, :], PTA[psl][:, j * 512:(j + 1) * 512], kp == 0 and j == 0,
                               kp == NK - 1 and j == 1, R("VA%d" % kt, "PTA%d" % psl), R("bO%d" % osl))
                        ju = i % UB
                        if qb + 1 < NQB:
                            if UB >= 40:
                                ev = sched.get(ju)
                                if ev is not None:
                                    (qprepA if ev[0] == "A" else qprepC)(qb + 1, ev[1])
                        if kp == NK - 1:
                            RECIP(RL[64:128, :], bO[64:128, :], R("bO%d" % osl), R("RL"))
                            pr = (h % 2) * 64
                            TTo("dve", YT[:, :].rearrange("p (k s) -> p k s", k=8)[pr:pr + 64, 4 + h // 2, qb * 512:(qb + 1) * 512],
                                bO[0:64, :], RL[64:128, :], ALU.mult, R("bO%d" % osl, "RL"), R("YTm%d_%d" % (h, qb)))
                        if UB < 40 and ju == UB - 1 and qb + 1 < NQB:
                            for s in range(4):
                                qprepA(qb + 1, s)
                                qprepC(qb + 1, s)
                            emitQK(i + 1)
                            if i + 2 < len(units):
                                emitQK(i + 2)
                    P.barrier()

                checkpoint(7)
                WO = arb(0, 8192).rearrange("p (k n) -> p k n", k=8)
                X1 = arf(8192, 4096).rearrange("p (s n) -> p s n", s=4)
                H2T = arb(16384, 4096).rearrange("p (k s) -> p k s", k=8)
                AT = arb(20480, NF * 512).rearrange("p (f s) -> p f s", f=NF)
                NWG = 4
                WGU = [arb(31744 + i * 2048, 2048).rearrange("p (g k n) -> p g k n", g=2, k=8) for i in range(NWG)]
                WDQ = [arb(39936 + i * 5632, 5632).rearrange("p (f n) -> p f n", f=NF) for i in range(3)]
                SG = [arf(56832 + i * 1024, 512) for i in range(2)]
                DMA("sp", WO[:, :, :], wo_s.rearrange("(k p) n -> p k n", p=128), [], R("WO"))
                YT3 = YT[:, :].rearrange("p (k s) -> p k s", k=8)
                wq = [0]
                for qb in range(NQB):
                    for s in range(4):
                        row0 = base + (qb * 4 + s) * 128
                        DMA("pool", X1[:, s, :], x_d[row0:row0 + 128, :], [], R("X1_%d" % s))
                    plists = []
                    for s in range(4):
                        tq = qb * 4 + s
                        xs = "X1_%d" % s
                        p = s % 2
                        wb = (0, 1) if p == 0 else (3, 4)
                        tbi = 2 if p == 0 else 7
                        Hs, Hst = (H, "H") if p == 0 else (HT[0], "HT0")
                        so = 0 if p == 0 else 4
                        P.capture()
                        for c2 in range(2):
                            for k in range(8):
                                MM(bank(wb[c2]), YT3[:, k, tq * 128:(tq + 1) * 128], WO[:, k, c2 * 512:(c2 + 1) * 512], k == 0, k == 7,
                                   R("WO"), R("b%d" % wb[c2]))
                            TTo("dve", X1[:, s, c2 * 512:(c2 + 1) * 512], X1[:, s, c2 * 512:(c2 + 1) * 512], bank(wb[c2]), ALU.add,
                                R(xs, "b%d" % wb[c2]), R(xs))
                        ACTV(Hs[:, :], X1[:, s, :], AF.Square, R(xs), R(Hst, "xss%d" % p), accum=SM[:, so:so + 1])
                        ACTV(SM[:, so + 1:so + 2], SM[:, so:so + 1], AF.Ln, R("xss%d" % p), R("xtmp%d" % p), scale=1.0 / D, bias=EPS)
                        ACTV(SM[:, so + 2:so + 3], SM[:, so + 1:so + 2], AF.Exp, R("xtmp%d" % p), R("xrs%d" % p), scale=-0.5)
                        TS(Hs[:, :], X1[:, s, :], SM[:, so + 2:so + 3], None, ALU.mult, None, R(xs, "xrs%d" % p), R(Hst))
                        bT = bankb(tbi)
                        for k in range(8):
                            TR(bT[:, k * 128:(k + 1) * 128], Hs[:, k * 128:(k + 1) * 128], R(Hst), R("b%d" % tbi))
                        CP("act", H2T[:, :, s * 128:(s + 1) * 128], bT[:, :].rearrange("p (k s) -> p k s", k=8), R("b%d" % tbi), R("H2T"))
                        plists.append(P.end_capture())
                    Lp = len(plists[0])
                    P.replay_skewed([(ops, si * Lp / 2.0, 1.0) for si, ops in enumerate(plists)])
                    for f in range(NF):
                        sl = f % 2
                        wsl_ = f % NWG
                        DMA("sp", WGU[wsl_][:, :, :, :], wgu_s[f], [], R("WGU%d" % wsl_))
                        for g in range(2):
                            bi = 3 + 2 * g + sl
                            for k in range(8):
                                MM(bank(bi), WGU[wsl_][:, g, k, :], H2T[:, k, :], k == 0, k == 7, R("WGU%d" % wsl_, "H2T"), R("b%d" % bi))
                        bG, bU = bank(3 + sl), bank(5 + sl)
                        ACTV(SG[0][:, :], bG, AF.Exp, R("b%d" % (3 + sl)), R("SG0"), scale=-1.0)
                        ACTV(SG[0][:, :], SG[0][:, :], AF.Ln, R("SG0"), R("SG0"), bias=1.0)
                        ACTV(SG[0][:, :], SG[0][:, :], AF.Exp, R("SG0"), R("SG0"), scale=-1.0)
                        TTo("dve", SG[1][:, :], bG, SG[0][:, :], ALU.mult, R("b%d" % (3 + sl), "SG0"), R("SG1"))
                        TTo("dve", AT[:, f, :], SG[1][:, :], bU, ALU.mult, R("SG1", "b%d" % (5 + sl)), R("AT"))
                    for q4 in range(4):
                        wsl = wq[0] % 3
                        wq[0] += 1
                        DMA("sp", WDQ[wsl][:, :, :], wd_s[:, q4 * 256:(q4 + 1) * 256].rearrange("(f p) n -> p f n", p=128), [],
                            R("WDQ%d" % wsl))
                        for s in range(4):
                            bi = s % 2
                            for f in range(NF):
                                MM(bank(bi)[:, 0:256], AT[:, f, s * 128:(s + 1) * 128], WDQ[wsl][:, f, :], f == 0, f == NF - 1,
                                   R("AT", "WDQ%d" % wsl), R("b%d" % bi))
                            TTo("dve", X1[:, s, q4 * 256:(q4 + 1) * 256], X1[:, s, q4 * 256:(q4 + 1) * 256], bank(bi)[:, 0:256], ALU.add,
                                R("X1_%d" % s, "b%d" % bi), R("X1_%d" % s))
                    for s in range(4):
                        row0 = base + (qb * 4 + s) * 128
                        DMA("pool", out_d[row0:row0 + 128, :], X1[:, s, :], R("X1_%d" % s), [])
                P.barrier()
        except _Stop:
            pass
        if P.sched_on:
            while len(P.caps) > 1:
                inner = P.caps.pop()
                P.caps[-1].extend(inner)
            ops = P.caps.pop()
            P.sched_on = False
            for o in P.schedule(ops):
                P.add(*o)
        P.emit(st)
    return nc


def make_consts():
    cf = np.zeros((128, 308), np.float32)
    cf[:, 0:32] = (10000.0 ** (-np.arange(32, dtype=np.float32) / 32.0))[None, :]
    cf[:, 32:48] = (10000.0 ** (-np.arange(16, dtype=np.float32) / 16.0))[None, :]
    m = np.arange(128, dtype=np.float32)[:, None]
    c = np.arange(128, dtype=np.float32)[None, :]
    cf[:, 48:176] = np.maximum(c - m, 0.0)
    cf[:, 176:304] = np.maximum(m - c, 0.0)
    cf[:, 304] = 127.0 - m[:, 0]
    cf[:, 305] = m[:, 0]
    cf[:, 306] = m[:, 0] + 1.0
    cf[:, 307] = 128.0 - m[:, 0]
    ident = np.eye(128, dtype=np.float32).astype(ml_dtypes.bfloat16)
    return cf, ident


_CACHE = {}


def run(inputs, S, NSEQ, ncores):
    key = (S, NSEQ)
    if key not in _CACHE:
        _CACHE[key] = build(S, NSEQ)
    nc = _CACHE[key]
    cf, ident = make_consts()
    x = np.ascontiguousarray(np.asarray(inputs["x"], dtype=np.float32))
    pos = np.ascontiguousarray(np.asarray(inputs["positions"], dtype=np.int32))
    wnames = ["norm1_g", "w_in", "ret_decay_logit_fwd", "ret_decay_logit_bwd", "q_a_norm_g", "w_uq", "kv_a_norm_g",
              "w_ukv", "q_norm_g", "k_norm_g", "w_o", "norm2_g", "w_gate", "w_up", "w_down"]
    shared = {}
    for n in wnames:
        a = np.asarray(inputs[n], dtype=np.float32)
        shared[n] = np.ascontiguousarray(a[0])
    for n in ["norm1_g", "ret_decay_logit_fwd", "ret_decay_logit_bwd", "q_a_norm_g", "kv_a_norm_g", "q_norm_g", "k_norm_g", "norm2_g"]:
        shared[n] = shared[n].reshape(1, -1)
    shared["cf"] = cf
    shared["ident"] = ident
    in_maps = []
    for c in range(ncores):
        m = dict(shared)
        m["x"] = x[c * NSEQ:(c + 1) * NSEQ].reshape(NSEQ * S, D)
        m["positions"] = pos[c * NSEQ:(c + 1) * NSEQ].reshape(NSEQ * S)
        in_maps.append(m)
    res = run_bass_kernel_spmd(nc, in_maps, core_ids=list(range(ncores)))
    outs = [np.asarray(r["out"]).reshape(NSEQ, S, D) for r in res.results]
    return np.concatenate(outs, axis=0).astype(np.float32)


def kernel(**inputs):
    return run(inputs, 4096, 2, 8)
```
